# Optimizing a Trainium2 kernel written in Bass

```python
import math
import jax, jax.numpy as jnp
from jax import lax
import numpy as np

D_MODEL = 4096
BATCH = 16
SEQ = 256
DEPTH = 1
DEC_BATCH = 2
DEC_SEQ = 1024
PAST_LEN = 256

GRID_W = 64
D_LRU = D_MODEL // 2
LRU_BLOCKS = 16
LRU_BLOCK_W = D_LRU // LRU_BLOCKS
LRU_C = 8.0
LRU_MIN_RAD = 0.9
LRU_MAX_RAD = 0.999
CONV_W = 4
DN_HEADS = 16
DN_DK = D_MODEL // 32
DN_DV = D_MODEL // 32
DN_QK = DN_HEADS * DN_DK
DN_V = DN_HEADS * DN_DV
DN_CONV_CH = 2 * DN_QK + DN_V
DN_CHUNK = 64
D_FF = 4 * D_MODEL
DEEPNORM_ALPHA = (2.0 * DEPTH) ** 0.25
DEEPNORM_BETA = (8.0 * DEPTH) ** -0.25
LN_EPS = 1e-5
RMS_EPS = 1e-6
POS_BASE = 10000.0
IN_WIDTHS = (D_LRU, D_LRU, DN_CONV_CH, DN_V, 2 * DN_HEADS, 2 * DN_HEADS, 2 * D_MODEL)
IN_SPLITS = tuple(int(s) for s in np.cumsum(IN_WIDTHS)[:-1])
N_IN = int(sum(IN_WIDTHS))

kernel_name = 'hawk_deltanet_diffusion_step'

F32 = jnp.float32


def layer_norm(x, g, b):
    xf = x.astype(F32)
    mu = jnp.mean(xf, axis=-1, keepdims=True)
    var = jnp.mean(jnp.square(xf - mu), axis=-1, keepdims=True)
    return ((xf - mu) * lax.rsqrt(var + LN_EPS) * g.astype(F32) + b.astype(F32)).astype(x.dtype)


def l2_normalize(x):
    return x * lax.rsqrt(jnp.sum(jnp.square(x), axis=-1, keepdims=True) + RMS_EPS)


def centred_depthwise_conv(x, w):
    left = CONV_W // 2
    return lax.conv_general_dilated(
        x, w[:, None, :].astype(x.dtype), window_strides=(1,),
        padding=[(left, CONV_W - 1 - left)],
        dimension_numbers=('NWC', 'WIO', 'NWC'),
        feature_group_count=x.shape[-1])


def linear_scan(a, b, h0, reverse):
    def combine(left, right):
        a_l, b_l = left
        a_r, b_r = right
        return a_l * a_r, a_r * b_l + b_r
    a_cum, b_cum = lax.associative_scan(combine, (a, b), axis=1, reverse=reverse)
    return a_cum * h0[:, None, :] + b_cum


def grid_pos_embed(n_tokens):
    rows = n_tokens // GRID_W
    row = jnp.repeat(jnp.arange(rows, dtype=F32), GRID_W)
    col = jnp.tile(jnp.arange(GRID_W, dtype=F32), rows)
    quarter = D_MODEL // 4
    omega = 1.0 / (POS_BASE ** (jnp.arange(quarter, dtype=F32) / quarter))
    er = row[:, None] * omega
    ec = col[:, None] * omega
    return jnp.concatenate([jnp.sin(er), jnp.cos(er), jnp.sin(ec), jnp.cos(ec)], axis=-1)


def chunk_gated_delta(q, k, v, beta, g, s0):
    bsz, t_len = q.shape[0], q.shape[1]
    n_chunks = t_len // DN_CHUNK

    def to_chunks(a):
        a = a.reshape((bsz, n_chunks, DN_CHUNK) + a.shape[2:])
        return jnp.moveaxis(a, (1, 2), (0, 3))

    qc, kc, vc, bc = to_chunks(q), to_chunks(k), to_chunks(v), to_chunks(beta)
    gc = jnp.cumsum(to_chunks(g), axis=-1)
    idx = jnp.arange(DN_CHUNK)
    causal = idx[:, None] >= idx[None, :]
    strict = idx[:, None] > idx[None, :]
    decay = jnp.exp(jnp.where(causal, gc[..., :, None] - gc[..., None, :], -jnp.inf))
    kb = kc * bc[..., None]
    l_mat = jnp.where(strict, jnp.einsum('nbhcd,nbhsd->nbhcs', kb, kc) * decay, 0.0)
    eye = jnp.eye(DN_CHUNK, dtype=F32)
    rhs = jnp.concatenate([vc * bc[..., None], kb * jnp.exp(gc)[..., None]], axis=-1)
    sol = lax.linalg.triangular_solve(l_mat + eye, rhs, left_side=True, lower=True, unit_diagonal=True)
    u, w = sol[..., :DN_DV], sol[..., DN_DV:]
    attn = jnp.einsum('nbhcd,nbhsd->nbhcs', qc, kc) * decay
    g_last = gc[..., -1]
    q_dec = qc * jnp.exp(gc)[..., None]
    k_dec = kc * jnp.exp(g_last[..., None] - gc)[..., None]

    def step(s, xs):
        u_n, w_n, q_n, k_n, a_n, gl_n = xs
        v_new = u_n - jnp.einsum('bhck,bhkv->bhcv', w_n, s)
        o_n = jnp.einsum('bhck,bhkv->bhcv', q_n, s) + jnp.einsum('bhcs,bhsv->bhcv', a_n, v_new)
        s = s * jnp.exp(gl_n)[..., None, None] + jnp.einsum('bhck,bhcv->bhkv', k_n, v_new)
        return s, o_n

    s_fin, o = lax.scan(step, s0, (u, w, q_dec, k_dec, attn, g_last))
    o = jnp.moveaxis(o, (0, 3), (1, 2)).reshape(bsz, t_len, DN_HEADS, DN_DV)
    return o, s_fin


def rg_lru_branch(x_in, y_in, h0, p):
    bsz, t_len, _ = x_in.shape
    xc = (centred_depthwise_conv(x_in, p['lru_conv_w']) + p['lru_conv_b']).astype(F32)
    xb = xc.reshape(bsz, t_len, LRU_BLOCKS, LRU_BLOCK_W)
    gate_pre = jnp.einsum('btnk,dgnkj->dgbtnj', xb, p['lru_gate_w'].astype(F32))
    gate_pre = gate_pre.reshape(2, 2, bsz, t_len, D_LRU)
    gates = jax.nn.sigmoid(gate_pre + p['lru_gate_b'].astype(F32)[:, :, None, None, :])
    r, i = gates[:, 0], gates[:, 1]
    log_a = -LRU_C * r * jax.nn.softplus(-p['lru_lambda'].astype(F32))[:, None, None, :]
    a = jnp.exp(log_a)
    b = jnp.sqrt(-jnp.expm1(2.0 * log_a)) * (i * xc)
    h0 = h0.astype(F32)
    h_f = linear_scan(a[0], b[0], h0[:, 0], reverse=False)
    h_b = linear_scan(a[1], b[1], h0[:, 1], reverse=True)
    state = jnp.stack([h_f[:, -1], h_b[:, 0]], axis=1)
    out = (h_f + h_b) * jax.nn.gelu(y_in.astype(F32))
    return out, state


def gdn_branch(qkv_in, z, beta_logit, a_logit, s0, p):
    bsz, t_len, _ = qkv_in.shape
    qkv = jax.nn.silu(centred_depthwise_conv(qkv_in, p['dn_conv_w']).astype(F32))
    q, k, v = jnp.split(qkv, (DN_QK, 2 * DN_QK), axis=-1)
    q = l2_normalize(q.reshape(bsz, t_len, DN_HEADS, DN_DK)) * (DN_DK ** -0.5)
    k = l2_normalize(k.reshape(bsz, t_len, DN_HEADS, DN_DK))
    v = v.reshape(bsz, t_len, DN_HEADS, DN_DV)
    beta = jax.nn.sigmoid(beta_logit.astype(F32)).reshape(bsz, t_len, 2, DN_HEADS)
    g = -jnp.exp(p['dn_a_log'].astype(F32)) * jax.nn.softplus(
        a_logit.astype(F32).reshape(bsz, t_len, 2, DN_HEADS) + p['dn_dt_bias'].astype(F32))
    s0 = s0.astype(F32)
    o_f, s_f = chunk_gated_delta(q, k, v, beta[:, :, 0], g[:, :, 0], s0[:, 0])
    rev = lambda t: jnp.flip(t, axis=1)
    o_b, s_b = chunk_gated_delta(rev(q), rev(k), rev(v), rev(beta[:, :, 1]), rev(g[:, :, 1]), s0[:, 1])
    o = o_f + rev(o_b)
    o = o * lax.rsqrt(jnp.mean(jnp.square(o), axis=-1, keepdims=True) + RMS_EPS) * p['dn_norm_w'].astype(F32)
    o = o * jax.nn.silu(z.astype(F32).reshape(bsz, t_len, DN_HEADS, DN_DV))
    return o.reshape(bsz, t_len, DN_V), jnp.stack([s_f, s_b], axis=1)


def token_mixers(h, lru_h0, dn_s0, p):
    bsz, t_len, _ = h.shape
    x_lru, y_lru, qkv, z, beta_logit, a_logit, gate_logit = jnp.split(h @ p['w_in'], IN_SPLITS, axis=-1)
    lru_out, lru_state = rg_lru_branch(x_lru, y_lru, lru_h0, p)
    dn_out, dn_state = gdn_branch(qkv, z, beta_logit, a_logit, dn_s0, p)
    gate = jax.nn.sigmoid(gate_logit.reshape(bsz, t_len, 2, D_MODEL) + p['b_branch'])
    merged = (gate[:, :, 0] * (lru_out.astype(h.dtype) @ p['w_lru_proj'])
              + gate[:, :, 1] * (dn_out.astype(h.dtype) @ p['w_dn_proj']))
    return merged @ p['w_o'], lru_state, dn_state


def sq_relu_mlp(h, w_up, w_down):
    return jnp.square(jax.nn.relu(h @ w_up)) @ w_down


def trunk_layer(x, mod, lru_h0, dn_s0, p):
    shift_m, scale_m, gate_m, shift_f, scale_f, gate_f = jnp.split(mod[:, None, :].astype(x.dtype), 6, axis=-1)
    mix, lru_state, dn_state = token_mixers(x * (1 + scale_m) + shift_m, lru_h0, dn_s0, p)
    x = layer_norm(DEEPNORM_ALPHA * x + gate_m * mix, p['ln1_g'], p['ln1_b'])
    ffn = sq_relu_mlp(x * (1 + scale_f) + shift_f, p['w_up'], p['w_down'])
    x = layer_norm(DEEPNORM_ALPHA * x + gate_f * ffn, p['ln2_g'], p['ln2_b'])
    return x, lru_state, dn_state


def setup_inputs(seed: int = 0) -> dict:
    key = jax.random.key(seed)
    ks = jax.random.split(key, 32)
    nrm = lambda k, shape, s: jax.random.normal(k, shape, F32) * s
    x_prompt = nrm(ks[0], (BATCH, SEQ, D_MODEL), 1.0)
    x_sample = nrm(ks[1], (DEC_BATCH, DEC_SEQ, D_MODEL), 1.0)
    state_lru = nrm(ks[2], (DEC_BATCH, DEPTH, 2, D_LRU), 0.5)
    state_dn = nrm(ks[3], (DEC_BATCH, DEPTH, 2, DN_HEADS, DN_DK, DN_DV), 0.1)
    c = nrm(ks[4], (DEC_BATCH, D_MODEL), 1.0)
    c_ctx = nrm(ks[5], (D_MODEL,), 1.0)
    w_mod = nrm(ks[6], (DEPTH, D_MODEL, 6 * D_MODEL), 0.5 * D_MODEL ** -0.5)
    b_mod = nrm(ks[7], (DEPTH, 6 * D_MODEL), 0.02)
    w_in = nrm(ks[8], (DEPTH, D_MODEL, N_IN), D_MODEL ** -0.5)
    lru_conv_w = nrm(ks[9], (DEPTH, CONV_W, D_LRU), CONV_W ** -0.5)
    lru_conv_b = nrm(ks[10], (DEPTH, D_LRU), 0.02)
    lru_gate_w = nrm(ks[11], (DEPTH, 2, 2, LRU_BLOCKS, LRU_BLOCK_W, LRU_BLOCK_W), LRU_BLOCK_W ** -0.5)
    lru_gate_b = nrm(ks[12], (DEPTH, 2, 2, D_LRU), 0.1)
    rad = jax.random.uniform(ks[13], (DEPTH, 2, D_LRU), F32, LRU_MIN_RAD, LRU_MAX_RAD)
    sig = rad ** (1.0 / LRU_C)
    lru_lambda = jnp.log(sig) - jnp.log1p(-sig)
    dn_conv_w = nrm(ks[14], (DEPTH, CONV_W, DN_CONV_CH), CONV_W ** -0.5)
    dn_a_log = jnp.log(jax.random.uniform(ks[15], (DEPTH, 2, DN_HEADS), F32, 1.0, 16.0))
    dt = jnp.exp(jax.random.uniform(ks[16], (DEPTH, 2, DN_HEADS), F32, math.log(1e-3), math.log(1e-1)))
    dn_dt_bias = dt + jnp.log(-jnp.expm1(-dt))
    dn_norm_w = 1.0 + nrm(ks[17], (DEPTH, DN_DV), 0.1)
    b_branch = nrm(ks[18], (DEPTH, 2, D_MODEL), 0.1)
    w_lru_proj = nrm(ks[19], (DEPTH, D_LRU, D_MODEL), DEEPNORM_BETA * D_LRU ** -0.5)
    w_dn_proj = nrm(ks[20], (DEPTH, DN_V, D_MODEL), DEEPNORM_BETA * DN_V ** -0.5)
    w_o = nrm(ks[21], (DEPTH, D_MODEL, D_MODEL), DEEPNORM_BETA * D_MODEL ** -0.5)
    ln1_g = 1.0 + nrm(ks[22], (DEPTH, D_MODEL), 0.1)
    ln1_b = nrm(ks[23], (DEPTH, D_MODEL), 0.02)
    w_up = nrm(ks[24], (DEPTH, D_MODEL, D_FF), D_MODEL ** -0.5)
    w_down = nrm(ks[25], (DEPTH, D_FF, D_MODEL), DEEPNORM_BETA * D_FF ** -0.5)
    ln2_g = 1.0 + nrm(ks[26], (DEPTH, D_MODEL), 0.1)
    ln2_b = nrm(ks[27], (DEPTH, D_MODEL), 0.02)
    return {'x_prompt': x_prompt, 'x_sample': x_sample, 'state_lru': state_lru, 'state_dn': state_dn,
            'c': c, 'c_ctx': c_ctx, 'w_mod': w_mod, 'b_mod': b_mod, 'w_in': w_in,
            'lru_conv_w': lru_conv_w, 'lru_conv_b': lru_conv_b, 'lru_gate_w': lru_gate_w,
            'lru_gate_b': lru_gate_b, 'lru_lambda': lru_lambda, 'dn_conv_w': dn_conv_w,
            'dn_a_log': dn_a_log, 'dn_dt_bias': dn_dt_bias, 'dn_norm_w': dn_norm_w,
            'b_branch': b_branch, 'w_lru_proj': w_lru_proj, 'w_dn_proj': w_dn_proj, 'w_o': w_o,
            'ln1_g': ln1_g, 'ln1_b': ln1_b, 'w_up': w_up, 'w_down': w_down,
            'ln2_g': ln2_g, 'ln2_b': ln2_b}


def reference(x_prompt, x_sample, state_lru, state_dn, c, c_ctx, w_mod, b_mod, w_in,
              lru_conv_w, lru_conv_b, lru_gate_w, lru_gate_b, lru_lambda, dn_conv_w,
              dn_a_log, dn_dt_bias, dn_norm_w, b_branch, w_lru_proj, w_dn_proj, w_o,
              ln1_g, ln1_b, w_up, w_down, ln2_g, ln2_b):
    y_prompt = x_prompt
    y_sample = x_sample + grid_pos_embed(x_sample.shape[1]).astype(x_sample.dtype)[None]
    n_ctx = x_prompt.shape[0]
    zeros_lru = jnp.zeros((n_ctx, 2, D_LRU), F32)
    zeros_dn = jnp.zeros((n_ctx, 2, DN_HEADS, DN_DK, DN_DV), F32)
    new_lru, new_dn = [], []
    for l in range(DEPTH):
        p = {'w_in': w_in[l], 'lru_conv_w': lru_conv_w[l], 'lru_conv_b': lru_conv_b[l],
             'lru_gate_w': lru_gate_w[l], 'lru_gate_b': lru_gate_b[l], 'lru_lambda': lru_lambda[l],
             'dn_conv_w': dn_conv_w[l], 'dn_a_log': dn_a_log[l], 'dn_dt_bias': dn_dt_bias[l],
             'dn_norm_w': dn_norm_w[l], 'b_branch': b_branch[l], 'w_lru_proj': w_lru_proj[l],
             'w_dn_proj': w_dn_proj[l], 'w_o': w_o[l], 'ln1_g': ln1_g[l], 'ln1_b': ln1_b[l],
             'w_up': w_up[l], 'w_down': w_down[l], 'ln2_g': ln2_g[l], 'ln2_b': ln2_b[l]}
        mod_ctx = jax.nn.silu(c_ctx[None, :]) @ w_mod[l] + b_mod[l]
        mod_lat = jax.nn.silu(c) @ w_mod[l] + b_mod[l]
        y_prompt, s_lru, s_dn = trunk_layer(y_prompt, mod_ctx, zeros_lru, zeros_dn, p)
        new_lru.append(s_lru)
        new_dn.append(s_dn)
        y_sample, _, _ = trunk_layer(y_sample, mod_lat, state_lru[:, l], state_dn[:, l], p)
    new_state_lru = jnp.stack(new_lru, axis=1).astype(x_prompt.dtype)
    new_state_dn = jnp.stack(new_dn, axis=1).astype(x_prompt.dtype)
    return (y_prompt, y_sample, new_state_lru, new_state_dn)
```

```python
import numpy as np
import concourse.bass as bass
import concourse.mybir as mybir
from concourse.bass_utils import run_bass_kernel_spmd

F32 = mybir.dt.float32
BF16 = mybir.dt.bfloat16
AF = mybir.ActivationFunctionType
ALU = mybir.AluOpType

D = 4096; T = 1024; NSEG = 4; SEG = 256; NH = 16; KT = 32; C = 64; NCH = 16; DFF = 16384
HALF = 512
N_IN = 20544
ALPHA = 2.0 ** 0.25
LN_EPS = 1e-5
RMS_EPS = 1e-6
NEG = -30000.0

C_ID = 0; C_ONE = 128; C_TRIF = 256; C_TRIB = 320; C_NMF = 384; C_NMT = 448; C_SMF = 512; C_SMT = 576
C_JIDX = 640; C_NIDX = 648; NCONST = 712
P_CT = 0; P_BMOD = 32; P_LCW = 224; P_LCB = 288; P_LGB = 304; P_LAM = 368; P_DCW = 400; P_ALOG = 592
P_DTB = 624; P_NW = 656; P_BBR = 657; P_L1G = 721; P_L1B = 753; P_L2G = 785; P_L2B = 817; P_FLAG = 849
P_H0 = 851; NPRM = 883

DEBUG = {}


class KB:
    def __init__(self, nc):
        self.nc = nc
        self.E = {'pe': nc.tensor, 'act': nc.scalar, 'dve': nc.vector, 'pool': nc.gpsimd, 'sp': nc.sync}
        self.sem = {}
        self.cnt = {}
        for e in self.E:
            self.sem[('c', e)] = nc.alloc_semaphore(name=f"c_{e}")
            self.cnt[e] = 0
        self.NDS = 8
        self.dcnt = {'pool': 0, 'sp': 0}
        for q in self.dcnt:
            for i in range(self.NDS):
                self.sem[('d', q, i)] = nc.alloc_semaphore(name=f"d_{q}{i}")
        self.known = {e: {} for e in self.E}
        self.lastw = {}
        self.readers = {}
        self.nwait = 0
        self.ninst = 0

    def need(self, e, sk, val):
        if self.known[e].get(sk, 0) < val:
            self.E[e].wait_ge(self.sem[sk], val)
            self.known[e][sk] = val
            self.nwait += 1

    def _deps(self, e, reads, writes):
        own = ('c', e)
        for k in reads:
            lw = self.lastw.get(k)
            if lw is not None:
                if lw[0] == own and e == 'pe':
                    continue
                self.need(e, lw[0], lw[1])
        for k in writes:
            lw = self.lastw.get(k)
            if lw is not None and lw[0] != own:
                self.need(e, lw[0], lw[1])
            for rk, rv in self.readers.get(k, {}).items():
                if rk != own:
                    self.need(e, rk, rv)

    def _mark(self, sk, val, reads, writes):
        for k in writes:
            self.lastw[k] = (sk, val)
            self.readers[k] = {}
        for k in reads:
            d = self.readers.setdefault(k, {})
            if d.get(sk, 0) < val:
                d[sk] = val

    def op(self, e, fn, reads=(), writes=(), inc=True):
        self._deps(e, reads, writes)
        inst = fn(self.E[e])
        self.ninst += 1
        if inc:
            self.cnt[e] += 1
            inst.then_inc(self.sem[('c', e)], 1)
            val = self.cnt[e]
        else:
            val = self.cnt[e] + 1
        self._mark(('c', e), val, reads, writes)
        return inst

    def dma(self, q, out, in_, reads=(), writes=()):
        i = self.dcnt[q]
        slot = i % self.NDS
        rnd = i // self.NDS
        sk = ('d', q, slot)
        if rnd > 0:
            self.need(q, sk, 16 * rnd)
        self._deps(q, reads, writes)
        inst = self.E[q].dma_start(out=out, in_=in_)
        inst.then_inc(self.sem[sk], 16)
        self.ninst += 1
        self.dcnt[q] += 1
        self._mark(sk, 16 * (rnd + 1), reads, writes)
        return inst

    def barrier(self, engines=None):
        cur = {}
        for e in self.E:
            if self.cnt[e] > 0:
                cur[('c', e)] = self.cnt[e]
        for q, n in self.dcnt.items():
            for s in range(self.NDS):
                k = (n - 1 - s) // self.NDS + 1 if n > s else 0
                if k > 0:
                    cur[('d', q, s)] = 16 * k
        for e in (engines or self.E):
            for sk, v in cur.items():
                if sk == ('c', e):
                    continue
                self.need(e, sk, v)


def build_program(dbg=None, stop=None):
    dbg = dbg or {}
    nc = bass.Bass("TRN2", target_bir_lowering=False)
    kb = KB(nc)

    def din(name, shape, dt=F32):
        return nc.dram_tensor(name, list(shape), dt, kind="ExternalInput").ap()

    def dout(name, shape, dt=F32):
        return nc.dram_tensor(name, list(shape), dt, kind="ExternalOutput").ap()

    def dscr(name, shape, dt):
        return nc.dram_tensor(name, list(shape), dt).ap()

    NEED = {'0': {'w_mod'}, 'A': {'w_mod'}, 'S': {'w_mod', 'w_in'}, 'S1': {'w_mod', 'w_in'}, 'S2': {'w_mod', 'w_in'}, 'L0': {'w_mod', 'w_in', 'lru_gw'}, 'D1': {'w_mod', 'w_in', 'lru_gw'}, 'D2': {'w_mod', 'w_in', 'lru_gw'}, 'D3': {'w_mod', 'w_in', 'lru_gw'}, 'M1': {'w_mod', 'w_in', 'lru_gw'},
            'M': {'w_mod', 'w_in', 'lru_gw'}, 'G': {'w_mod', 'w_in', 'lru_gw', 'w_lp', 'w_dp'}}
    need = NEED.get(stop)
    modc_in = dbg.pop('__modc_in', None) is not None
    if need is not None and modc_in:
        need = need - {'w_mod'}
    _din = din

    def din(name, shape, dt=F32):
        if need is not None and name.startswith('w_') or (need is not None and name == 'lru_gw'):
            if name not in need:
                return None
        return _din(name, shape, dt)

    x_d = din("x", [T, D])
    prm_d = din("prm", [128, NPRM])
    cst_d = din("cst", [128, NCONST])
    s0_d = din("dn_s0", [2, NH, 128, 128])
    wmod_d = din("w_mod", [D, 6 * D])
    win_d = din("w_in", [D, N_IN])
    lgw_d = din("lru_gw", [NH, 128, 4, 128])
    wlp_d = din("w_lp", [2048, D])
    wdp_d = din("w_dp", [2048, D])
    wo_d = din("w_o", [D, D])
    wup_d = din("w_up", [D, DFF])
    wdn_d = din("w_down", [DFF, D])
    y_d = dout("y", [T, D])
    stl_d = dout("st_lru", [128, NH * NSEG * 2])
    std_d = dout("st_dn", [NSEG, 2, NH, 128, 128])
    xT_scr = dscr("xT_scr", [KT, 128, T], F32)
    lru_scr = dscr("lru_scr", [NH, 128, T], BF16)
    dn_scr = dscr("dn_scr", [NH, 128, T], BF16)
    mg_scr = dscr("mg_scr", [2, KT, 128, HALF], BF16)
    r1_scr = dscr("r1_scr", [KT, 128, HALF], F32)
    x1_scr = dscr("x1_scr", [KT, 128, HALF], F32)
    dbg_d = {}
    for name, shape in dbg.items():
        if isinstance(shape, tuple):
            dbg_d[name] = dout("dbg_" + name, shape[0], shape[1])
        else:
            dbg_d[name] = dout("dbg_" + name, shape)
    modc_d = _din("modc_in", [128, 192]) if modc_in else None

    def sb(name, shape, dt=F32):
        return nc.alloc_sbuf_tensor(name + "_sb", list(shape), dt)

    cst = sb("cst", [128, NCONST])
    prm = sb("prm", [128, NPRM])
    identb = sb("identb", [128, 128], BF16)
    modc = sb("modc", [128, 192])
    sc1m = sb("sc1m", [128, 32])
    sc1f = sb("sc1f", [128, 32])
    cdl = sb("cdl", [128, 32])
    cdl2 = sb("cdl2", [128, 32])
    stl = sb("stl", [128, NH * NSEG * 2])
    wring = sb("wring", [128, 4 * 32 * 128], BF16)
    psum = [nc.alloc_psum_tensor(f"ps{i}", [128, 512], F32) for i in range(8)]
    ps_state = {'i': 0}

    def newps():
        i = ps_state['i']
        ps_state['i'] = (i + 1) % 6
        return psum[i], ('ps', i)

    def resps(i):
        return psum[i], ('ps', i)

    ident = cst[:, C_ID:C_ID + 128]
    ones = cst[:, C_ONE:C_ONE + 128]

    def pcol(off, n=1):
        return prm[:, off:off + n]

    class Ring:
        def __init__(self, nslot, kt, cols):
            self.nslot = nslot; self.kt = kt; self.cols = cols; self.i = 0
            per = kt * cols
            assert nslot * per <= 4 * 32 * 128
            self.views = [wring[:, s * per:(s + 1) * per].rearrange("p (k c) -> p k c", k=kt) for s in range(nslot)]

        def load(self, src_ap, nk=None, ncol=None):
            s = self.i % self.nslot
            self.i += 1
            v = self.views[s]
            if nk is not None or ncol is not None:
                v = v[:, 0:(nk or self.kt), 0:(ncol or self.cols)]
            key = ('wr', s)
            kb.dma('pool', out=v, in_=src_ap, writes=[key])
            return v, key

    def wview(w_d, r0, nk, c0, ncol):
        return w_d[r0:r0 + nk * 128, c0:c0 + ncol].rearrange("(k p) n -> p k n", p=128)

    def dump(name, ap, key):
        if name in dbg_d:
            kb.dma('sp', out=dbg_d[name], in_=ap, reads=[key] if not isinstance(key, list) else key)

    def finish():
        kb.barrier(['sp'])
        print("program: insts", kb.ninst, "waits", kb.nwait)
        return nc

    kb.dma('sp', out=cst[:], in_=cst_d, writes=['cst'])
    kb.dma('sp', out=prm[:], in_=prm_d, writes=['prm'])
    kb.op('dve', lambda e: e.tensor_copy(out=identb[:], in_=ident), reads=['cst'], writes=['identb'])
    kb.op('dve', lambda e: e.memset(stl[:], 0.0), writes=['stl'])

    if modc_in:
        kb.dma('sp', out=modc[:], in_=modc_d, writes=['modc'])
    with nc.sbuf_tensor("cs", [128, 32], BF16) as cs, nc.sbuf_tensor("rowb", [1, 2 * 256], F32) as rowb:
        kb.op('act', lambda e: e.activation(out=cs[:], in_=pcol(P_CT, 32), func=AF.Silu), reads=['prm'], writes=['cs'])
        ring = Ring(2, 32, 256)
        for blk in range(0 if modc_in else 96):
            wv, wk = ring.load(wview(wmod_d, 0, 32, blk * 256, 256))
            ps, pk = newps()
            for kt in range(KT):
                kb.op('pe', lambda e, kt=kt: e.matmul(ps[0:1, 0:256], cs[:, kt:kt + 1], wv[:, kt, :], start=(kt == 0), stop=(kt == KT - 1)),
                      reads=['cs', wk], writes=[pk], inc=(kt == KT - 1))
            rb = rowb[0:1, (blk % 2) * 256:(blk % 2) * 256 + 256]
            rk = ('rowb', blk % 2)
            kb.op('act', lambda e: e.activation(out=rb, in_=ps[0:1, 0:256], func=AF.Identity), reads=[pk], writes=[rk])
            ps2, pk2 = newps()
            for j in range(2):
                kb.op('pe', lambda e, j=j: e.matmul(ps2[:, j:j + 1], rb[0:1, j * 128:(j + 1) * 128], ones[0:1, 0:1], start=True, stop=True),
                      reads=[rk, 'cst'], writes=[pk2], inc=(j == 1))
            kb.op('dve', lambda e: e.tensor_tensor(out=modc[:, blk * 2:blk * 2 + 2], in0=ps2[:, 0:2], in1=pcol(P_BMOD + blk * 2, 2), op=ALU.add),
                  reads=[pk2, 'prm'], writes=['modc'])
    kb.op('dve', lambda e: e.tensor_scalar_add(out=sc1m[:], in0=modc[:, 32:64], scalar1=1.0), reads=['modc'], writes=['sc1m'])
    kb.op('dve', lambda e: e.tensor_scalar_add(out=sc1f[:], in0=modc[:, 128:160], scalar1=1.0), reads=['modc'], writes=['sc1f'])
    shm = modc[:, 0:32]; gtm = modc[:, 64:96]; shf = modc[:, 96:128]; gtf = modc[:, 160:192]
    with nc.sbuf_tensor("etmp", [128, 32], F32) as etmp, nc.sbuf_tensor("etmp2", [128, 32], F32) as etmp2:
        kb.op('act', lambda e: e.activation(out=etmp[:], in_=pcol(P_LAM, 32), func=AF.Exp, scale=-1.0), reads=['prm'], writes=['etmp'])
        kb.op('dve', lambda e: e.tensor_scalar(out=etmp2[:], in0=etmp[:], scalar1=1.0 / 3.0, scalar2=-0.5, op0=ALU.mult, op1=ALU.add), reads=['etmp'], writes=['etmp2'])
        kb.op('dve', lambda e: e.tensor_tensor(out=etmp2[:], in0=etmp2[:], in1=etmp[:], op=ALU.mult), reads=['etmp', 'etmp2'], writes=['etmp2'])
        kb.op('dve', lambda e: e.tensor_scalar_add(out=etmp2[:], in0=etmp2[:], scalar1=1.0), reads=['etmp2'], writes=['etmp2'])
        kb.op('dve', lambda e: e.tensor_tensor(out=etmp2[:], in0=etmp2[:], in1=etmp[:], op=ALU.mult), reads=['etmp', 'etmp2'], writes=['etmp2'])
        kb.op('dve', lambda e: e.tensor_scalar_mul(out=cdl[:], in0=etmp2[:], scalar1=-8.0), reads=['etmp2'], writes=['cdl'])
        kb.op('dve', lambda e: e.tensor_scalar_mul(out=cdl2[:], in0=etmp2[:], scalar1=-16.0), reads=['etmp2'], writes=['cdl'])
    dump('modc', modc[:], 'modc')
    kb.barrier()
    if stop == '0':
        return finish()
    ringW = Ring(4, 32, 128)

    hT_guard = nc.sbuf_tensor("hT", [128, KT, T], BF16)
    hT = hT_guard.__enter__()

    with nc.sbuf_tensor("xtok", [128, 4, D], F32) as xtok, nc.sbuf_tensor("xr", [128, 4, HALF], F32) as xr, \
            nc.sbuf_tensor("tabS", [128, 8, 64], F32) as tabS, nc.sbuf_tensor("tabC", [128, 8, 64], F32) as tabC, \
            nc.sbuf_tensor("om", [128, 8], F32) as om, nc.sbuf_tensor("targ", [128, 8, 64], F32) as targ, \
            nc.sbuf_tensor("tk", [128, 8, 64], F32) as tk:
        kb.op('act', lambda e: e.activation(out=om[:], in_=cst[:, C_JIDX:C_JIDX + 8], func=AF.Exp, scale=-float(np.log(10000.0)) / 1024.0),
              reads=['cst'], writes=['om'])
        for tab, off in ((tabS, 0.0), (tabC, 0.25)):
            kb.op('dve', lambda e: e.tensor_tensor(out=targ[:], in0=om[:].unsqueeze(2).to_broadcast([128, 8, 64]),
                                                   in1=cst[:, C_NIDX:C_NIDX + 64].unsqueeze(1).to_broadcast([128, 8, 64]), op=ALU.mult),
                  reads=['om', 'cst'], writes=['targ'])
            kb.op('dve', lambda e, off=off: e.tensor_scalar(out=targ[:], in0=targ[:], scalar1=1.0 / (2 * np.pi), scalar2=off, op0=ALU.mult, op1=ALU.add),
                  reads=['targ'], writes=['targ'])
            kb.op('dve', lambda e: e.memset(tk[:], 0.0), writes=['tk'])
            for m in range(1, 12):
                kb.op('dve', lambda e, m=m: e.scalar_tensor_tensor(out=tk[:], in0=targ[:], scalar=float(m), in1=tk[:], op0=ALU.is_ge, op1=ALU.add),
                      reads=['targ', 'tk'], writes=['tk'])
            kb.op('dve', lambda e: e.tensor_tensor(out=targ[:], in0=targ[:], in1=tk[:], op=ALU.subtract), reads=['targ', 'tk'], writes=['targ'])
            kb.op('dve', lambda e: e.tensor_scalar(out=tk[:], in0=targ[:], scalar1=0.5, scalar2=None, op0=ALU.is_gt), reads=['targ'], writes=['tk'])
            kb.op('dve', lambda e: e.tensor_tensor(out=targ[:], in0=targ[:], in1=tk[:], op=ALU.subtract), reads=['targ', 'tk'], writes=['targ'])
            kb.op('act', lambda e, tab=tab: e.activation(out=tab[:], in_=targ[:], func=AF.Sin, scale=float(2 * np.pi)), reads=['targ'], writes=['tab'])
        posflag = pcol(P_FLAG + 1)
        for hf in range(2):
            for j in range(4):
                tt = hf * 4 + j
                kb.dma('sp', out=xtok[:, j, :], in_=x_d[tt * 128:(tt + 1) * 128, :], writes=[('xtok', j)])
            for ft in range(KT):
                ps, pk = newps()
                for j in range(4):
                    kb.op('pe', lambda e, j=j: e.transpose(ps[:, j * 128:(j + 1) * 128], xtok[:, j, ft * 128:(ft + 1) * 128], ident),
                          reads=[('xtok', j), 'cst'], writes=[pk], inc=(j == 3))
                qd, jt = ft // 8, ft % 8
                tab = tabS if qd in (0, 2) else tabC
                if qd < 2:
                    pos_ap = tab[:, jt, 8 * hf:8 * hf + 8].unsqueeze(2).to_broadcast([128, 8, 64])
                else:
                    pos_ap = tab[:, jt, :].unsqueeze(1).to_broadcast([128, 8, 64])
                xrv = xr[:, ft % 4, :]
                xk = ('xr', ft % 4)
                kb.op('dve', lambda e: e.scalar_tensor_tensor(out=xrv.rearrange("p (a b) -> p a b", a=8), in0=pos_ap, scalar=posflag,
                                                              in1=ps[:, :].rearrange("p (a b) -> p a b", a=8), op0=ALU.mult, op1=ALU.add),
                      reads=[pk, 'tab', 'prm'], writes=[xk])
                kb.op('act', lambda e: e.activation(out=hT[:, ft, hf * HALF:(hf + 1) * HALF], in_=xrv, func=AF.Identity,
                                                    scale=sc1m[:, ft:ft + 1], bias=shm[:, ft:ft + 1]),
                      reads=[xk, 'sc1m', 'modc'], writes=['hT'])
                kb.dma('sp', out=xT_scr[ft, :, hf * HALF:(hf + 1) * HALF], in_=xrv, reads=[xk], writes=['xT_scr'])
    dump('hT', hT[:, 0:4, :], 'hT')
    if stop == 'A':
        return finish()

    pm = nc.sbuf_tensor("pmF", [128, 14 * T], F32)
    pmF = pm.__enter__()
    pmb = nc.sbuf_tensor("pmB", [128, 12 * T], BF16)
    pmB = pmb.__enter__()
    xpg = nc.sbuf_tensor("xpad", [128, NSEG, SEG + 4], F32)
    xpad = xpg.__enter__()
    smallT = pmF[0:64, 13 * T:14 * T].rearrange("p (c k) -> p c k", c=NCH)
    tmg = nc.sbuf_tensor("tokm", [64, 4, NCH, 32], F32)
    tokm = tmg.__enter__()
    ntg = nc.sbuf_tensor("negA", [64, 32], F32)
    negA = ntg.__enter__()
    s3g = nc.sbuf_tensor("S32", [128, 2, 128], F32)
    S32 = s3g.__enter__()
    sbg = nc.sbuf_tensor("Sbf", [128, 2, 128], BF16)
    Sbf = sbg.__enter__()
    sog = nc.sbuf_tensor("Sout", [128, 4, 128], F32)
    Sout = sog.__enter__()
    gwg = nc.sbuf_tensor("gw", [128, 2, 4, 128], BF16)
    gwt = gwg.__enter__()
    hig = nc.sbuf_tensor("hinit", [128, 2], F32)
    hinit = hig.__enter__()
    rrg = nc.sbuf_tensor("RR", [64, 4, 128], BF16)
    RR = rrg.__enter__()
    vng = nc.sbuf_tensor("VN", [64, 4, 128], BF16)
    VN = vng.__enter__()

    def FB(i):
        return pmF[:, i * T:(i + 1) * T], ('F', i)

    def BB(i):
        return pmB[:, i * T:(i + 1) * T], ('B', i)

    chain = pcol(P_FLAG)
    kb.op('dve', lambda e: e.memset(xpad[:], 0.0), writes=['xpad'])

    wsm, wsk = ringW.load(wview(win_d, 0, 32, 12288, 64), 32, 64)
    for half in range(2):
        ps, pk = newps()
        for cc in range(8):
            c = half * 8 + cc
            for kt in range(KT):
                kb.op('pe', lambda e, kt=kt, c=c, cc=cc: e.matmul(ps[0:64, cc * 64:(cc + 1) * 64], hT[:, kt, c * 64:(c + 1) * 64], wsm[:, kt, :],
                                                                  start=(kt == 0), stop=(kt == KT - 1)),
                      reads=['hT', wsk], writes=[pk], inc=(kt == KT - 1 and cc == 7))
        kb.op('act', lambda e: e.activation(out=smallT[:, half * 8:(half + 1) * 8, :], in_=ps[0:64, :].rearrange("p (a b) -> p a b", a=8), func=AF.Identity),
              reads=[pk], writes=[('F', 13)])
    if stop == 'S1':
        dump('tokm', smallT[:, :, 0:32].rearrange("p (a c) k -> p a c k", a=4), [('F', 13)]) if False else None
        return finish()
    TB, TGC, TKD, TTMP = range(4)
    TG = TKD
    TGT = TTMP
    kb.op('act', lambda e: e.activation(out=tokm[:, TB], in_=smallT[:, :, 0:32], func=AF.Sigmoid), reads=[('F', 13)], writes=[('tokm', TB)])
    kb.op('act', lambda e: e.activation(out=negA[:], in_=prm[0:64, P_ALOG:P_ALOG + 32], func=AF.Exp), reads=['prm'], writes=['negA'])
    kb.op('dve', lambda e: e.tensor_tensor(out=tokm[:, TTMP], in0=smallT[:, :, 32:64], in1=prm[0:64, P_DTB:P_DTB + 32].unsqueeze(1).to_broadcast([64, NCH, 32]), op=ALU.add),
          reads=[('F', 13), 'prm'], writes=[('tokm', TTMP)])
    kb.op('act', lambda e: e.activation(out=tokm[:, TTMP], in_=tokm[:, TTMP], func=AF.Exp), reads=[('tokm', TTMP)], writes=[('tokm', TTMP)])
    kb.op('act', lambda e: e.activation(out=tokm[:, TTMP], in_=tokm[:, TTMP], func=AF.Ln, bias=1.0), reads=[('tokm', TTMP)], writes=[('tokm', TTMP)])
    kb.op('dve', lambda e: e.scalar_tensor_tensor(out=tokm[:, TG], in0=tokm[:, TTMP], scalar=-1.0, in1=negA[:].unsqueeze(1).to_broadcast([64, NCH, 32]), op0=ALU.mult, op1=ALU.mult),
          reads=[('tokm', TTMP), 'negA'], writes=[('tokm', TG)])
    if stop == 'S2':
        dump('tokm', tokm[:].rearrange("p a b c -> p (a b c)"), [('tokm', i) for i in range(4)])
        return finish()
    ps, pk = newps()
    ps2, pk2 = newps()
    for c in range(NCH):
        for d in range(2):
            tri = cst[0:64, (C_TRIF if d == 0 else C_TRIB):(C_TRIF if d == 0 else C_TRIB) + 64]
            last = (c == NCH - 1 and d == 1)
            kb.op('pe', lambda e, c=c, d=d, tri=tri: e.matmul(ps[0:64, c * 32 + d * 16:c * 32 + d * 16 + 16], tri, tokm[:, TG, c, d * 16:(d + 1) * 16], start=True, stop=True),
                  reads=['cst', ('tokm', TG)], writes=[pk], inc=False)
            kb.op('pe', lambda e, c=c, d=d: e.matmul(ps2[0:64, c * 32 + d * 16:c * 32 + d * 16 + 16], ones[0:64, 0:64], tokm[:, TG, c, d * 16:(d + 1) * 16], start=True, stop=True),
                  reads=['cst', ('tokm', TG)], writes=[pk2], inc=last)
    kb.op('dve', lambda e: e.tensor_copy(out=tokm[:, TGC], in_=ps[0:64, :].rearrange("p (a b) -> p a b", a=NCH)), reads=[pk], writes=[('tokm', TGC)])
    kb.op('dve', lambda e: e.tensor_copy(out=tokm[:, TGT], in_=ps2[0:64, :].rearrange("p (a b) -> p a b", a=NCH)), reads=[pk2], writes=[('tokm', TGT)])
    kb.op('dve', lambda e: e.tensor_tensor(out=tokm[:, TKD], in0=tokm[:, TGT], in1=tokm[:, TGC], op=ALU.subtract), reads=[('tokm', TGT), ('tokm', TGC)], writes=[('tokm', TKD)])
    kb.op('act', lambda e: e.activation(out=tokm[:, TKD], in_=tokm[:, TKD], func=AF.Exp), reads=[('tokm', TKD)], writes=[('tokm', TKD)])
    dump('tokm', tokm[:].rearrange("p a b c -> p (a b c)"), [('tokm', i) for i in range(4)])
    if stop == 'S':
        return finish()

    ringM = ringW

    def proj_fm(col0):
        wv, wk = ringM.load(wview(win_d, 0, 32, col0, 128))
        pa, ka = newps()
        pb, kbk = newps()
        for kt in range(KT):
            kb.op('pe', lambda e, kt=kt: e.matmul(pa[:, :], wv[:, kt, :], hT[:, kt, 0:HALF], start=(kt == 0), stop=(kt == KT - 1)),
                  reads=['hT', wk], writes=[ka], inc=False)
            kb.op('pe', lambda e, kt=kt: e.matmul(pb[:, :], wv[:, kt, :], hT[:, kt, HALF:T], start=(kt == 0), stop=(kt == KT - 1)),
                  reads=['hT', wk], writes=[kbk], inc=(kt == KT - 1))
        return (pa, ka), (pb, kbk)

    def conv_from_psum(pp, cw_off, bias_ap, out_ap, out_key):
        for h2, (p_, k_) in enumerate(pp):
            kb.op('act', lambda e, h2=h2, p_=p_: e.activation(out=xpad[:, 2 * h2:2 * h2 + 2, 2:2 + SEG], in_=p_[:, :].rearrange("p (a b) -> p a b", a=2), func=AF.Identity),
                  reads=[k_], writes=['xpad'])
        kb.op('dve', lambda e: e.tensor_scalar(out=xpad[:, 1:4, 0:2], in0=xpad[:, 0:3, SEG:SEG + 2], scalar1=chain, scalar2=None, op0=ALU.mult),
              reads=['xpad', 'prm'], writes=['xpad'])
        kb.op('dve', lambda e: e.tensor_scalar(out=xpad[:, 0:3, SEG + 2:SEG + 3], in0=xpad[:, 1:4, 2:3], scalar1=chain, scalar2=None, op0=ALU.mult),
              reads=['xpad', 'prm'], writes=['xpad'])
        o3 = out_ap.rearrange("p (a b) -> p a b", a=NSEG)
        if bias_ap is None:
            kb.op('dve', lambda e: e.tensor_scalar(out=o3, in0=xpad[:, :, 0:SEG], scalar1=pcol(cw_off), scalar2=None, op0=ALU.mult),
                  reads=['xpad', 'prm'], writes=[out_key])
        else:
            kb.op('dve', lambda e: e.tensor_scalar(out=o3, in0=xpad[:, :, 0:SEG], scalar1=pcol(cw_off), scalar2=bias_ap, op0=ALU.mult, op1=ALU.add),
                  reads=['xpad', 'prm'], writes=[out_key])
        for j in range(1, 4):
            kb.op('dve', lambda e, j=j: e.scalar_tensor_tensor(out=o3, in0=xpad[:, :, j:j + SEG], scalar=pcol(cw_off + j), in1=o3, op0=ALU.mult, op1=ALU.add),
                  reads=['xpad', 'prm', out_key], writes=[out_key])

    def l2norm_to_bf(src, sk, tmp, tk_, rn, rk, dst, dk_, scale):
        kb.op('act', lambda e: e.activation(out=tmp, in_=src, func=AF.Square), reads=[sk], writes=[tk_])
        for h2 in range(2):
            ps, pk = newps()
            kb.op('pe', lambda e, h2=h2: e.matmul(ps[:, :], ones, tmp[:, h2 * HALF:(h2 + 1) * HALF], start=True, stop=True), reads=['cst', tk_], writes=[pk])
            kb.op('act', lambda e, h2=h2: e.activation(out=rn[:, h2 * HALF:(h2 + 1) * HALF], in_=ps[:, :], func=AF.Sqrt, bias=RMS_EPS), reads=[pk], writes=[rk])
        kb.op('dve', lambda e: e.reciprocal(out=rn, in_=rn), reads=[rk], writes=[rk])
        kb.op('dve', lambda e: e.scalar_tensor_tensor(out=dst, in0=src, scalar=float(scale), in1=rn, op0=ALU.mult, op1=ALU.mult), reads=[sk, rk], writes=[dk_])

    for n in range(NH):
        gws = gwt[:, n % 2]
        gwk = ('gw', n % 2)
        kb.dma('pool', out=gws, in_=lgw_d[n], writes=[gwk])
        xc, xck = FB(0)
        xcb, xcbk = BB(0)
        gy, gyk = BB(1)
        pp = proj_fm(n * 128)
        conv_from_psum(pp, P_LCW + n * 4, pcol(P_LCB + n), xc, xck)
        kb.op('act', lambda e: e.activation(out=xcb, in_=xc, func=AF.Identity), reads=[xck], writes=[xcbk])
        pp = proj_fm(2048 + n * 128)
        for h2, (p_, k_) in enumerate(pp):
            kb.op('act', lambda e, h2=h2, p_=p_: e.activation(out=gy[:, h2 * HALF:(h2 + 1) * HALF], in_=p_[:, :], func=AF.Gelu), reads=[k_], writes=[gyk])
        hbufs = []
        for d in range(2):
            A_, Ak = FB(1 + d * 4)
            S_, Sk = FB(2 + d * 4)
            B_, Bk = FB(3 + d * 4)
            H_, Hk = FB(4 + d * 4)
            for g in range(2):
                for h2 in range(2):
                    ps, pk = newps()
                    kb.op('pe', lambda e, h2=h2, g=g: e.matmul(ps[:, :], gws[:, d * 2 + g, :], xcb[:, h2 * HALF:(h2 + 1) * HALF], start=True, stop=True),
                          reads=[gwk, xcbk], writes=[pk])
                    dst, dk_ = (A_, Ak) if g == 0 else (B_, Bk)
                    kb.op('act', lambda e, h2=h2, dst=dst, g=g: e.activation(out=dst[:, h2 * HALF:(h2 + 1) * HALF], in_=ps[:, :], func=AF.Sigmoid,
                                                                           bias=pcol(P_LGB + n * 4 + d * 2 + g)),
                          reads=[pk, 'prm'], writes=[dk_])
            kb.op('act', lambda e: e.activation(out=S_, in_=A_, func=AF.Exp, scale=cdl2[:, n * 2 + d:n * 2 + d + 1]), reads=[Ak, 'cdl'], writes=[Sk])
            kb.op('act', lambda e: e.activation(out=A_, in_=A_, func=AF.Exp, scale=cdl[:, n * 2 + d:n * 2 + d + 1]), reads=[Ak, 'cdl'], writes=[Ak])
            kb.op('act', lambda e: e.activation(out=S_, in_=S_, func=AF.Sqrt, scale=-1.0, bias=1.0), reads=[Sk], writes=[Sk])
            kb.op('dve', lambda e: e.tensor_tensor(out=B_, in0=B_, in1=xc, op=ALU.mult), reads=[Bk, xck], writes=[Bk])
            kb.op('dve', lambda e: e.tensor_tensor(out=B_, in0=B_, in1=S_, op=ALU.mult), reads=[Bk, Sk], writes=[Bk])
            order = range(NSEG) if d == 0 else range(NSEG - 1, -1, -1)
            for si, s in enumerate(order):
                lo_, hi_ = s * SEG, (s + 1) * SEG
                if si == 0:
                    init = pcol(P_H0 + n * 2 + d)
                    ik = 'prm'
                else:
                    init = hinit[:, d:d + 1]
                    ik = ('hinit', d)
                if d == 0:
                    kb.op('dve', lambda e, lo_=lo_, hi_=hi_, init=init: e.tensor_tensor_scan(out=H_[:, lo_:hi_], data0=A_[:, lo_:hi_], data1=B_[:, lo_:hi_], initial=init, op0=ALU.mult, op1=ALU.add),
                          reads=[Ak, Bk, ik], writes=[Hk])
                    endcol = H_[:, hi_ - 1:hi_]
                else:
                    kb.op('dve', lambda e, lo_=lo_, hi_=hi_, init=init: e.tensor_tensor_scan(out=H_[:, hi_ - 1:lo_ - 1 if lo_ > 0 else None:-1], data0=A_[:, hi_ - 1:lo_ - 1 if lo_ > 0 else None:-1],
                                                                                       data1=B_[:, hi_ - 1:lo_ - 1 if lo_ > 0 else None:-1], initial=init, op0=ALU.mult, op1=ALU.add),
                          reads=[Ak, Bk, ik], writes=[Hk])
                    endcol = H_[:, lo_:lo_ + 1]
                sidx = (n * NSEG + s) * 2 + d
                kb.op('act', lambda e, endcol=endcol, sidx=sidx: e.activation(out=stl[:, sidx:sidx + 1], in_=endcol, func=AF.Identity), reads=[Hk], writes=['stl'])
                if si < NSEG - 1:
                    kb.op('dve', lambda e, endcol=endcol: e.tensor_tensor(out=hinit[:, d:d + 1], in0=endcol, in1=chain, op=ALU.mult), reads=[Hk, 'prm'], writes=[('hinit', d)])
            hbufs.append((H_, Hk))
        lo, lok = BB(2 + (n % 2))
        tmpf, tmpk = FB(1)
        kb.op('dve', lambda e: e.tensor_tensor(out=tmpf, in0=hbufs[0][0], in1=hbufs[1][0], op=ALU.add), reads=[hbufs[0][1], hbufs[1][1]], writes=[tmpk])
        kb.op('dve', lambda e: e.tensor_tensor(out=lo, in0=tmpf, in1=gy, op=ALU.mult), reads=[tmpk, gyk], writes=[lok])
        kb.dma('sp', out=lru_scr[n], in_=lo, reads=[lok], writes=['lru_scr'])
        if n == 0:
            dump('lru0', lo, lok)
            if stop == 'L0':
                return finish()

        qs, qsk = FB(0)
        t1, t1k = FB(1)
        t2, t2k = FB(2)
        qT, qTk = BB(4)
        kT, kTk = BB(5)
        zs, zsk = BB(6)
        pp = proj_fm(4096 + n * 128)
        conv_from_psum(pp, P_DCW + 0 * 64 + n * 4, None, qs, qsk)
        kb.op('act', lambda e: e.activation(out=qs, in_=qs, func=AF.Silu), reads=[qsk], writes=[qsk])
        l2norm_to_bf(qs, qsk, t1, t1k, t2, t2k, qT, qTk, 128.0 ** -0.5)
        pp = proj_fm(6144 + n * 128)
        conv_from_psum(pp, P_DCW + 1 * 64 + n * 4, None, qs, qsk)
        kb.op('act', lambda e: e.activation(out=qs, in_=qs, func=AF.Silu), reads=[qsk], writes=[qsk])
        l2norm_to_bf(qs, qsk, t1, t1k, t2, t2k, kT, kTk, 1.0)
        pp = proj_fm(8192 + n * 128)
        conv_from_psum(pp, P_DCW + 2 * 64 + n * 4, None, qs, qsk)
        kb.op('act', lambda e: e.activation(out=qs, in_=qs, func=AF.Silu), reads=[qsk], writes=[qsk])
        Vc = pmF[0:64, 3 * T:5 * T].rearrange("p (c v) -> p c v", c=NCH)
        Vck = ('F', 3)
        Vck2 = ('F', 4)
        for grp in range(4):
            ps, pk = newps()
            for cc in range(4):
                c = grp * 4 + cc
                kb.op('pe', lambda e, c=c, cc=cc: e.transpose(ps[0:64, cc * 128:(cc + 1) * 128], qs[:, c * 64:(c + 1) * 64], ident), reads=[qsk, 'cst'], writes=[pk], inc=(cc == 3))
            kb.op('act', lambda e, grp=grp: e.activation(out=Vc[:, grp * 4:(grp + 1) * 4, :], in_=ps[0:64, :].rearrange("p (a b) -> p a b", a=4), func=AF.Identity),
                  reads=[pk], writes=[Vck, Vck2])
        ktok = pmB[0:64, 7 * T:9 * T].rearrange("p (c v) -> p c v", c=NCH)
        ktk = [('B', 7), ('B', 8)]
        for grp in range(4):
            ps, pk = newps()
            psv = ps[:, :].bitcast(BF16)
            for cc in range(4):
                c = grp * 4 + cc
                kb.op('pe', lambda e, c=c, cc=cc: e.transpose(psv[0:64, cc * 128:(cc + 1) * 128], kT[:, c * 64:(c + 1) * 64], identb[:]), reads=[kTk, 'identb'], writes=[pk], inc=(cc == 3))
            kb.op('act', lambda e, grp=grp: e.activation(out=ktok[:, grp * 4:(grp + 1) * 4, :], in_=psv[0:64, 0:512].rearrange("p (a b) -> p a b", a=4), func=AF.Identity),
                  reads=[pk], writes=ktk)
        pp = proj_fm(10240 + n * 128)
        for h2, (p_, k_) in enumerate(pp):
            kb.op('act', lambda e, h2=h2, p_=p_: e.activation(out=zs[:, h2 * HALF:(h2 + 1) * HALF], in_=p_[:, :], func=AF.Silu), reads=[k_], writes=[zsk])
        Oacc, Oack = FB(0)
        if stop == 'D1':
            return finish()

        for d in range(2):
            dh = d * 16 + n
            nmD = cst[0:64, (C_NMF if d == 0 else C_NMT):(C_NMF if d == 0 else C_NMT) + 64]
            nmDT = cst[0:64, (C_NMT if d == 0 else C_NMF):(C_NMT if d == 0 else C_NMF) + 64]
            smY = cst[0:64, (C_SMF if d == 0 else C_SMT):(C_SMF if d == 0 else C_SMT) + 64]
            smW = cst[0:64, (C_SMT if d == 0 else C_SMF):(C_SMT if d == 0 else C_SMF) + 64]
            id64 = cst[0:64, C_ID:C_ID + 64]

            def c3(ap):
                return ap.rearrange("p (c i) -> p c i", c=NCH)

            def bc_f(ap2):
                return ap2.unsqueeze(1).to_broadcast([64, NCH, 64])

            def bc_col(slot):
                return tokm[:, slot, :, dh:dh + 1].to_broadcast([64, NCH, 64])

            GCb, GCbk = FB(5)
            Eg, Egk = FB(6)
            DG, DGk = FB(1)
            BbS, BbSk = FB(2)
            DT, DTk = FB(7)
            Dn, Dnk = FB(8)
            Wa, Wak = FB(9)
            Ya, Yak = FB(10)
            Wb, Wbk = FB(11)
            Yb, Ybk = FB(12)
            Z_, Zk = FB(13)
            QgT, QgTk = BB(9)
            KgT, KgTk = BB(10)
            MTAT, MTATk = BB(11)
            Kd = pmB[0:64, 0 * T:2 * T].rearrange("p (c v) -> p c v", c=NCH)
            Kdk = [('B', 0), ('B', 1)]
            AT, ATk = BB(2 + ((n + 1) % 2))

            for which, dst, dstk in ((TGC, GCb, GCbk), (TB, BbS, BbSk)):
                kb.op('dve', lambda e, which=which: e.tensor_tensor(out=c3(DG[0:64, :]), in0=bc_f(id64), in1=bc_col(which), op=ALU.mult),
                      reads=['cst', ('tokm', which)], writes=[DGk])
                for h2 in range(2):
                    ps, pk = newps()
                    kb.op('pe', lambda e, h2=h2: e.matmul(ps[:, :], ones[0:64, :], DG[0:64, h2 * HALF:(h2 + 1) * HALF], start=True, stop=True), reads=['cst', DGk], writes=[pk])
                    if which == TGC:
                        kb.op('act', lambda e, h2=h2: e.activation(out=Eg[:, h2 * HALF:(h2 + 1) * HALF], in_=ps[:, :], func=AF.Exp), reads=[pk], writes=[Egk])
                        kb.op('act', lambda e, h2=h2: e.activation(out=GCb[:, h2 * HALF:(h2 + 1) * HALF], in_=ps[:, :], func=AF.Identity), reads=[pk], writes=[GCbk])
                    else:
                        kb.op('dve', lambda e, h2=h2: e.tensor_tensor(out=BbS[0:64, h2 * HALF:(h2 + 1) * HALF].rearrange("p (c i) -> p c i", c=8), in0=ps[0:64, :].rearrange("p (c i) -> p c i", c=8),
                                                                      in1=smW.unsqueeze(1).to_broadcast([64, 8, 64]), op=ALU.mult),
                              reads=[pk, 'cst'], writes=[BbSk])
            kb.op('dve', lambda e: e.tensor_tensor(out=QgT, in0=qT, in1=Eg, op=ALU.mult), reads=[qTk, Egk], writes=[QgTk])
            kb.op('dve', lambda e: e.tensor_tensor(out=KgT, in0=kT, in1=Eg, op=ALU.mult), reads=[kTk, Egk], writes=[KgTk])
            kb.op('dve', lambda e: e.tensor_tensor(out=Kd, in0=ktok, in1=tokm[:, TKD, :, dh:dh + 1].to_broadcast([64, NCH, 128]), op=ALU.mult),
                  reads=ktk + [('tokm', TKD)], writes=Kdk)
            raw = GCb[0:64, :]
            kb.op('dve', lambda e: e.tensor_tensor(out=c3(raw), in0=c3(raw), in1=bc_col(TGC), op=ALU.subtract), reads=[GCbk, ('tokm', TGC)], writes=[GCbk])
            kb.op('dve', lambda e: e.scalar_tensor_tensor(out=c3(DT[0:64, :]), in0=c3(raw), scalar=0.0, in1=bc_f(nmDT), op0=ALU.min, op1=ALU.add), reads=[GCbk, 'cst'], writes=[DTk])
            kb.op('act', lambda e: e.activation(out=DT[0:64, :], in_=DT[0:64, :], func=AF.Exp), reads=[DTk], writes=[DTk])
            kb.op('dve', lambda e: e.scalar_tensor_tensor(out=c3(Dn[0:64, :]), in0=c3(raw), scalar=0.0, in1=bc_f(nmD), op0=ALU.max, op1=ALU.subtract), reads=[GCbk, 'cst'], writes=[Dnk])
            kb.op('act', lambda e: e.activation(out=Dn[0:64, :], in_=Dn[0:64, :], func=AF.Exp, scale=-1.0), reads=[Dnk], writes=[Dnk])
            if stop == 'D2':
                return finish()
            gps = []
            qps = []
            for h2 in range(2):
                ps, pk = newps()
                for cc in range(8):
                    c = h2 * 8 + cc
                    kb.op('pe', lambda e, c=c, cc=cc: e.matmul(ps[0:64, cc * 64:(cc + 1) * 64], kT[:, c * 64:(c + 1) * 64], kT[:, c * 64:(c + 1) * 64], start=True, stop=True),
                          reads=[kTk], writes=[pk], inc=(cc == 7))
                gps.append((ps, pk))
                ps, pk = newps()
                for cc in range(8):
                    c = h2 * 8 + cc
                    kb.op('pe', lambda e, c=c, cc=cc: e.matmul(ps[0:64, cc * 64:(cc + 1) * 64], kT[:, c * 64:(c + 1) * 64], qT[:, c * 64:(c + 1) * 64], start=True, stop=True),
                          reads=[kTk, qTk], writes=[pk], inc=(cc == 7))
                qps.append((ps, pk))
            for h2 in range(2):
                kb.op('dve', lambda e, h2=h2: e.tensor_tensor(out=AT[0:64, h2 * HALF:(h2 + 1) * HALF], in0=qps[h2][0][0:64, :], in1=DT[0:64, h2 * HALF:(h2 + 1) * HALF], op=ALU.mult),
                      reads=[qps[h2][1], DTk], writes=[ATk])
            kb.op('dve', lambda e: e.scalar_tensor_tensor(out=DT[0:64, :], in0=DT[0:64, :], scalar=-1.0, in1=BbS[0:64, :], op0=ALU.mult, op1=ALU.mult), reads=[DTk, BbSk], writes=[DTk])
            kb.op('dve', lambda e: e.tensor_tensor(out=c3(DG[0:64, :]), in0=bc_f(smY), in1=bc_col(TB), op=ALU.mult), reads=['cst', ('tokm', TB)], writes=[DGk])
            kb.op('dve', lambda e: e.scalar_tensor_tensor(out=Dn[0:64, :], in0=Dn[0:64, :], scalar=-1.0, in1=DG[0:64, :], op0=ALU.mult, op1=ALU.mult), reads=[Dnk, DGk], writes=[Dnk])
            for h2 in range(2):
                sl = slice(h2 * HALF, (h2 + 1) * HALF)
                kb.op('dve', lambda e, h2=h2, sl=sl: e.tensor_tensor(out=Wa[0:64, sl], in0=gps[h2][0][0:64, :], in1=DT[0:64, sl], op=ALU.mult), reads=[gps[h2][1], DTk], writes=[Wak])
                kb.op('dve', lambda e, h2=h2, sl=sl: e.tensor_tensor(out=Ya[0:64, sl], in0=gps[h2][0][0:64, :], in1=Dn[0:64, sl], op=ALU.mult), reads=[gps[h2][1], Dnk], writes=[Yak])
            kb.op('dve', lambda e: e.tensor_tensor(out=c3(Z_[0:64, :]), in0=c3(Wa[0:64, :]), in1=bc_f(id64), op=ALU.add), reads=[Wak, 'cst'], writes=[Zk])
            Wc, Wck, Yc, Yck = Wa, Wak, Ya, Yak
            Wn, Wnk, Yn, Ynk = Wb, Wbk, Yb, Ybk
            for lvl in range(1, 6):
                for h2 in range(2):
                    ps, pk = newps()
                    for cc in range(8):
                        c = h2 * 8 + cc
                        sl = slice(c * 64, (c + 1) * 64)
                        kb.op('pe', lambda e, cc=cc, sl=sl: e.matmul(ps[0:64, cc * 64:(cc + 1) * 64], Wc[0:64, sl], Yc[0:64, sl], start=True, stop=True), reads=[Wck, Yck], writes=[pk], inc=(cc == 7))
                    kb.op('act', lambda e, h2=h2: e.activation(out=Yn[0:64, h2 * HALF:(h2 + 1) * HALF], in_=ps[0:64, :], func=AF.Identity), reads=[pk], writes=[Ynk])
                    if lvl < 5:
                        ps, pk = newps()
                        for cc in range(8):
                            c = h2 * 8 + cc
                            sl = slice(c * 64, (c + 1) * 64)
                            kb.op('pe', lambda e, cc=cc, sl=sl: e.matmul(ps[0:64, cc * 64:(cc + 1) * 64], Yc[0:64, sl], Wc[0:64, sl], start=True, stop=True), reads=[Wck, Yck], writes=[pk], inc=(cc == 7))
                        kb.op('act', lambda e, h2=h2: e.activation(out=Wn[0:64, h2 * HALF:(h2 + 1) * HALF], in_=ps[0:64, :], func=AF.Identity), reads=[pk], writes=[Wnk])
                for h2 in range(2):
                    ps, pk = newps()
                    for cc in range(8):
                        c = h2 * 8 + cc
                        sl = slice(c * 64, (c + 1) * 64)
                        kb.op('pe', lambda e, cc=cc, sl=sl: e.matmul(ps[0:64, cc * 64:(cc + 1) * 64], Yn[0:64, sl], Z_[0:64, sl], start=True, stop=True), reads=[Ynk, Zk], writes=[pk], inc=(cc == 7))
                    kb.op('dve', lambda e, h2=h2: e.tensor_tensor(out=Z_[0:64, h2 * HALF:(h2 + 1) * HALF], in0=Z_[0:64, h2 * HALF:(h2 + 1) * HALF], in1=ps[0:64, :], op=ALU.add), reads=[pk, Zk], writes=[Zk])
                Wc, Wck, Yc, Yck, Wn, Wnk, Yn, Ynk = Wn, Wnk, Yn, Ynk, Wc, Wck, Yc, Yck
            MT = MTAT
            kb.op('dve', lambda e: e.tensor_tensor(out=c3(MT[0:64, :]), in0=c3(Z_[0:64, :]), in1=bc_col(TB), op=ALU.mult), reads=[Zk, ('tokm', TB)], writes=[MTATk])
            if n == 0 and d == 0:
                dump('MT', MT[0:64, :], MTATk)
                dump('AT', AT[0:64, :], ATk)
                dump('Eg', Eg, Egk)

            if stop == 'D3':
                return finish()
            Sd32 = S32[:, d, :]
            Sdb = Sbf[:, d, :]
            sk32 = ('S32', d)
            skb = ('Sbf', d)
            kb.dma('sp', out=Sd32, in_=s0_d[d, n], writes=[sk32])
            kb.op('act', lambda e: e.activation(out=Sdb, in_=Sd32, func=AF.Identity), reads=[sk32], writes=[skb])
            corder = range(NCH) if d == 0 else range(NCH - 1, -1, -1)
            ops_bank = None
            for ci, c in enumerate(corder):
                t0 = c * 64
                seg = c // 4
                first_in_seg = (c % 4 == 0) if d == 0 else (c % 4 == 3)
                last_in_seg = (c % 4 == 3) if d == 0 else (c % 4 == 0)
                if first_in_seg and ci > 0:
                    kb.op('dve', lambda e: e.tensor_scalar(out=Sd32, in0=Sd32, scalar1=chain, scalar2=None, op0=ALU.mult), reads=[sk32, 'prm'], writes=[sk32])
                    kb.op('act', lambda e: e.activation(out=Sdb, in_=Sd32, func=AF.Identity), reads=[sk32], writes=[skb])
                if ci % 8 == 0:
                    ops_bank = resps(6 + (ci // 8) % 2)
                ob, obk = ops_bank
                ps, pk = newps()
                kb.op('pe', lambda e, t0=t0: e.matmul(ps[0:64, 0:128], KgT[:, t0:t0 + 64], Sdb, start=True, stop=True), reads=[KgTk, skb], writes=[pk])
                rslot = ci % 2
                Rv = RR[:, d * 2 + rslot, :]
                Rkey = ('RR', d * 2 + rslot)
                kb.op('dve', lambda e, c=c, Rv=Rv: e.tensor_tensor(out=Rv, in0=Vc[:, c, :], in1=ps[0:64, 0:128], op=ALU.subtract), reads=[Vck, Vck2, pk], writes=[Rkey])
                ps2, pk2 = newps()
                kb.op('pe', lambda e, t0=t0, Rv=Rv: e.matmul(ps2[0:64, 0:128], MT[0:64, t0:t0 + 64], Rv, start=True, stop=True), reads=[MTATk, Rkey], writes=[pk2])
                vn = VN[:, d * 2 + rslot, :]
                vnk = ('VN', d * 2 + rslot)
                kb.op('act', lambda e, vn=vn: e.activation(out=vn, in_=ps2[0:64, 0:128], func=AF.Identity), reads=[pk2], writes=[vnk])
                col = (c % 8) * 64
                kb.op('pe', lambda e, t0=t0, col=col: e.matmul(ob[:, col:col + 64], Sdb, QgT[:, t0:t0 + 64], start=True, stop=False), reads=[skb, QgTk], writes=[obk], inc=False)
                kb.op('pe', lambda e, t0=t0, col=col, vn=vn: e.matmul(ob[:, col:col + 64], vn, AT[0:64, t0:t0 + 64], start=False, stop=True), reads=[vnk, ATk], writes=[obk])
                ps3, pk3 = newps()
                kb.op('pe', lambda e, c=c, vn=vn: e.matmul(ps3[:, 0:128], Kd[:, c, :], vn, start=True, stop=True), reads=Kdk + [vnk], writes=[pk3])
                tl = t0 + 63 if d == 0 else t0
                kb.op('dve', lambda e, tl=tl: e.scalar_tensor_tensor(out=Sd32, in0=Sd32, scalar=Eg[:, tl:tl + 1], in1=ps3[:, 0:128], op0=ALU.mult, op1=ALU.add), reads=[sk32, Egk, pk3], writes=[sk32])
                kb.op('act', lambda e: e.activation(out=Sdb, in_=Sd32, func=AF.Identity), reads=[sk32], writes=[skb])
                if last_in_seg:
                    so = Sout[:, (ci // 4) % 4, :]
                    sok = ('Sout', (ci // 4) % 4)
                    kb.op('act', lambda e, so=so: e.activation(out=so, in_=Sd32, func=AF.Identity), reads=[sk32], writes=[sok])
                    kb.dma('sp', out=std_d[seg, d, n], in_=so, reads=[sok], writes=['std'])
                if ci % 8 == 7:
                    hb = (c // 8)
                    sl = slice(hb * HALF, (hb + 1) * HALF)
                    if d == 0:
                        kb.op('act', lambda e, sl=sl: e.activation(out=Oacc[:, sl], in_=ob[:, :], func=AF.Identity), reads=[obk], writes=[Oack])
                    else:
                        kb.op('dve', lambda e, sl=sl: e.tensor_tensor(out=Oacc[:, sl], in0=Oacc[:, sl], in1=ob[:, :], op=ALU.add), reads=[obk, Oack], writes=[Oack])
        t1, t1k = FB(1)
        t2, t2k = FB(2)
        kb.op('act', lambda e: e.activation(out=t1, in_=Oacc, func=AF.Square), reads=[Oack], writes=[t1k])
        for h2 in range(2):
            ps, pk = newps()
            kb.op('pe', lambda e, h2=h2: e.matmul(ps[:, :], ones, t1[:, h2 * HALF:(h2 + 1) * HALF], start=True, stop=True), reads=['cst', t1k], writes=[pk])
            kb.op('act', lambda e, h2=h2: e.activation(out=t2[:, h2 * HALF:(h2 + 1) * HALF], in_=ps[:, :], func=AF.Sqrt, scale=1.0 / 128.0, bias=RMS_EPS), reads=[pk], writes=[t2k])
        kb.op('dve', lambda e: e.reciprocal(out=t2, in_=t2), reads=[t2k], writes=[t2k])
        kb.op('dve', lambda e: e.scalar_tensor_tensor(out=t1, in0=Oacc, scalar=pcol(P_NW), in1=t2, op0=ALU.mult, op1=ALU.mult), reads=[Oack, t2k, 'prm'], writes=[t1k])
        dob, dobk = BB(2 + ((n + 1) % 2))
        kb.op('dve', lambda e: e.tensor_tensor(out=dob, in0=t1, in1=zs, op=ALU.mult), reads=[t1k, zsk], writes=[dobk])
        kb.dma('sp', out=dn_scr[n], in_=dob, reads=[dobk], writes=['dn_scr'])
        if n == 0:
            dump('dn0', dob, dobk)
            dump('oacc0', Oacc, Oack)
            if stop == 'M1':
                kb.dma('sp', out=stl_d, in_=stl[:], reads=['stl'], writes=['stl_d'])
                return finish()

    kb.dma('sp', out=stl_d, in_=stl[:], reads=['stl'], writes=['stl_d'])
    kb.barrier()
    for g_ in (vng, rrg, hig, gwg, sog, sbg, s3g, ntg, tmg, xpg, pmb, pm):
        g_.__exit__(None, None, None)

    ring32 = ringW
    with nc.sbuf_tensor("lod", [128, 2, NH, HALF], BF16) as lod, nc.sbuf_tensor("gsb", [128, 2, 2, HALF], F32) as gsb, \
            nc.sbuf_tensor("mgt", [128, 2, HALF], BF16) as mgt:
        for hf in range(2):
            tsl = slice(hf * HALF, (hf + 1) * HALF)
            for n in range(NH):
                kb.dma('sp', out=lod[:, 0, n, :], in_=lru_scr[n, :, tsl], reads=['lru_scr'], writes=['lod'])
                kb.dma('sp', out=lod[:, 1, n, :], in_=dn_scr[n, :, tsl], reads=['dn_scr'], writes=['lod'])
            for f in range(KT):
                pss = []
                for br, w_d in ((0, wlp_d), (1, wdp_d)):
                    v, key = ring32.load(wview(w_d, 0, 16, f * 128, 128), 16, 128)
                    ps, pk = newps()
                    for kt in range(16):
                        kb.op('pe', lambda e, kt=kt, v=v, br=br: e.matmul(ps[:, :], v[:, kt, :], lod[:, br, kt, :], start=(kt == 0), stop=(kt == 15)), reads=[key, 'lod'], writes=[pk], inc=(kt == 15))
                    pss.append((ps, pk))
                gss = []
                for g in range(2):
                    wv, wk = ring32.load(wview(win_d, 0, 32, 12352 + g * D + f * 128, 128))
                    ps, pk = newps()
                    for kt in range(KT):
                        kb.op('pe', lambda e, kt=kt, wv=wv: e.matmul(ps[:, :], wv[:, kt, :], hT[:, kt, tsl], start=(kt == 0), stop=(kt == KT - 1)), reads=[wk, 'hT'], writes=[pk], inc=(kt == KT - 1))
                    gk = ('gsb', f % 2, g)
                    kb.op('act', lambda e, g=g, ps=ps: e.activation(out=gsb[:, f % 2, g, :], in_=ps[:, :], func=AF.Sigmoid, bias=pcol(P_BBR + g * 32 + f)), reads=[pk, 'prm'], writes=[gk])
                    gss.append(gk)
                for g in range(2):
                    kb.op('dve', lambda e, g=g: e.tensor_tensor(out=gsb[:, f % 2, g, :], in0=gsb[:, f % 2, g, :], in1=pss[g][0][:, :], op=ALU.mult), reads=[gss[g], pss[g][1]], writes=[gss[g]])
                mk = ('mgt', f % 2)
                kb.op('dve', lambda e: e.tensor_tensor(out=mgt[:, f % 2, :], in0=gsb[:, f % 2, 0, :], in1=gsb[:, f % 2, 1, :], op=ALU.add), reads=gss, writes=[mk])
                kb.dma('sp', out=mg_scr[hf, f], in_=mgt[:, f % 2, :], reads=[mk], writes=['mg_scr'])
    kb.barrier()
    hT_guard.__exit__(None, None, None)

    with nc.sbuf_tensor("acc", [128, KT, HALF], F32) as acc, nc.sbuf_tensor("H2", [128, KT, HALF], BF16) as H2, \
            nc.sbuf_tensor("mgh", [128, KT, HALF], BF16) as mgh, \
            nc.sbuf_tensor("lnt", [128, 4, HALF], F32) as lnt, nc.sbuf_tensor("lns", [128, 5, HALF], F32) as lns, \
            nc.sbuf_tensor("yst", [128, 2, HALF], F32) as yst:
        ones_f = ones
        actb = mgh[:, 0:16, :]
        MGK = [('actb', j) for j in range(16)] + ['mgh']

        def ln_stats(get_tile, tag):
            s1, k1 = resps(6)
            s2, k2 = resps(7)
            for f in range(KT):
                ap, key = get_tile(f)
                sq = lnt[:, f % 2, :]
                sqk = ('lnt', f % 2)
                kb.op('act', lambda e, ap=ap, sq=sq: e.activation(out=sq, in_=ap, func=AF.Square), reads=[key], writes=[sqk])
                kb.op('pe', lambda e, ap=ap: e.matmul(s1[:, :], ones_f, ap, start=(f == 0), stop=(f == KT - 1)), reads=['cst', key], writes=[k1], inc=(f == KT - 1))
                kb.op('pe', lambda e, sq=sq: e.matmul(s2[:, :], ones_f, sq, start=(f == 0), stop=(f == KT - 1)), reads=['cst', sqk], writes=[k2], inc=True)
            mean, msq, var, rstd, nmr = (lns[:, i, :] for i in range(5))
            kb.op('dve', lambda e: e.tensor_scalar_mul(out=mean, in0=s1[:, :], scalar1=1.0 / D), reads=[k1], writes=[('lns', 0)])
            kb.op('dve', lambda e: e.tensor_tensor(out=msq, in0=mean, in1=mean, op=ALU.mult), reads=[('lns', 0)], writes=[('lns', 1)])
            kb.op('dve', lambda e: e.scalar_tensor_tensor(out=var, in0=s2[:, :], scalar=1.0 / D, in1=msq, op0=ALU.mult, op1=ALU.subtract), reads=[k2, ('lns', 1)], writes=[('lns', 2)])
            kb.op('act', lambda e: e.activation(out=var, in_=var, func=AF.Sqrt, bias=LN_EPS), reads=[('lns', 2)], writes=[('lns', 2)])
            kb.op('dve', lambda e: e.reciprocal(out=rstd, in_=var), reads=[('lns', 2)], writes=[('lns', 3)])
            kb.op('dve', lambda e: e.scalar_tensor_tensor(out=nmr, in0=mean, scalar=-1.0, in1=rstd, op0=ALU.mult, op1=ALU.mult), reads=[('lns', 0), ('lns', 3)], writes=[('lns', 4)])
            return rstd, ('lns', 3), nmr, ('lns', 4)

        ringO = ringW
        ringU = ringW
        for hf in range(2):
            tsl = slice(hf * HALF, (hf + 1) * HALF)
            kb.dma('sp', out=mgh[:], in_=mg_scr[hf].rearrange("k p t -> p k t"), reads=['mg_scr'], writes=MGK)
            for f in range(KT):
                wv, wk = ringO.load(wview(wo_d, 0, 32, f * 128, 128))
                ps, pk = newps()
                for kt in range(KT):
                    kb.op('pe', lambda e, kt=kt, wv=wv: e.matmul(ps[:, :], wv[:, kt, :], mgh[:, kt, :], start=(kt == 0), stop=(kt == KT - 1)), reads=[wk] + MGK, writes=[pk], inc=(kt == KT - 1))
                xa = lnt[:, 2 + f % 2, :]
                xak = ('lnt', 2 + f % 2)
                kb.dma('sp', out=xa, in_=xT_scr[f, :, tsl], reads=['xT_scr'], writes=[xak])
                kb.op('act', lambda e, xa=xa: e.activation(out=xa, in_=xa, func=AF.Identity, scale=float(ALPHA)), reads=[xak], writes=[xak])
                kb.op('dve', lambda e, xa=xa, ps=ps: e.scalar_tensor_tensor(out=acc[:, f, :], in0=ps[:, :], scalar=gtm[:, f:f + 1], in1=xa, op0=ALU.mult, op1=ALU.add),
                      reads=[pk, xak, 'modc'], writes=[('acc', f)])
            rstd, rk_, nmr, nk_ = ln_stats(lambda f: (acc[:, f, :], ('acc', f)), 'ln1')
            for f in range(KT):
                tt_ = lnt[:, f % 2, :]
                ttk = ('lnt', f % 2)
                kb.op('dve', lambda e, tt_=tt_: e.tensor_tensor(out=tt_, in0=acc[:, f, :], in1=rstd, op=ALU.mult), reads=[('acc', f), rk_], writes=[ttk])
                kb.op('dve', lambda e, tt_=tt_: e.tensor_tensor(out=tt_, in0=tt_, in1=nmr, op=ALU.add), reads=[ttk, nk_], writes=[ttk])
                x1t = lnt[:, 2 + f % 2, :]
                x1k = ('lnt', 2 + f % 2)
                kb.op('act', lambda e, tt_=tt_, x1t=x1t: e.activation(out=x1t, in_=tt_, func=AF.Identity, scale=pcol(P_L1G + f), bias=pcol(P_L1B + f)), reads=[ttk, 'prm'], writes=[x1k])
                kb.dma('sp', out=x1_scr[f], in_=x1t, reads=[x1k], writes=[('x1_scr', f)])
                kb.op('act', lambda e, x1t=x1t: e.activation(out=H2[:, f, :], in_=x1t, func=AF.Identity, scale=sc1f[:, f:f + 1], bias=shf[:, f:f + 1]), reads=[x1k, 'sc1f', 'modc'], writes=['H2'])
            for g in range(8):
                for j in range(16):
                    wv, wk = ringU.load(wview(wup_d, 0, 32, (g * 16 + j) * 128, 128))
                    ps, pk = newps()
                    for kt in range(KT):
                        kb.op('pe', lambda e, kt=kt, wv=wv: e.matmul(ps[:, :], wv[:, kt, :], H2[:, kt, :], start=(kt == 0), stop=(kt == KT - 1)), reads=[wk, 'H2'], writes=[pk], inc=(kt == KT - 1))
                    rl = lnt[:, j % 2, :]
                    rlk = ('lnt', j % 2)
                    kb.op('act', lambda e, rl=rl, ps=ps: e.activation(out=rl, in_=ps[:, :], func=AF.Relu), reads=[pk], writes=[rlk])
                    kb.op('dve', lambda e, rl=rl, j=j: e.tensor_tensor(out=actb[:, j, :], in0=rl, in1=rl, op=ALU.mult), reads=[rlk], writes=[('actb', j)])
                for f in range(KT):
                    v, key = ringU.load(wview(wdn_d, g * 2048, 16, f * 128, 128), 16, 128)
                    ps, pk = newps()
                    for j in range(16):
                        kb.op('pe', lambda e, j=j, v=v: e.matmul(ps[:, :], v[:, j, :], actb[:, j, :], start=(j == 0), stop=(j == 15)), reads=[key, ('actb', j)], writes=[pk], inc=(j == 15))
                    if g == 0:
                        kb.op('act', lambda e, ps=ps: e.activation(out=acc[:, f, :], in_=ps[:, :], func=AF.Identity), reads=[pk], writes=[('acc', f)])
                    else:
                        kb.op('dve', lambda e, ps=ps: e.tensor_tensor(out=acc[:, f, :], in0=acc[:, f, :], in1=ps[:, :], op=ALU.add), reads=[pk, ('acc', f)], writes=[('acc', f)])
            for f in range(KT):
                xa = lnt[:, 2 + f % 2, :]
                xak = ('lnt', 2 + f % 2)
                kb.dma('sp', out=xa, in_=x1_scr[f], reads=[('x1_scr', f)], writes=[xak])
                kb.op('act', lambda e, xa=xa: e.activation(out=xa, in_=xa, func=AF.Identity, scale=float(ALPHA)), reads=[xak], writes=[xak])
                kb.op('dve', lambda e, xa=xa: e.scalar_tensor_tensor(out=acc[:, f, :], in0=acc[:, f, :], scalar=gtf[:, f:f + 1], in1=xa, op0=ALU.mult, op1=ALU.add),
                      reads=[('acc', f), xak, 'modc'], writes=[('acc', f)])
            rstd, rk_, nmr, nk_ = ln_stats(lambda f: (acc[:, f, :], ('acc', f)), 'ln2')
            for f in range(KT):
                kb.op('dve', lambda e: e.tensor_tensor(out=acc[:, f, :], in0=acc[:, f, :], in1=rstd, op=ALU.mult), reads=[('acc', f), rk_], writes=[('acc', f)])
                kb.op('dve', lambda e: e.tensor_tensor(out=acc[:, f, :], in0=acc[:, f, :], in1=nmr, op=ALU.add), reads=[('acc', f), nk_], writes=[('acc', f)])
                kb.op('act', lambda e: e.activation(out=acc[:, f, :], in_=acc[:, f, :], func=AF.Identity, scale=pcol(P_L2G + f), bias=pcol(P_L2B + f)), reads=[('acc', f), 'prm'], writes=[('acc', f)])
            for f4 in range(8):
                for tq in range(4):
                    ps, pk = newps()
                    for q in range(4):
                        f = f4 * 4 + q
                        kb.op('pe', lambda e, q=q, f=f: e.transpose(ps[:, q * 128:(q + 1) * 128], acc[:, f, tq * 128:(tq + 1) * 128], ident), reads=[('acc', f), 'cst'], writes=[pk], inc=(q == 3))
                    yi = (f4 * 4 + tq) % 2
                    kb.op('act', lambda e, yi=yi, ps=ps: e.activation(out=yst[:, yi, :], in_=ps[:, :], func=AF.Identity), reads=[pk], writes=[('yst', yi)])
                    r0 = hf * HALF + tq * 128
                    kb.dma('sp', out=y_d[r0:r0 + 128, f4 * 512:(f4 + 1) * 512], in_=yst[:, yi, :], reads=[('yst', yi)], writes=['y_d'])
    kb.barrier(['sp'])
    print("program: insts", kb.ninst, "waits", kb.nwait)
    return nc


def _consts():
    c = np.zeros((128, NCONST), np.float32)
    c[:, C_ID:C_ID + 128] = np.eye(128, dtype=np.float32)
    c[:, C_ONE:C_ONE + 128] = 1.0
    i = np.arange(64)
    src = i[:, None]; dst = i[None, :]
    c[0:64, C_TRIF:C_TRIF + 64] = (src <= dst)
    c[0:64, C_TRIB:C_TRIB + 64] = (src >= dst)
    c[0:64, C_NMF:C_NMF + 64] = np.where(src >= dst, 0.0, NEG)
    c[0:64, C_NMT:C_NMT + 64] = np.where(dst >= src, 0.0, NEG)
    c[0:64, C_SMF:C_SMF + 64] = (src > dst)
    c[0:64, C_SMT:C_SMT + 64] = (dst > src)
    c[:, C_JIDX:C_JIDX + 8] = np.arange(8)[None, :] * 128 + np.arange(128)[:, None]
    c[:, C_NIDX:C_NIDX + 64] = np.arange(64)[None, :]
    return c


def _colT(v, nt):
    return np.ascontiguousarray(np.asarray(v).reshape(nt, 128).T)


def _params(cvec, chain, posflag, h0, I):
    p = np.zeros((128, NPRM), np.float32)
    p[:, P_CT:P_CT + 32] = _colT(cvec, 32)
    p[:, P_BMOD:P_BMOD + 192] = _colT(I['b_mod'][0], 192)
    p[:, P_LCW:P_LCW + 64] = I['lru_conv_w'][0].reshape(4, 16, 128).transpose(2, 1, 0).reshape(128, 64)
    p[:, P_LCB:P_LCB + 16] = _colT(I['lru_conv_b'][0], 16)
    p[:, P_LGB:P_LGB + 64] = I['lru_gate_b'][0].reshape(2, 2, 16, 128).transpose(3, 2, 0, 1).reshape(128, 64)
    p[:, P_LAM:P_LAM + 32] = I['lru_lambda'][0].reshape(2, 16, 128).transpose(2, 1, 0).reshape(128, 32)
    p[:, P_DCW:P_DCW + 192] = I['dn_conv_w'][0].reshape(4, 3, 16, 128).transpose(3, 1, 2, 0).reshape(128, 192)
    p[:, P_ALOG:P_ALOG + 32] = I['dn_a_log'][0].reshape(1, 32)
    p[:, P_DTB:P_DTB + 32] = I['dn_dt_bias'][0].reshape(1, 32)
    p[:, P_NW] = I['dn_norm_w'][0]
    p[:, P_BBR:P_BBR + 64] = I['b_branch'][0].reshape(2, 32, 128).transpose(2, 0, 1).reshape(128, 64)
    p[:, P_L1G:P_L1G + 32] = _colT(I['ln1_g'][0], 32)
    p[:, P_L1B:P_L1B + 32] = _colT(I['ln1_b'][0], 32)
    p[:, P_L2G:P_L2G + 32] = _colT(I['ln2_g'][0], 32)
    p[:, P_L2B:P_L2B + 32] = _colT(I['ln2_b'][0], 32)
    p[:, P_FLAG] = chain
    p[:, P_FLAG + 1] = posflag
    p[:, P_H0:P_H0 + 32] = h0.reshape(2, 16, 128).transpose(2, 1, 0).reshape(128, 32)
    return p


_NC_CACHE = {}


def kernel(**I):
    I = {k: np.asarray(v) for k, v in I.items()}
    ncores = 8
    dbg = DEBUG.get('dbg', None)
    key = (repr(sorted((dbg or {}).items(), key=lambda kv: kv[0])), DEBUG.get('stop'))
    if key not in _NC_CACHE:
        d2 = dict(dbg or {})
        if 'modc_in' in DEBUG:
            d2['__modc_in'] = 1
        _NC_CACHE[key] = build_program(d2, DEBUG.get('stop'))
    nc = _NC_CACHE[key]
    cst = _consts()
    shared = {
        'cst': cst,
        'w_mod': np.ascontiguousarray(I['w_mod'][0]), 'w_in': np.ascontiguousarray(I['w_in'][0]),
        'lru_gw': np.ascontiguousarray(I['lru_gate_w'][0].transpose(2, 3, 0, 1, 4).reshape(16, 128, 4, 128)),
        'w_lp': np.ascontiguousarray(I['w_lru_proj'][0]), 'w_dp': np.ascontiguousarray(I['w_dn_proj'][0]),
        'w_o': np.ascontiguousarray(I['w_o'][0]), 'w_up': np.ascontiguousarray(I['w_up'][0]), 'w_down': np.ascontiguousarray(I['w_down'][0]),
    }
    zeros_x = np.zeros((T, D), np.float32)
    zeros_s0 = np.zeros((2, NH, 128, 128), np.float32)
    in_maps = []
    ncores_used = DEBUG.get('ncores', ncores)
    for core in range(ncores_used):
        m = dict(shared)
        if core < 2:
            b = core
            m['x'] = np.ascontiguousarray(I['x_sample'][b])
            m['prm'] = _params(I['c'][b], 1.0, 1.0, I['state_lru'][b, 0], I)
            m['dn_s0'] = np.ascontiguousarray(I['state_dn'][b, 0])
        elif core < 6:
            s = (core - 2) * 4
            m['x'] = np.ascontiguousarray(I['x_prompt'][s:s + 4].reshape(T, D))
            m['prm'] = _params(I['c_ctx'], 0.0, 0.0, np.zeros((2, 2048), np.float32), I)
            m['dn_s0'] = zeros_s0
        else:
            m['x'] = zeros_x
            m['prm'] = _params(I['c_ctx'], 0.0, 0.0, np.zeros((2, 2048), np.float32), I)
            m['dn_s0'] = zeros_s0
        in_maps.append(m)
    declared = set()
    for alloc in nc.allocations:
        try:
            if alloc.kind == "ExternalInput":
                declared.add(alloc.memorylocations[0].name)
        except Exception:
            pass
    if 'modc_in' in declared:
        for m in in_maps:
            m['modc_in'] = DEBUG['modc_in']
    in_maps = [{k: v for k, v in m.items() if k in declared} for m in in_maps]
    res = run_bass_kernel_spmd(nc, in_maps, core_ids=list(range(ncores_used)))
    R = res.results
    DEBUG['last'] = R
    if DEBUG.get('stop'):
        return None
    B = I['x_prompt'].shape[0]
    y_prompt = np.zeros((B, SEG, D), np.float32)
    y_sample = np.zeros((2, T, D), np.float32)
    st_lru = np.zeros((B, 1, 2, 2048), np.float32)
    st_dn = np.zeros((B, 1, 2, NH, 128, 128), np.float32)
    for core in range(min(ncores_used, 6)):
        r = R[core]
        if core < 2:
            y_sample[core] = r['y']
        else:
            s = (core - 2) * 4
            y_prompt[s:s + 4] = r['y'].reshape(4, SEG, D)
            sl = r['st_lru'].reshape(128, NH, NSEG, 2)
            st_lru[s:s + 4, 0] = sl.transpose(2, 3, 1, 0).reshape(4, 2, 2048)
            st_dn[s:s + 4, 0] = r['st_dn']
    return (y_prompt, y_sample, st_lru, st_dn)
```

```python
import numpy as np
import concourse.bass as bass
import concourse.mybir as mybir
from concourse.bass_utils import run_bass_kernel_spmd

F32 = mybir.dt.float32
BF16 = mybir.dt.bfloat16
AF = mybir.ActivationFunctionType
ALU = mybir.AluOpType

D = 4096; T = 1024; NSEG = 4; SEG = 256; NH = 16; KT = 32; C = 64; NCH = 16; DFF = 16384
HALF = 512
N_IN = 20544
ALPHA = 2.0 ** 0.25
LN_EPS = 1e-5
RMS_EPS = 1e-6
NEG = -30000.0

C_ID = 0; C_ONE = 128; C_TRIF = 256; C_TRIB = 320; C_NMF = 384; C_NMT = 448; C_SMF = 512; C_SMT = 576
C_JIDX = 640; C_NIDX = 648; NCONST = 712
P_CT = 0; P_BMOD = 32; P_LCW = 224; P_LCB = 288; P_LGB = 304; P_LAM = 368; P_DCW = 400; P_ALOG = 592
P_DTB = 624; P_NW = 656; P_BBR = 657; P_L1G = 721; P_L1B = 753; P_L2G = 785; P_L2B = 817; P_FLAG = 849
P_H0 = 851; NPRM = 883

DEBUG = {}


class KB:
    def __init__(self, nc):
        self.nc = nc
        self.E = {'pe': nc.tensor, 'act': nc.scalar, 'dve': nc.vector, 'pool': nc.gpsimd, 'sp': nc.sync}
        self.sem = {}
        self.cnt = {}
        for e in self.E:
            self.sem[('c', e)] = nc.alloc_semaphore(name=f"c_{e}")
            self.cnt[e] = 0
        self.NDS = 8
        self.dcnt = {'pool': 0, 'sp': 0}
        for q in self.dcnt:
            for i in range(self.NDS):
                self.sem[('d', q, i)] = nc.alloc_semaphore(name=f"d_{q}{i}")
        self.known = {e: {} for e in self.E}
        self.lastw = {}
        self.readers = {}
        self.nwait = 0
        self.ninst = 0

    def need(self, e, sk, val):
        if self.known[e].get(sk, 0) < val:
            self.E[e].wait_ge(self.sem[sk], val)
            self.known[e][sk] = val
            self.nwait += 1

    def _deps(self, e, reads, writes):
        own = ('c', e)
        for k in reads:
            lw = self.lastw.get(k)
            if lw is not None:
                if lw[0] == own and e == 'pe':
                    continue
                self.need(e, lw[0], lw[1])
        for k in writes:
            lw = self.lastw.get(k)
            if lw is not None and lw[0] != own:
                self.need(e, lw[0], lw[1])
            for rk, rv in self.readers.get(k, {}).items():
                if rk != own:
                    self.need(e, rk, rv)

    def _mark(self, sk, val, reads, writes):
        for k in writes:
            self.lastw[k] = (sk, val)
            self.readers[k] = {}
        for k in reads:
            d = self.readers.setdefault(k, {})
            if d.get(sk, 0) < val:
                d[sk] = val

    def op(self, e, fn, reads=(), writes=(), inc=True):
        self._deps(e, reads, writes)
        inst = fn(self.E[e])
        self.ninst += 1
        if inc:
            self.cnt[e] += 1
            inst.then_inc(self.sem[('c', e)], 1)
            val = self.cnt[e]
        else:
            val = self.cnt[e] + 1
        self._mark(('c', e), val, reads, writes)
        return inst

    def dma(self, q, out, in_, reads=(), writes=()):
        i = self.dcnt[q]
        slot = i % self.NDS
        rnd = i // self.NDS
        sk = ('d', q, slot)
        if rnd > 0:
            self.need(q, sk, 16 * rnd)
        self._deps(q, reads, writes)
        inst = self.E[q].dma_start(out=out, in_=in_)
        inst.then_inc(self.sem[sk], 16)
        self.ninst += 1
        self.dcnt[q] += 1
        self._mark(sk, 16 * (rnd + 1), reads, writes)
        return inst

    def barrier(self, engines=None):
        cur = {}
        for e in self.E:
            if self.cnt[e] > 0:
                cur[('c', e)] = self.cnt[e]
        for q, n in self.dcnt.items():
            for s in range(self.NDS):
                k = (n - 1 - s) // self.NDS + 1 if n > s else 0
                if k > 0:
                    cur[('d', q, s)] = 16 * k
        for e in (engines or self.E):
            for sk, v in cur.items():
                if sk == ('c', e):
                    continue
                self.need(e, sk, v)


def build_program(dbg=None, stop=None):
    dbg = dbg or {}
    nc = bass.Bass("TRN2", target_bir_lowering=False)
    kb = KB(nc)

    def din(name, shape, dt=F32):
        return nc.dram_tensor(name, list(shape), dt, kind="ExternalInput").ap()

    def dout(name, shape, dt=F32):
        return nc.dram_tensor(name, list(shape), dt, kind="ExternalOutput").ap()

    def dscr(name, shape, dt):
        return nc.dram_tensor(name, list(shape), dt).ap()

    NEED = {'0': {'w_mod'}, 'A': {'w_mod'}, 'S': {'w_mod', 'w_in'}, 'S1': {'w_mod', 'w_in'}, 'S2': {'w_mod', 'w_in'}, 'L0': {'w_mod', 'w_in', 'lru_gw'}, 'D1': {'w_mod', 'w_in', 'lru_gw'}, 'D2': {'w_mod', 'w_in', 'lru_gw'}, 'D3': {'w_mod', 'w_in', 'lru_gw'}, 'M1': {'w_mod', 'w_in', 'lru_gw'},
            'M': {'w_mod', 'w_in', 'lru_gw'}, 'G': {'w_mod', 'w_in', 'lru_gw', 'w_lp', 'w_dp'}}
    need = NEED.get(stop)
    modc_in = dbg.pop('__modc_in', None) is not None
    if need is not None and modc_in:
        need = need - {'w_mod'}
    _din = din

    def din(name, shape, dt=F32):
        if need is not None and name.startswith('w_') or (need is not None and name == 'lru_gw'):
            if name not in need:
                return None
        return _din(name, shape, dt)

    x_d = din("x", [T, D])
    prm_d = din("prm", [128, NPRM])
    cst_d = din("cst", [128, NCONST])
    s0_d = din("dn_s0", [2, NH, 128, 128])
    wmod_d = din("w_mod", [D, 6 * D])
    win_d = din("w_in", [D, N_IN])
    lgw_d = din("lru_gw", [NH, 128, 4, 128])
    wlp_d = din("w_lp", [2048, D])
    wdp_d = din("w_dp", [2048, D])
    wo_d = din("w_o", [D, D])
    wup_d = din("w_up", [D, DFF])
    wdn_d = din("w_down", [DFF, D])
    y_d = dout("y", [T, D])
    stl_d = dout("st_lru", [128, NH * NSEG * 2])
    std_d = dout("st_dn", [NSEG, 2, NH, 128, 128])
    xT_scr = dscr("xT_scr", [KT, 128, T], F32)
    lru_scr = dscr("lru_scr", [NH, 128, T], BF16)
    dn_scr = dscr("dn_scr", [NH, 128, T], BF16)
    mg_scr = dscr("mg_scr", [2, KT, 128, HALF], BF16)
    r1_scr = dscr("r1_scr", [KT, 128, HALF], F32)
    x1_scr = dscr("x1_scr", [KT, 128, HALF], F32)
    dbg_d = {}
    for name, shape in dbg.items():
        if isinstance(shape, tuple):
            dbg_d[name] = dout("dbg_" + name, shape[0], shape[1])
        else:
            dbg_d[name] = dout("dbg_" + name, shape)
    modc_d = _din("modc_in", [128, 192]) if modc_in else None

    def sb(name, shape, dt=F32):
        return nc.alloc_sbuf_tensor(name + "_sb", list(shape), dt)

    cst = sb("cst", [128, NCONST])
    prm = sb("prm", [128, NPRM])
    identb = sb("identb", [128, 128], BF16)
    modc = sb("modc", [128, 192])
    sc1m = sb("sc1m", [128, 32])
    sc1f = sb("sc1f", [128, 32])
    cdl = sb("cdl", [128, 32])
    cdl2 = sb("cdl2", [128, 32])
    stl = sb("stl", [128, NH * NSEG * 2])
    wring = sb("wring", [128, 3 * 32 * 128], BF16)
    psum = [nc.alloc_psum_tensor(f"ps{i}", [128, 512], F32) for i in range(8)]
    ps_state = {'i': 0}

    def newps():
        i = ps_state['i']
        ps_state['i'] = (i + 1) % 6
        return psum[i], ('ps', i)

    def resps(i):
        return psum[i], ('ps', i)

    ident = cst[:, C_ID:C_ID + 128]
    ones = cst[:, C_ONE:C_ONE + 128]

    def pcol(off, n=1):
        return prm[:, off:off + n]

    class Ring:
        def __init__(self, nslot, kt, cols, extra=None, base=None, prefix='wr'):
            self.kt = kt; self.cols = cols; self.i = 0; self.prefix = prefix
            per = kt * cols
            base = wring if base is None else base
            assert nslot * per <= base.shape[1]
            self.views = [base[:, s * per:(s + 1) * per].rearrange("p (k c) -> p k c", k=kt) for s in range(nslot)]
            if extra is not None:
                ne = extra.shape[1] // per
                self.views += [extra[:, s * per:(s + 1) * per].rearrange("p (k c) -> p k c", k=kt) for s in range(ne)]
            self.nslot = len(self.views)

        def load(self, src_ap, nk=None, ncol=None):
            s = self.i % self.nslot
            self.i += 1
            v = self.views[s]
            if nk is not None or ncol is not None:
                v = v[:, 0:(nk or self.kt), 0:(ncol or self.cols)]
            key = (self.prefix, s)
            kb.dma('pool', out=v, in_=src_ap, writes=[key])
            return v, key

    def wview(w_d, r0, nk, c0, ncol):
        return w_d[r0:r0 + nk * 128, c0:c0 + ncol].rearrange("(k p) n -> p k n", p=128)

    def dump(name, ap, key):
        if name in dbg_d:
            kb.dma('sp', out=dbg_d[name], in_=ap, reads=[key] if not isinstance(key, list) else key)

    def finish():
        kb.barrier(['sp'])
        print("program: insts", kb.ninst, "waits", kb.nwait)
        return nc

    kb.dma('sp', out=cst[:], in_=cst_d, writes=['cst'])
    kb.dma('sp', out=prm[:], in_=prm_d, writes=['prm'])
    kb.op('dve', lambda e: e.tensor_copy(out=identb[:], in_=ident), reads=['cst'], writes=['identb'])
    kb.op('dve', lambda e: e.memset(stl[:], 0.0), writes=['stl'])

    if modc_in:
        kb.dma('sp', out=modc[:], in_=modc_d, writes=['modc'])
    cs = sb("cs", [128, 32], BF16)
    rowb = sb("rowb", [1, 2 * 128], F32)
    kb.op('act', lambda e: e.activation(out=cs[:], in_=pcol(P_CT, 32), func=AF.Silu), reads=['prm'], writes=['cs'])
    mb_state = {'i': 0}

    def mod_block(ring, col0, ncols, rowb=rowb):
        i = mb_state['i']
        mb_state['i'] += 1
        wv, wk = ring.load(wview(wmod_d, 0, 32, col0, ncols), 32, ncols)
        ps, pk = newps()
        for kt in range(KT):
            kb.op('pe', lambda e, kt=kt: e.matmul(ps[0:1, 0:ncols], cs[:, kt:kt + 1], wv[:, kt, :], start=(kt == 0), stop=(kt == KT - 1)),
                  reads=['cs', wk], writes=[pk], inc=(kt == KT - 1))
        rb = rowb[0:1, (i % 2) * ncols:(i % 2) * ncols + ncols]
        rk = ('rowb', i % 2)
        kb.op('act', lambda e: e.activation(out=rb, in_=ps[0:1, 0:ncols], func=AF.Identity), reads=[pk], writes=[rk])
        ps2, pk2 = newps()
        nj = ncols // 128
        for j in range(nj):
            kb.op('pe', lambda e, j=j: e.matmul(ps2[:, j:j + 1], rb[0:1, j * 128:(j + 1) * 128], ones[0:1, 0:1], start=True, stop=True),
                  reads=[rk, 'cst'], writes=[pk2], inc=(j == nj - 1))
        c0 = col0 // 128
        kb.op('dve', lambda e: e.tensor_tensor(out=modc[:, c0:c0 + nj], in0=ps2[:, 0:nj], in1=pcol(P_BMOD + c0, nj), op=ALU.add),
              reads=[pk2, 'prm'], writes=['modc'])

    if not modc_in:
        with nc.sbuf_tensor("wr0", [128, 3 * 32 * 256], BF16) as wr0, nc.sbuf_tensor("rowb0", [1, 2 * 256], F32) as rowb0:
            ring0 = Ring(3, 32, 256, base=wr0, prefix='wr0')
            for blk in range(32):
                mod_block(ring0, blk * 256, 256, rowb=rowb0)
            kb.barrier()
    kb.op('dve', lambda e: e.tensor_scalar_add(out=sc1m[:], in0=modc[:, 32:64], scalar1=1.0), reads=['modc'], writes=['sc1m'])
    shm = modc[:, 0:32]; gtm = modc[:, 64:96]; shf = modc[:, 96:128]; gtf = modc[:, 160:192]
    with nc.sbuf_tensor("etmp", [128, 32], F32) as etmp, nc.sbuf_tensor("etmp2", [128, 32], F32) as etmp2:
        kb.op('act', lambda e: e.activation(out=etmp[:], in_=pcol(P_LAM, 32), func=AF.Exp, scale=-1.0), reads=['prm'], writes=['etmp'])
        kb.op('dve', lambda e: e.tensor_scalar(out=etmp2[:], in0=etmp[:], scalar1=1.0 / 3.0, scalar2=-0.5, op0=ALU.mult, op1=ALU.add), reads=['etmp'], writes=['etmp2'])
        kb.op('dve', lambda e: e.tensor_tensor(out=etmp2[:], in0=etmp2[:], in1=etmp[:], op=ALU.mult), reads=['etmp', 'etmp2'], writes=['etmp2'])
        kb.op('dve', lambda e: e.tensor_scalar_add(out=etmp2[:], in0=etmp2[:], scalar1=1.0), reads=['etmp2'], writes=['etmp2'])
        kb.op('dve', lambda e: e.tensor_tensor(out=etmp2[:], in0=etmp2[:], in1=etmp[:], op=ALU.mult), reads=['etmp', 'etmp2'], writes=['etmp2'])
        kb.op('dve', lambda e: e.tensor_scalar_mul(out=cdl[:], in0=etmp2[:], scalar1=-8.0), reads=['etmp2'], writes=['cdl'])
        kb.op('dve', lambda e: e.tensor_scalar_mul(out=cdl2[:], in0=etmp2[:], scalar1=-16.0), reads=['etmp2'], writes=['cdl'])
    dump('modc', modc[:], 'modc')
    kb.barrier()
    if stop == '0':
        return finish()
    ringW = Ring(3, 32, 128)

    hT_guard = nc.sbuf_tensor("hT", [128, KT, T], BF16)
    hT = hT_guard.__enter__()

    with nc.sbuf_tensor("xtok", [128, 4, D], F32) as xtok, nc.sbuf_tensor("xr", [128, 4, HALF], F32) as xr, \
            nc.sbuf_tensor("tabS", [128, 8, 64], F32) as tabS, nc.sbuf_tensor("tabC", [128, 8, 64], F32) as tabC, \
            nc.sbuf_tensor("om", [128, 8], F32) as om, nc.sbuf_tensor("targ", [128, 8, 64], F32) as targ, \
            nc.sbuf_tensor("tk", [128, 8, 64], F32) as tk:
        kb.op('act', lambda e: e.activation(out=om[:], in_=cst[:, C_JIDX:C_JIDX + 8], func=AF.Exp, scale=-float(np.log(10000.0)) / 1024.0),
              reads=['cst'], writes=['om'])
        for tab, off in ((tabS, 0.0), (tabC, 0.25)):
            kb.op('dve', lambda e: e.tensor_tensor(out=targ[:], in0=om[:].unsqueeze(2).to_broadcast([128, 8, 64]),
                                                   in1=cst[:, C_NIDX:C_NIDX + 64].unsqueeze(1).to_broadcast([128, 8, 64]), op=ALU.mult),
                  reads=['om', 'cst'], writes=['targ'])
            kb.op('dve', lambda e, off=off: e.tensor_scalar(out=targ[:], in0=targ[:], scalar1=1.0 / (2 * np.pi), scalar2=off, op0=ALU.mult, op1=ALU.add),
                  reads=['targ'], writes=['targ'])
            kb.op('dve', lambda e: e.memset(tk[:], 0.0), writes=['tk'])
            for m in range(1, 12):
                kb.op('dve', lambda e, m=m: e.scalar_tensor_tensor(out=tk[:], in0=targ[:], scalar=float(m), in1=tk[:], op0=ALU.is_ge, op1=ALU.add),
                      reads=['targ', 'tk'], writes=['tk'])
            kb.op('dve', lambda e: e.tensor_tensor(out=targ[:], in0=targ[:], in1=tk[:], op=ALU.subtract), reads=['targ', 'tk'], writes=['targ'])
            kb.op('dve', lambda e: e.tensor_scalar(out=tk[:], in0=targ[:], scalar1=0.5, scalar2=None, op0=ALU.is_gt), reads=['targ'], writes=['tk'])
            kb.op('dve', lambda e: e.tensor_tensor(out=targ[:], in0=targ[:], in1=tk[:], op=ALU.subtract), reads=['targ', 'tk'], writes=['targ'])
            kb.op('act', lambda e, tab=tab: e.activation(out=tab[:], in_=targ[:], func=AF.Sin, scale=float(2 * np.pi)), reads=['targ'], writes=['tab'])
        posflag = pcol(P_FLAG + 1)
        for hf in range(2):
            for j in range(4):
                tt = hf * 4 + j
                kb.dma('sp', out=xtok[:, j, :], in_=x_d[tt * 128:(tt + 1) * 128, :], writes=[('xtok', j)])
            for ft in range(KT):
                ps, pk = newps()
                for j in range(4):
                    kb.op('pe', lambda e, j=j: e.transpose(ps[:, j * 128:(j + 1) * 128], xtok[:, j, ft * 128:(ft + 1) * 128], ident),
                          reads=[('xtok', j), 'cst'], writes=[pk], inc=(j == 3))
                qd, jt = ft // 8, ft % 8
                tab = tabS if qd in (0, 2) else tabC
                if qd < 2:
                    pos_ap = tab[:, jt, 8 * hf:8 * hf + 8].unsqueeze(2).to_broadcast([128, 8, 64])
                else:
                    pos_ap = tab[:, jt, :].unsqueeze(1).to_broadcast([128, 8, 64])
                xrv = xr[:, ft % 4, :]
                xk = ('xr', ft % 4)
                kb.op('dve', lambda e: e.scalar_tensor_tensor(out=xrv.rearrange("p (a b) -> p a b", a=8), in0=pos_ap, scalar=posflag,
                                                              in1=ps[:, :].rearrange("p (a b) -> p a b", a=8), op0=ALU.mult, op1=ALU.add),
                      reads=[pk, 'tab', 'prm'], writes=[xk])
                kb.op('act', lambda e: e.activation(out=hT[:, ft, hf * HALF:(hf + 1) * HALF], in_=xrv, func=AF.Identity,
                                                    scale=sc1m[:, ft:ft + 1], bias=shm[:, ft:ft + 1]),
                      reads=[xk, 'sc1m', 'modc'], writes=['hT'])
                kb.dma('sp', out=xT_scr[ft, :, hf * HALF:(hf + 1) * HALF], in_=xrv, reads=[xk], writes=['xT_scr'])
    dump('hT', hT[:, 0:4, :], 'hT')
    if stop == 'A':
        return finish()

    pm = nc.sbuf_tensor("pmF", [128, 14 * T], F32)
    pmF = pm.__enter__()
    pmb = nc.sbuf_tensor("pmB", [128, 16 * T], BF16)
    pmB = pmb.__enter__()
    xpg = nc.sbuf_tensor("xpad", [128, NSEG, SEG + 4], F32)
    xpad = xpg.__enter__()
    smallT = pmF[0:64, 13 * T:14 * T].rearrange("p (c k) -> p c k", c=NCH)
    tmg = nc.sbuf_tensor("tokm", [64, 3, NCH, 32], F32)
    tokm = tmg.__enter__()
    ntg = nc.sbuf_tensor("negA", [64, 32], F32)
    negA = ntg.__enter__()
    s3g = nc.sbuf_tensor("S32", [128, 2, 128], F32)
    S32 = s3g.__enter__()
    sbg = nc.sbuf_tensor("Sbf", [128, 2, 128], BF16)
    Sbf = sbg.__enter__()
    sog = nc.sbuf_tensor("Sout", [128, 8, 128], F32)
    Sout = sog.__enter__()
    gwg = nc.sbuf_tensor("gw", [128, 2, 4, 128], BF16)
    gwt = gwg.__enter__()
    hig = nc.sbuf_tensor("hinit", [128, 2], F32)
    hinit = hig.__enter__()
    egg = nc.sbuf_tensor("egl", [128, 2, NCH], F32)
    egl = egg.__enter__()
    rrg = nc.sbuf_tensor("RR", [64, 4, 128], BF16)
    RR = rrg.__enter__()
    vng = nc.sbuf_tensor("VN", [64, 4, 128], BF16)
    VN = vng.__enter__()

    def FB(i):
        return pmF[:, i * T:(i + 1) * T], ('F', i)

    def BB(i):
        return pmB[:, i * T:(i + 1) * T], ('B', i)

    chain = pcol(P_FLAG)
    kb.op('dve', lambda e: e.memset(xpad[:], 0.0), writes=['xpad'])

    wsm, wsk = ringW.load(wview(win_d, 0, 32, 12288, 64), 32, 64)
    for half in range(2):
        ps, pk = newps()
        for cc in range(8):
            c = half * 8 + cc
            for kt in range(KT):
                kb.op('pe', lambda e, kt=kt, c=c, cc=cc: e.matmul(ps[0:64, cc * 64:(cc + 1) * 64], hT[:, kt, c * 64:(c + 1) * 64], wsm[:, kt, :],
                                                                  start=(kt == 0), stop=(kt == KT - 1)),
                      reads=['hT', wsk], writes=[pk], inc=(kt == KT - 1 and cc == 7))
        kb.op('act', lambda e: e.activation(out=smallT[:, half * 8:(half + 1) * 8, :], in_=ps[0:64, :].rearrange("p (a b) -> p a b", a=8), func=AF.Identity),
              reads=[pk], writes=[('F', 13)])
    if stop == 'S1':
        dump('tokm', smallT[:, :, 0:32].rearrange("p (a c) k -> p a c k", a=4), [('F', 13)]) if False else None
        return finish()
    TB, TGC, TKD = range(3)
    TG = TKD
    tkt = pmF[0:64, 12 * T:12 * T + NCH * 32].rearrange("p (c k) -> p c k", c=NCH)
    kb.op('act', lambda e: e.activation(out=tokm[:, TB], in_=smallT[:, :, 0:32], func=AF.Sigmoid), reads=[('F', 13)], writes=[('tokm', TB)])
    kb.op('act', lambda e: e.activation(out=negA[:], in_=prm[0:64, P_ALOG:P_ALOG + 32], func=AF.Exp), reads=['prm'], writes=['negA'])
    kb.op('dve', lambda e: e.tensor_tensor(out=tkt, in0=smallT[:, :, 32:64], in1=prm[0:64, P_DTB:P_DTB + 32].unsqueeze(1).to_broadcast([64, NCH, 32]), op=ALU.add),
          reads=[('F', 13), 'prm'], writes=[('F', 12)])
    kb.op('act', lambda e: e.activation(out=tkt, in_=tkt, func=AF.Exp), reads=[('F', 12)], writes=[('F', 12)])
    kb.op('act', lambda e: e.activation(out=tkt, in_=tkt, func=AF.Ln, bias=1.0), reads=[('F', 12)], writes=[('F', 12)])
    kb.op('dve', lambda e: e.scalar_tensor_tensor(out=tokm[:, TG], in0=tkt, scalar=-1.0, in1=negA[:].unsqueeze(1).to_broadcast([64, NCH, 32]), op0=ALU.mult, op1=ALU.mult),
          reads=[('F', 12), 'negA'], writes=[('tokm', TG)])
    if stop == 'S2':
        dump('tokm', tokm[:].rearrange("p a b c -> p (a b c)"), [('tokm', i) for i in range(3)])
        return finish()
    ps, pk = newps()
    ps2, pk2 = newps()
    for c in range(NCH):
        for d in range(2):
            tri = cst[0:64, (C_TRIF if d == 0 else C_TRIB):(C_TRIF if d == 0 else C_TRIB) + 64]
            last = (c == NCH - 1 and d == 1)
            kb.op('pe', lambda e, c=c, d=d, tri=tri: e.matmul(ps[0:64, c * 32 + d * 16:c * 32 + d * 16 + 16], tri, tokm[:, TG, c, d * 16:(d + 1) * 16], start=True, stop=True),
                  reads=['cst', ('tokm', TG)], writes=[pk], inc=False)
            kb.op('pe', lambda e, c=c, d=d: e.matmul(ps2[0:64, c * 32 + d * 16:c * 32 + d * 16 + 16], ones[0:64, 0:64], tokm[:, TG, c, d * 16:(d + 1) * 16], start=True, stop=True),
                  reads=['cst', ('tokm', TG)], writes=[pk2], inc=last)
    kb.op('dve', lambda e: e.tensor_copy(out=tokm[:, TGC], in_=ps[0:64, :].rearrange("p (a b) -> p a b", a=NCH)), reads=[pk], writes=[('tokm', TGC)])
    kb.op('dve', lambda e: e.tensor_copy(out=tkt, in_=ps2[0:64, :].rearrange("p (a b) -> p a b", a=NCH)), reads=[pk2], writes=[('F', 12)])
    kb.op('dve', lambda e: e.tensor_tensor(out=tokm[:, TKD], in0=tkt, in1=tokm[:, TGC], op=ALU.subtract), reads=[('F', 12), ('tokm', TGC)], writes=[('tokm', TKD)])
    kb.op('act', lambda e: e.activation(out=tokm[:, TKD], in_=tokm[:, TKD], func=AF.Exp), reads=[('tokm', TKD)], writes=[('tokm', TKD)])
    dump('tokm', tokm[:].rearrange("p a b c -> p (a b c)"), [('tokm', i) for i in range(3)])
    if stop == 'S':
        return finish()

    ringM = ringW

    def proj_fm(col0):
        wv, wk = ringM.load(wview(win_d, 0, 32, col0, 128))
        pa, ka = newps()
        pb, kbk = newps()
        for kt in range(KT):
            kb.op('pe', lambda e, kt=kt: e.matmul(pa[:, :], wv[:, kt, :], hT[:, kt, 0:HALF], start=(kt == 0), stop=(kt == KT - 1)),
                  reads=['hT', wk], writes=[ka], inc=False)
            kb.op('pe', lambda e, kt=kt: e.matmul(pb[:, :], wv[:, kt, :], hT[:, kt, HALF:T], start=(kt == 0), stop=(kt == KT - 1)),
                  reads=['hT', wk], writes=[kbk], inc=(kt == KT - 1))
        return (pa, ka), (pb, kbk)

    def conv_from_psum(pp, cw_off, bias_ap, out_ap, out_key):
        for h2, (p_, k_) in enumerate(pp):
            kb.op('act', lambda e, h2=h2, p_=p_: e.activation(out=xpad[:, 2 * h2:2 * h2 + 2, 2:2 + SEG], in_=p_[:, :].rearrange("p (a b) -> p a b", a=2), func=AF.Identity),
                  reads=[k_], writes=['xpad'])
        kb.op('dve', lambda e: e.tensor_scalar(out=xpad[:, 1:4, 0:2], in0=xpad[:, 0:3, SEG:SEG + 2], scalar1=chain, scalar2=None, op0=ALU.mult),
              reads=['xpad', 'prm'], writes=['xpad'])
        kb.op('dve', lambda e: e.tensor_scalar(out=xpad[:, 0:3, SEG + 2:SEG + 3], in0=xpad[:, 1:4, 2:3], scalar1=chain, scalar2=None, op0=ALU.mult),
              reads=['xpad', 'prm'], writes=['xpad'])
        o3 = out_ap.rearrange("p (a b) -> p a b", a=NSEG)
        if bias_ap is None:
            kb.op('dve', lambda e: e.tensor_scalar(out=o3, in0=xpad[:, :, 0:SEG], scalar1=pcol(cw_off), scalar2=None, op0=ALU.mult),
                  reads=['xpad', 'prm'], writes=[out_key])
        else:
            kb.op('dve', lambda e: e.tensor_scalar(out=o3, in0=xpad[:, :, 0:SEG], scalar1=pcol(cw_off), scalar2=bias_ap, op0=ALU.mult, op1=ALU.add),
                  reads=['xpad', 'prm'], writes=[out_key])
        for j in range(1, 4):
            kb.op('dve', lambda e, j=j: e.scalar_tensor_tensor(out=o3, in0=xpad[:, :, j:j + SEG], scalar=pcol(cw_off + j), in1=o3, op0=ALU.mult, op1=ALU.add),
                  reads=['xpad', 'prm', out_key], writes=[out_key])

    def l2norm_to_bf(src, sk, tmp, tk_, rn, rk, dst, dk_, scale):
        kb.op('act', lambda e: e.activation(out=tmp, in_=src, func=AF.Square), reads=[sk], writes=[tk_])
        for h2 in range(2):
            ps, pk = newps()
            kb.op('pe', lambda e, h2=h2: e.matmul(ps[:, :], ones, tmp[:, h2 * HALF:(h2 + 1) * HALF], start=True, stop=True), reads=['cst', tk_], writes=[pk])
            kb.op('act', lambda e, h2=h2: e.activation(out=rn[:, h2 * HALF:(h2 + 1) * HALF], in_=ps[:, :], func=AF.Sqrt, bias=RMS_EPS), reads=[pk], writes=[rk])
        kb.op('dve', lambda e: e.reciprocal(out=rn, in_=rn), reads=[rk], writes=[rk])
        kb.op('dve', lambda e: e.scalar_tensor_tensor(out=dst, in0=src, scalar=float(scale), in1=rn, op0=ALU.mult, op1=ALU.mult), reads=[sk, rk], writes=[dk_])

    for n in range(NH):
        if not modc_in:
            for i_ in range(8):
                mod_block(ringW, 8192 + (n * 8 + i_) * 128, 128)
        gws = gwt[:, n % 2]
        gwk = ('gw', n % 2)
        kb.dma('pool', out=gws, in_=lgw_d[n], writes=[gwk])
        xc, xck = FB(0)
        xcb, xcbk = BB(0)
        gy, gyk = BB(1)
        pp = proj_fm(n * 128)
        conv_from_psum(pp, P_LCW + n * 4, pcol(P_LCB + n), xc, xck)
        kb.op('act', lambda e: e.activation(out=xcb, in_=xc, func=AF.Identity), reads=[xck], writes=[xcbk])
        pp = proj_fm(2048 + n * 128)
        for h2, (p_, k_) in enumerate(pp):
            kb.op('act', lambda e, h2=h2, p_=p_: e.activation(out=gy[:, h2 * HALF:(h2 + 1) * HALF], in_=p_[:, :], func=AF.Gelu), reads=[k_], writes=[gyk])
        hbufs = []
        for d in range(2):
            A_, Ak = FB(1 + d * 4)
            S_, Sk = FB(2 + d * 4)
            B_, Bk = FB(3 + d * 4)
            H_, Hk = FB(4 + d * 4)
            for g in range(2):
                for h2 in range(2):
                    ps, pk = newps()
                    kb.op('pe', lambda e, h2=h2, g=g: e.matmul(ps[:, :], gws[:, d * 2 + g, :], xcb[:, h2 * HALF:(h2 + 1) * HALF], start=True, stop=True),
                          reads=[gwk, xcbk], writes=[pk])
                    dst, dk_ = (A_, Ak) if g == 0 else (B_, Bk)
                    kb.op('act', lambda e, h2=h2, dst=dst, g=g: e.activation(out=dst[:, h2 * HALF:(h2 + 1) * HALF], in_=ps[:, :], func=AF.Sigmoid,
                                                                           bias=pcol(P_LGB + n * 4 + d * 2 + g)),
                          reads=[pk, 'prm'], writes=[dk_])
            kb.op('act', lambda e: e.activation(out=S_, in_=A_, func=AF.Exp, scale=cdl2[:, n * 2 + d:n * 2 + d + 1]), reads=[Ak, 'cdl'], writes=[Sk])
            kb.op('act', lambda e: e.activation(out=A_, in_=A_, func=AF.Exp, scale=cdl[:, n * 2 + d:n * 2 + d + 1]), reads=[Ak, 'cdl'], writes=[Ak])
            kb.op('act', lambda e: e.activation(out=S_, in_=S_, func=AF.Sqrt, scale=-1.0, bias=1.0), reads=[Sk], writes=[Sk])
            kb.op('dve', lambda e: e.tensor_tensor(out=B_, in0=B_, in1=xc, op=ALU.mult), reads=[Bk, xck], writes=[Bk])
            kb.op('dve', lambda e: e.tensor_tensor(out=B_, in0=B_, in1=S_, op=ALU.mult), reads=[Bk, Sk], writes=[Bk])
            order = range(NSEG) if d == 0 else range(NSEG - 1, -1, -1)
            for si, s in enumerate(order):
                lo_, hi_ = s * SEG, (s + 1) * SEG
                if si == 0:
                    init = pcol(P_H0 + n * 2 + d)
                    ik = 'prm'
                else:
                    init = hinit[:, d:d + 1]
                    ik = ('hinit', d)
                if d == 0:
                    kb.op('dve', lambda e, lo_=lo_, hi_=hi_, init=init: e.tensor_tensor_scan(out=H_[:, lo_:hi_], data0=A_[:, lo_:hi_], data1=B_[:, lo_:hi_], initial=init, op0=ALU.mult, op1=ALU.add),
                          reads=[Ak, Bk, ik], writes=[Hk])
                    endcol = H_[:, hi_ - 1:hi_]
                else:
                    kb.op('dve', lambda e, lo_=lo_, hi_=hi_, init=init: e.tensor_tensor_scan(out=H_[:, hi_ - 1:lo_ - 1 if lo_ > 0 else None:-1], data0=A_[:, hi_ - 1:lo_ - 1 if lo_ > 0 else None:-1],
                                                                                       data1=B_[:, hi_ - 1:lo_ - 1 if lo_ > 0 else None:-1], initial=init, op0=ALU.mult, op1=ALU.add),
                          reads=[Ak, Bk, ik], writes=[Hk])
                    endcol = H_[:, lo_:lo_ + 1]
                sidx = (n * NSEG + s) * 2 + d
                kb.op('act', lambda e, endcol=endcol, sidx=sidx: e.activation(out=stl[:, sidx:sidx + 1], in_=endcol, func=AF.Identity), reads=[Hk], writes=['stl'])
                if si < NSEG - 1:
                    kb.op('dve', lambda e, endcol=endcol: e.tensor_tensor(out=hinit[:, d:d + 1], in0=endcol, in1=chain, op=ALU.mult), reads=[Hk, 'prm'], writes=[('hinit', d)])
            hbufs.append((H_, Hk))
        lo, lok = BB(2 + (n % 2))
        tmpf, tmpk = FB(1)
        kb.op('dve', lambda e: e.tensor_tensor(out=tmpf, in0=hbufs[0][0], in1=hbufs[1][0], op=ALU.add), reads=[hbufs[0][1], hbufs[1][1]], writes=[tmpk])
        kb.op('dve', lambda e: e.tensor_tensor(out=lo, in0=tmpf, in1=gy, op=ALU.mult), reads=[tmpk, gyk], writes=[lok])
        kb.dma('sp', out=lru_scr[n], in_=lo, reads=[lok], writes=['lru_scr'])
        if n == 0:
            dump('lru0', lo, lok)
            if stop == 'L0':
                return finish()

        qs, qsk = FB(0)
        t1, t1k = FB(1)
        t2, t2k = FB(2)
        qT, qTk = BB(4)
        kT, kTk = BB(5)
        zs, zsk = BB(6)
        pp = proj_fm(4096 + n * 128)
        conv_from_psum(pp, P_DCW + 0 * 64 + n * 4, None, qs, qsk)
        kb.op('act', lambda e: e.activation(out=qs, in_=qs, func=AF.Silu), reads=[qsk], writes=[qsk])
        l2norm_to_bf(qs, qsk, t1, t1k, t2, t2k, qT, qTk, 128.0 ** -0.5)
        pp = proj_fm(6144 + n * 128)
        conv_from_psum(pp, P_DCW + 1 * 64 + n * 4, None, qs, qsk)
        kb.op('act', lambda e: e.activation(out=qs, in_=qs, func=AF.Silu), reads=[qsk], writes=[qsk])
        l2norm_to_bf(qs, qsk, t1, t1k, t2, t2k, kT, kTk, 1.0)
        pp = proj_fm(8192 + n * 128)
        conv_from_psum(pp, P_DCW + 2 * 64 + n * 4, None, qs, qsk)
        kb.op('act', lambda e: e.activation(out=qs, in_=qs, func=AF.Silu), reads=[qsk], writes=[qsk])
        Vc = pmF[0:64, 3 * T:5 * T].rearrange("p (c v) -> p c v", c=NCH)
        Vck = ('F', 3)
        Vck2 = ('F', 4)
        for grp in range(4):
            ps, pk = newps()
            for cc in range(4):
                c = grp * 4 + cc
                kb.op('pe', lambda e, c=c, cc=cc: e.transpose(ps[0:64, cc * 128:(cc + 1) * 128], qs[:, c * 64:(c + 1) * 64], ident), reads=[qsk, 'cst'], writes=[pk], inc=(cc == 3))
            kb.op('act', lambda e, grp=grp: e.activation(out=Vc[:, grp * 4:(grp + 1) * 4, :], in_=ps[0:64, :].rearrange("p (a b) -> p a b", a=4), func=AF.Identity),
                  reads=[pk], writes=[Vck, Vck2])
        ktok = pmB[0:64, 7 * T:9 * T].rearrange("p (c v) -> p c v", c=NCH)
        ktk = [('B', 7), ('B', 8)]
        for grp in range(4):
            ps, pk = newps()
            psv = ps[:, :].bitcast(BF16)
            for cc in range(4):
                c = grp * 4 + cc
                kb.op('pe', lambda e, c=c, cc=cc: e.transpose(psv[0:64, cc * 128:(cc + 1) * 128], kT[:, c * 64:(c + 1) * 64], identb[:]), reads=[kTk, 'identb'], writes=[pk], inc=(cc == 3))
            kb.op('act', lambda e, grp=grp: e.activation(out=ktok[:, grp * 4:(grp + 1) * 4, :], in_=psv[0:64, 0:512].rearrange("p (a b) -> p a b", a=4), func=AF.Identity),
                  reads=[pk], writes=ktk)
        pp = proj_fm(10240 + n * 128)
        for h2, (p_, k_) in enumerate(pp):
            kb.op('act', lambda e, h2=h2, p_=p_: e.activation(out=zs[:, h2 * HALF:(h2 + 1) * HALF], in_=p_[:, :], func=AF.Silu), reads=[k_], writes=[zsk])
        Oacc, Oack = FB(0)
        if stop == 'D1':
            return finish()

        DD = {}
        for d in range(2):
            dh = d * 16 + n
            nmD = cst[0:64, (C_NMF if d == 0 else C_NMT):(C_NMF if d == 0 else C_NMT) + 64]
            nmDT = cst[0:64, (C_NMT if d == 0 else C_NMF):(C_NMT if d == 0 else C_NMF) + 64]
            smY = cst[0:64, (C_SMF if d == 0 else C_SMT):(C_SMF if d == 0 else C_SMT) + 64]
            smW = cst[0:64, (C_SMT if d == 0 else C_SMF):(C_SMT if d == 0 else C_SMF) + 64]
            id64 = cst[0:64, C_ID:C_ID + 64]

            def c3(ap):
                return ap.rearrange("p (c i) -> p c i", c=NCH)

            def bc_f(ap2):
                return ap2.unsqueeze(1).to_broadcast([64, NCH, 64])

            def bc_col(slot):
                return tokm[:, slot, :, dh:dh + 1].to_broadcast([64, NCH, 64])

            GCb, GCbk = FB(5)
            Eg, Egk = FB(6)
            DG, DGk = FB(1)
            BbS, BbSk = FB(2)
            DT, DTk = FB(7)
            Dn, Dnk = FB(8)
            Wa, Wak = FB(9)
            Ya, Yak = FB(10)
            Wb, Wbk = FB(11)
            Yb, Ybk = FB(12)
            Z_, Zk = FB(13)
            if d == 0:
                QgT, QgTk = BB(9)
                KgT, KgTk = BB(10)
                MTAT, MTATk = BB(11)
                Kd = pmB[0:64, 0 * T:2 * T].rearrange("p (c v) -> p c v", c=NCH)
                Kdk = [('B', 0), ('B', 1)]
                AT, ATk = BB(2 + ((n + 1) % 2))
            else:
                QgT, QgTk = BB(12)
                KgT, KgTk = BB(13)
                MTAT, MTATk = BB(7)
                Kd = pmB[0:64, 14 * T:16 * T].rearrange("p (c v) -> p c v", c=NCH)
                Kdk = [('B', 14), ('B', 15)]
                AT, ATk = BB(8)

            for which, dst, dstk in ((TGC, GCb, GCbk), (TB, BbS, BbSk)):
                kb.op('dve', lambda e, which=which: e.tensor_tensor(out=c3(DG[0:64, :]), in0=bc_f(id64), in1=bc_col(which), op=ALU.mult),
                      reads=['cst', ('tokm', which)], writes=[DGk])
                for h2 in range(2):
                    ps, pk = newps()
                    kb.op('pe', lambda e, h2=h2: e.matmul(ps[:, :], ones[0:64, :], DG[0:64, h2 * HALF:(h2 + 1) * HALF], start=True, stop=True), reads=['cst', DGk], writes=[pk])
                    if which == TGC:
                        kb.op('act', lambda e, h2=h2: e.activation(out=Eg[:, h2 * HALF:(h2 + 1) * HALF], in_=ps[:, :], func=AF.Exp), reads=[pk], writes=[Egk])
                        kb.op('act', lambda e, h2=h2: e.activation(out=GCb[:, h2 * HALF:(h2 + 1) * HALF], in_=ps[:, :], func=AF.Identity), reads=[pk], writes=[GCbk])
                    else:
                        kb.op('dve', lambda e, h2=h2: e.tensor_tensor(out=BbS[0:64, h2 * HALF:(h2 + 1) * HALF].rearrange("p (c i) -> p c i", c=8), in0=ps[0:64, :].rearrange("p (c i) -> p c i", c=8),
                                                                      in1=smW.unsqueeze(1).to_broadcast([64, 8, 64]), op=ALU.mult),
                              reads=[pk, 'cst'], writes=[BbSk])
            egcol = 63 if d == 0 else 0
            kb.op('act', lambda e: e.activation(out=egl[:, d, :], in_=Eg.rearrange("p (c i) -> p c i", c=NCH)[:, :, egcol], func=AF.Identity), reads=[Egk], writes=[('egl', d)])
            kb.op('dve', lambda e: e.tensor_tensor(out=QgT, in0=qT, in1=Eg, op=ALU.mult), reads=[qTk, Egk], writes=[QgTk])
            kb.op('dve', lambda e: e.tensor_tensor(out=KgT, in0=kT, in1=Eg, op=ALU.mult), reads=[kTk, Egk], writes=[KgTk])
            kb.op('dve', lambda e: e.tensor_tensor(out=Kd, in0=ktok, in1=tokm[:, TKD, :, dh:dh + 1].to_broadcast([64, NCH, 128]), op=ALU.mult),
                  reads=ktk + [('tokm', TKD)], writes=Kdk)
            raw = GCb[0:64, :]
            kb.op('dve', lambda e: e.tensor_tensor(out=c3(raw), in0=c3(raw), in1=bc_col(TGC), op=ALU.subtract), reads=[GCbk, ('tokm', TGC)], writes=[GCbk])
            kb.op('dve', lambda e: e.scalar_tensor_tensor(out=c3(DT[0:64, :]), in0=c3(raw), scalar=0.0, in1=bc_f(nmDT), op0=ALU.min, op1=ALU.add), reads=[GCbk, 'cst'], writes=[DTk])
            kb.op('act', lambda e: e.activation(out=DT[0:64, :], in_=DT[0:64, :], func=AF.Exp), reads=[DTk], writes=[DTk])
            kb.op('dve', lambda e: e.scalar_tensor_tensor(out=c3(Dn[0:64, :]), in0=c3(raw), scalar=0.0, in1=bc_f(nmD), op0=ALU.max, op1=ALU.subtract), reads=[GCbk, 'cst'], writes=[Dnk])
            kb.op('act', lambda e: e.activation(out=Dn[0:64, :], in_=Dn[0:64, :], func=AF.Exp, scale=-1.0), reads=[Dnk], writes=[Dnk])
            if stop == 'D2':
                return finish()
            gps = []
            qps = []
            for h2 in range(2):
                ps, pk = newps()
                for cc in range(8):
                    c = h2 * 8 + cc
                    kb.op('pe', lambda e, c=c, cc=cc: e.matmul(ps[0:64, cc * 64:(cc + 1) * 64], kT[:, c * 64:(c + 1) * 64], kT[:, c * 64:(c + 1) * 64], start=True, stop=True),
                          reads=[kTk], writes=[pk], inc=(cc == 7))
                gps.append((ps, pk))
                ps, pk = newps()
                for cc in range(8):
                    c = h2 * 8 + cc
                    kb.op('pe', lambda e, c=c, cc=cc: e.matmul(ps[0:64, cc * 64:(cc + 1) * 64], kT[:, c * 64:(c + 1) * 64], qT[:, c * 64:(c + 1) * 64], start=True, stop=True),
                          reads=[kTk, qTk], writes=[pk], inc=(cc == 7))
                qps.append((ps, pk))
            for h2 in range(2):
                kb.op('dve', lambda e, h2=h2: e.tensor_tensor(out=AT[0:64, h2 * HALF:(h2 + 1) * HALF], in0=qps[h2][0][0:64, :], in1=DT[0:64, h2 * HALF:(h2 + 1) * HALF], op=ALU.mult),
                      reads=[qps[h2][1], DTk], writes=[ATk])
            kb.op('dve', lambda e: e.scalar_tensor_tensor(out=DT[0:64, :], in0=DT[0:64, :], scalar=-1.0, in1=BbS[0:64, :], op0=ALU.mult, op1=ALU.mult), reads=[DTk, BbSk], writes=[DTk])
            kb.op('dve', lambda e: e.tensor_tensor(out=c3(DG[0:64, :]), in0=bc_f(smY), in1=bc_col(TB), op=ALU.mult), reads=['cst', ('tokm', TB)], writes=[DGk])
            kb.op('dve', lambda e: e.scalar_tensor_tensor(out=Dn[0:64, :], in0=Dn[0:64, :], scalar=-1.0, in1=DG[0:64, :], op0=ALU.mult, op1=ALU.mult), reads=[Dnk, DGk], writes=[Dnk])
            for h2 in range(2):
                sl = slice(h2 * HALF, (h2 + 1) * HALF)
                kb.op('dve', lambda e, h2=h2, sl=sl: e.tensor_tensor(out=Wa[0:64, sl], in0=gps[h2][0][0:64, :], in1=DT[0:64, sl], op=ALU.mult), reads=[gps[h2][1], DTk], writes=[Wak])
                kb.op('dve', lambda e, h2=h2, sl=sl: e.tensor_tensor(out=Ya[0:64, sl], in0=gps[h2][0][0:64, :], in1=Dn[0:64, sl], op=ALU.mult), reads=[gps[h2][1], Dnk], writes=[Yak])
            kb.op('dve', lambda e: e.tensor_tensor(out=c3(Z_[0:64, :]), in0=c3(Wa[0:64, :]), in1=bc_f(id64), op=ALU.add), reads=[Wak, 'cst'], writes=[Zk])
            Wc, Wck, Yc, Yck = Wa, Wak, Ya, Yak
            Wn, Wnk, Yn, Ynk = Wb, Wbk, Yb, Ybk
            for lvl in range(1, 6):
                for h2 in range(2):
                    ps, pk = newps()
                    for cc in range(8):
                        c = h2 * 8 + cc
                        sl = slice(c * 64, (c + 1) * 64)
                        kb.op('pe', lambda e, cc=cc, sl=sl: e.matmul(ps[0:64, cc * 64:(cc + 1) * 64], Wc[0:64, sl], Yc[0:64, sl], start=True, stop=True), reads=[Wck, Yck], writes=[pk], inc=(cc == 7))
                    kb.op('act', lambda e, h2=h2: e.activation(out=Yn[0:64, h2 * HALF:(h2 + 1) * HALF], in_=ps[0:64, :], func=AF.Identity), reads=[pk], writes=[Ynk])
                    if lvl < 5:
                        ps, pk = newps()
                        for cc in range(8):
                            c = h2 * 8 + cc
                            sl = slice(c * 64, (c + 1) * 64)
                            kb.op('pe', lambda e, cc=cc, sl=sl: e.matmul(ps[0:64, cc * 64:(cc + 1) * 64], Yc[0:64, sl], Wc[0:64, sl], start=True, stop=True), reads=[Wck, Yck], writes=[pk], inc=(cc == 7))
                        kb.op('act', lambda e, h2=h2: e.activation(out=Wn[0:64, h2 * HALF:(h2 + 1) * HALF], in_=ps[0:64, :], func=AF.Identity), reads=[pk], writes=[Wnk])
                for h2 in range(2):
                    ps, pk = newps()
                    for cc in range(8):
                        c = h2 * 8 + cc
                        sl = slice(c * 64, (c + 1) * 64)
                        kb.op('pe', lambda e, cc=cc, sl=sl: e.matmul(ps[0:64, cc * 64:(cc + 1) * 64], Yn[0:64, sl], Z_[0:64, sl], start=True, stop=True), reads=[Ynk, Zk], writes=[pk], inc=(cc == 7))
                    kb.op('dve', lambda e, h2=h2: e.tensor_tensor(out=Z_[0:64, h2 * HALF:(h2 + 1) * HALF], in0=Z_[0:64, h2 * HALF:(h2 + 1) * HALF], in1=ps[0:64, :], op=ALU.add), reads=[pk, Zk], writes=[Zk])
                Wc, Wck, Yc, Yck, Wn, Wnk, Yn, Ynk = Wn, Wnk, Yn, Ynk, Wc, Wck, Yc, Yck
            MT = MTAT
            kb.op('dve', lambda e: e.tensor_tensor(out=c3(MT[0:64, :]), in0=c3(Z_[0:64, :]), in1=bc_col(TB), op=ALU.mult), reads=[Zk, ('tokm', TB)], writes=[MTATk])
            if n == 0 and d == 0:
                dump('MT', MT[0:64, :], MTATk)
                dump('AT', AT[0:64, :], ATk)
                dump('Eg', Eg, Egk)

            if stop == 'D3':
                return finish()
            DD[d] = dict(QgT=QgT, QgTk=QgTk, KgT=KgT, KgTk=KgTk, MT=MT, MTk=MTATk, AT=AT, ATk=ATk, Kd=Kd, Kdk=Kdk)

        Ob_, Obk = FB(5)
        for d in range(2):
            kb.dma('sp', out=S32[:, d, :], in_=s0_d[d, n], writes=[('S32', d)])
            kb.op('act', lambda e, d=d: e.activation(out=Sbf[:, d, :], in_=S32[:, d, :], func=AF.Identity), reads=[('S32', d)], writes=[('Sbf', d)])
        for ci in range(NCH):
            for d in range(2):
                P_ = DD[d]
                QgT, QgTk, KgT, KgTk, MT, MTk, AT, ATk, Kd, Kdk = (P_[k_] for k_ in ('QgT', 'QgTk', 'KgT', 'KgTk', 'MT', 'MTk', 'AT', 'ATk', 'Kd', 'Kdk'))
                c = ci if d == 0 else NCH - 1 - ci
                Sd32 = S32[:, d, :]
                Sdb = Sbf[:, d, :]
                sk32 = ('S32', d)
                skb = ('Sbf', d)
                t0 = c * 64
                seg = c // 4
                first_in_seg = (c % 4 == 0) if d == 0 else (c % 4 == 3)
                last_in_seg = (c % 4 == 3) if d == 0 else (c % 4 == 0)
                if first_in_seg and ci > 0:
                    kb.op('dve', lambda e, Sd32=Sd32: e.tensor_scalar(out=Sd32, in0=Sd32, scalar1=chain, scalar2=None, op0=ALU.mult), reads=[sk32, 'prm'], writes=[sk32])
                    kb.op('act', lambda e, Sd32=Sd32, Sdb=Sdb: e.activation(out=Sdb, in_=Sd32, func=AF.Identity), reads=[sk32], writes=[skb])
                ob, obk = resps(6 + d)
                ps, pk = newps()
                kb.op('pe', lambda e, t0=t0, KgT=KgT, Sdb=Sdb, ps=ps: e.matmul(ps[0:64, 0:128], KgT[:, t0:t0 + 64], Sdb, start=True, stop=True), reads=[KgTk, skb], writes=[pk])
                rslot = ci % 2
                Rv = RR[:, d * 2 + rslot, :]
                Rkey = ('RR', d * 2 + rslot)
                kb.op('dve', lambda e, c=c, Rv=Rv, ps=ps: e.tensor_tensor(out=Rv, in0=Vc[:, c, :], in1=ps[0:64, 0:128], op=ALU.subtract), reads=[Vck, Vck2, pk], writes=[Rkey])
                ps2, pk2 = newps()
                kb.op('pe', lambda e, t0=t0, Rv=Rv, MT=MT, ps2=ps2: e.matmul(ps2[0:64, 0:128], MT[0:64, t0:t0 + 64], Rv, start=True, stop=True), reads=[MTk, Rkey], writes=[pk2])
                vn = VN[:, d * 2 + rslot, :]
                vnk = ('VN', d * 2 + rslot)
                kb.op('act', lambda e, vn=vn, ps2=ps2: e.activation(out=vn, in_=ps2[0:64, 0:128], func=AF.Identity), reads=[pk2], writes=[vnk])
                col = (c % 8) * 64
                kb.op('pe', lambda e, t0=t0, col=col, ob=ob, Sdb=Sdb, QgT=QgT: e.matmul(ob[:, col:col + 64], Sdb, QgT[:, t0:t0 + 64], start=True, stop=False), reads=[skb, QgTk], writes=[obk], inc=False)
                kb.op('pe', lambda e, t0=t0, col=col, vn=vn, ob=ob, AT=AT: e.matmul(ob[:, col:col + 64], vn, AT[0:64, t0:t0 + 64], start=False, stop=True), reads=[vnk, ATk], writes=[obk])
                ps3, pk3 = newps()
                kb.op('pe', lambda e, c=c, vn=vn, Kd=Kd, ps3=ps3: e.matmul(ps3[:, 0:128], Kd[:, c, :], vn, start=True, stop=True), reads=Kdk + [vnk], writes=[pk3])
                kb.op('dve', lambda e, c=c, d=d, Sd32=Sd32, ps3=ps3: e.scalar_tensor_tensor(out=Sd32, in0=Sd32, scalar=egl[:, d, c:c + 1], in1=ps3[:, 0:128], op0=ALU.mult, op1=ALU.add),
                      reads=[sk32, ('egl', d), pk3], writes=[sk32])
                kb.op('act', lambda e, Sd32=Sd32, Sdb=Sdb: e.activation(out=Sdb, in_=Sd32, func=AF.Identity), reads=[sk32], writes=[skb])
                if last_in_seg:
                    si_ = d * 4 + (ci // 4) % 4
                    so = Sout[:, si_, :]
                    sok = ('Sout', si_)
                    kb.op('act', lambda e, so=so, Sd32=Sd32: e.activation(out=so, in_=Sd32, func=AF.Identity), reads=[sk32], writes=[sok])
                    kb.dma('sp', out=std_d[seg, d, n], in_=so, reads=[sok], writes=['std'])
                if ci % 8 == 7:
                    hb = (c // 8)
                    sl = slice(hb * HALF, (hb + 1) * HALF)
                    if d == 0:
                        kb.op('act', lambda e, sl=sl, ob=ob: e.activation(out=Oacc[:, sl], in_=ob[:, :], func=AF.Identity), reads=[obk], writes=[Oack])
                    else:
                        kb.op('act', lambda e, sl=sl, ob=ob: e.activation(out=Ob_[:, sl], in_=ob[:, :], func=AF.Identity), reads=[obk], writes=[Obk])
        kb.op('dve', lambda e: e.tensor_tensor(out=Oacc, in0=Oacc, in1=Ob_, op=ALU.add), reads=[Oack, Obk], writes=[Oack])
        t1, t1k = FB(1)
        t2, t2k = FB(2)
        kb.op('act', lambda e: e.activation(out=t1, in_=Oacc, func=AF.Square), reads=[Oack], writes=[t1k])
        for h2 in range(2):
            ps, pk = newps()
            kb.op('pe', lambda e, h2=h2: e.matmul(ps[:, :], ones, t1[:, h2 * HALF:(h2 + 1) * HALF], start=True, stop=True), reads=['cst', t1k], writes=[pk])
            kb.op('act', lambda e, h2=h2: e.activation(out=t2[:, h2 * HALF:(h2 + 1) * HALF], in_=ps[:, :], func=AF.Sqrt, scale=1.0 / 128.0, bias=RMS_EPS), reads=[pk], writes=[t2k])
        kb.op('dve', lambda e: e.reciprocal(out=t2, in_=t2), reads=[t2k], writes=[t2k])
        kb.op('dve', lambda e: e.scalar_tensor_tensor(out=t1, in0=Oacc, scalar=pcol(P_NW), in1=t2, op0=ALU.mult, op1=ALU.mult), reads=[Oack, t2k, 'prm'], writes=[t1k])
        dob, dobk = BB(2 + ((n + 1) % 2))
        kb.op('dve', lambda e: e.tensor_tensor(out=dob, in0=t1, in1=zs, op=ALU.mult), reads=[t1k, zsk], writes=[dobk])
        kb.dma('sp', out=dn_scr[n], in_=dob, reads=[dobk], writes=['dn_scr'])
        if n == 0:
            dump('dn0', dob, dobk)
            dump('oacc0', Oacc, Oack)
            if stop == 'M1':
                kb.dma('sp', out=stl_d, in_=stl[:], reads=['stl'], writes=['stl_d'])
                return finish()

    kb.dma('sp', out=stl_d, in_=stl[:], reads=['stl'], writes=['stl_d'])
    kb.op('dve', lambda e: e.tensor_scalar_add(out=sc1f[:], in0=modc[:, 128:160], scalar1=1.0), reads=['modc'], writes=['sc1f'])
    kb.barrier()
    for g_ in (vng, rrg, egg, hig, gwg, sog, sbg, s3g, ntg, tmg, xpg, pmb, pm):
        g_.__exit__(None, None, None)

    with nc.sbuf_tensor("wring2g", [128, 5 * 32 * 128], BF16) as wring2g, nc.sbuf_tensor("lod", [128, 2, NH, HALF], BF16) as lod, nc.sbuf_tensor("gsb", [128, 2, 2, HALF], F32) as gsb, \
            nc.sbuf_tensor("mgt", [128, 2, HALF], BF16) as mgt:
        ring32 = Ring(3, 32, 128, extra=wring2g)
        for hf in range(2):
            tsl = slice(hf * HALF, (hf + 1) * HALF)
            for n in range(NH):
                kb.dma('sp', out=lod[:, 0, n, :], in_=lru_scr[n, :, tsl], reads=['lru_scr'], writes=['lod'])
                kb.dma('sp', out=lod[:, 1, n, :], in_=dn_scr[n, :, tsl], reads=['dn_scr'], writes=['lod'])
            for f in range(KT):
                pss = []
                for br, w_d in ((0, wlp_d), (1, wdp_d)):
                    v, key = ring32.load(wview(w_d, 0, 16, f * 128, 128), 16, 128)
                    ps, pk = newps()
                    for kt in range(16):
                        kb.op('pe', lambda e, kt=kt, v=v, br=br: e.matmul(ps[:, :], v[:, kt, :], lod[:, br, kt, :], start=(kt == 0), stop=(kt == 15)), reads=[key, 'lod'], writes=[pk], inc=(kt == 15))
                    pss.append((ps, pk))
                gss = []
                for g in range(2):
                    wv, wk = ring32.load(wview(win_d, 0, 32, 12352 + g * D + f * 128, 128))
                    ps, pk = newps()
                    for kt in range(KT):
                        kb.op('pe', lambda e, kt=kt, wv=wv: e.matmul(ps[:, :], wv[:, kt, :], hT[:, kt, tsl], start=(kt == 0), stop=(kt == KT - 1)), reads=[wk, 'hT'], writes=[pk], inc=(kt == KT - 1))
                    gk = ('gsb', f % 2, g)
                    kb.op('act', lambda e, g=g, ps=ps: e.activation(out=gsb[:, f % 2, g, :], in_=ps[:, :], func=AF.Sigmoid, bias=pcol(P_BBR + g * 32 + f)), reads=[pk, 'prm'], writes=[gk])
                    gss.append(gk)
                for g in range(2):
                    kb.op('dve', lambda e, g=g: e.tensor_tensor(out=gsb[:, f % 2, g, :], in0=gsb[:, f % 2, g, :], in1=pss[g][0][:, :], op=ALU.mult), reads=[gss[g], pss[g][1]], writes=[gss[g]])
                mk = ('mgt', f % 2)
                kb.op('dve', lambda e: e.tensor_tensor(out=mgt[:, f % 2, :], in0=gsb[:, f % 2, 0, :], in1=gsb[:, f % 2, 1, :], op=ALU.add), reads=gss, writes=[mk])
                kb.dma('sp', out=mg_scr[hf, f], in_=mgt[:, f % 2, :], reads=[mk], writes=['mg_scr'])
    kb.barrier()
    hT_guard.__exit__(None, None, None)

    with nc.sbuf_tensor("acc", [128, KT, HALF], F32) as acc, nc.sbuf_tensor("H2", [128, KT, HALF], BF16) as H2, \
            nc.sbuf_tensor("mgh", [128, KT, HALF], BF16) as mgh, \
            nc.sbuf_tensor("lnt", [128, 4, HALF], F32) as lnt, nc.sbuf_tensor("lns", [128, 5, HALF], F32) as lns, \
            nc.sbuf_tensor("yst", [128, 2, HALF], F32) as yst, nc.sbuf_tensor("wring2o", [128, 2 * 32 * 128], BF16) as wring2o:
        ones_f = ones
        actb = mgh[:, 0:16, :]
        MGK = [('actb', j) for j in range(16)] + ['mgh']

        def ln_stats(get_tile, tag):
            s1, k1 = resps(6)
            s2, k2 = resps(7)
            for f in range(KT):
                ap, key = get_tile(f)
                sq = lnt[:, f % 2, :]
                sqk = ('lnt', f % 2)
                kb.op('act', lambda e, ap=ap, sq=sq: e.activation(out=sq, in_=ap, func=AF.Square), reads=[key], writes=[sqk])
                kb.op('pe', lambda e, ap=ap: e.matmul(s1[:, :], ones_f, ap, start=(f == 0), stop=(f == KT - 1)), reads=['cst', key], writes=[k1], inc=(f == KT - 1))
                kb.op('pe', lambda e, sq=sq: e.matmul(s2[:, :], ones_f, sq, start=(f == 0), stop=(f == KT - 1)), reads=['cst', sqk], writes=[k2], inc=True)
            mean, msq, var, rstd, nmr = (lns[:, i, :] for i in range(5))
            kb.op('dve', lambda e: e.tensor_scalar_mul(out=mean, in0=s1[:, :], scalar1=1.0 / D), reads=[k1], writes=[('lns', 0)])
            kb.op('dve', lambda e: e.tensor_tensor(out=msq, in0=mean, in1=mean, op=ALU.mult), reads=[('lns', 0)], writes=[('lns', 1)])
            kb.op('dve', lambda e: e.scalar_tensor_tensor(out=var, in0=s2[:, :], scalar=1.0 / D, in1=msq, op0=ALU.mult, op1=ALU.subtract), reads=[k2, ('lns', 1)], writes=[('lns', 2)])
            kb.op('act', lambda e: e.activation(out=var, in_=var, func=AF.Sqrt, bias=LN_EPS), reads=[('lns', 2)], writes=[('lns', 2)])
            kb.op('dve', lambda e: e.reciprocal(out=rstd, in_=var), reads=[('lns', 2)], writes=[('lns', 3)])
            kb.op('dve', lambda e: e.scalar_tensor_tensor(out=nmr, in0=mean, scalar=-1.0, in1=rstd, op0=ALU.mult, op1=ALU.mult), reads=[('lns', 0), ('lns', 3)], writes=[('lns', 4)])
            return rstd, ('lns', 3), nmr, ('lns', 4)

        ringO = Ring(3, 32, 128, extra=wring2o)
        ringU = ringO
        for hf in range(2):
            tsl = slice(hf * HALF, (hf + 1) * HALF)
            kb.dma('sp', out=mgh[:], in_=mg_scr[hf].rearrange("k p t -> p k t"), reads=['mg_scr'], writes=MGK)
            for f in range(KT):
                wv, wk = ringO.load(wview(wo_d, 0, 32, f * 128, 128))
                ps, pk = newps()
                for kt in range(KT):
                    kb.op('pe', lambda e, kt=kt, wv=wv: e.matmul(ps[:, :], wv[:, kt, :], mgh[:, kt, :], start=(kt == 0), stop=(kt == KT - 1)), reads=[wk] + MGK, writes=[pk], inc=(kt == KT - 1))
                xa = lnt[:, 2 + f % 2, :]
                xak = ('lnt', 2 + f % 2)
                kb.dma('sp', out=xa, in_=xT_scr[f, :, tsl], reads=['xT_scr'], writes=[xak])
                kb.op('act', lambda e, xa=xa: e.activation(out=xa, in_=xa, func=AF.Identity, scale=float(ALPHA)), reads=[xak], writes=[xak])
                kb.op('dve', lambda e, xa=xa, ps=ps: e.scalar_tensor_tensor(out=acc[:, f, :], in0=ps[:, :], scalar=gtm[:, f:f + 1], in1=xa, op0=ALU.mult, op1=ALU.add),
                      reads=[pk, xak, 'modc'], writes=[('acc', f)])
            rstd, rk_, nmr, nk_ = ln_stats(lambda f: (acc[:, f, :], ('acc', f)), 'ln1')
            for f in range(KT):
                tt_ = lnt[:, f % 2, :]
                ttk = ('lnt', f % 2)
                kb.op('dve', lambda e, tt_=tt_: e.tensor_tensor(out=tt_, in0=acc[:, f, :], in1=rstd, op=ALU.mult), reads=[('acc', f), rk_], writes=[ttk])
                kb.op('dve', lambda e, tt_=tt_: e.tensor_tensor(out=tt_, in0=tt_, in1=nmr, op=ALU.add), reads=[ttk, nk_], writes=[ttk])
                x1t = lnt[:, 2 + f % 2, :]
                x1k = ('lnt', 2 + f % 2)
                kb.op('act', lambda e, tt_=tt_, x1t=x1t: e.activation(out=x1t, in_=tt_, func=AF.Identity, scale=pcol(P_L1G + f), bias=pcol(P_L1B + f)), reads=[ttk, 'prm'], writes=[x1k])
                kb.dma('sp', out=x1_scr[f], in_=x1t, reads=[x1k], writes=[('x1_scr', f)])
                kb.op('act', lambda e, x1t=x1t: e.activation(out=H2[:, f, :], in_=x1t, func=AF.Identity, scale=sc1f[:, f:f + 1], bias=shf[:, f:f + 1]), reads=[x1k, 'sc1f', 'modc'], writes=['H2'])
            for g in range(8):
                for j in range(16):
                    wv, wk = ringU.load(wview(wup_d, 0, 32, (g * 16 + j) * 128, 128))
                    ps, pk = newps()
                    for kt in range(KT):
                        kb.op('pe', lambda e, kt=kt, wv=wv: e.matmul(ps[:, :], wv[:, kt, :], H2[:, kt, :], start=(kt == 0), stop=(kt == KT - 1)), reads=[wk, 'H2'], writes=[pk], inc=(kt == KT - 1))
                    rl = lnt[:, j % 2, :]
                    rlk = ('lnt', j % 2)
                    kb.op('act', lambda e, rl=rl, ps=ps: e.activation(out=rl, in_=ps[:, :], func=AF.Relu), reads=[pk], writes=[rlk])
                    kb.op('dve', lambda e, rl=rl, j=j: e.tensor_tensor(out=actb[:, j, :], in0=rl, in1=rl, op=ALU.mult), reads=[rlk], writes=[('actb', j)])
                for f in range(KT):
                    v, key = ringU.load(wview(wdn_d, g * 2048, 16, f * 128, 128), 16, 128)
                    ps, pk = newps()
                    for j in range(16):
                        kb.op('pe', lambda e, j=j, v=v: e.matmul(ps[:, :], v[:, j, :], actb[:, j, :], start=(j == 0), stop=(j == 15)), reads=[key, ('actb', j)], writes=[pk], inc=(j == 15))
                    if g == 0:
                        kb.op('act', lambda e, ps=ps: e.activation(out=acc[:, f, :], in_=ps[:, :], func=AF.Identity), reads=[pk], writes=[('acc', f)])
                    else:
                        kb.op('dve', lambda e, ps=ps: e.tensor_tensor(out=acc[:, f, :], in0=acc[:, f, :], in1=ps[:, :], op=ALU.add), reads=[pk, ('acc', f)], writes=[('acc', f)])
            for f in range(KT):
                xa = lnt[:, 2 + f % 2, :]
                xak = ('lnt', 2 + f % 2)
                kb.dma('sp', out=xa, in_=x1_scr[f], reads=[('x1_scr', f)], writes=[xak])
                kb.op('act', lambda e, xa=xa: e.activation(out=xa, in_=xa, func=AF.Identity, scale=float(ALPHA)), reads=[xak], writes=[xak])
                kb.op('dve', lambda e, xa=xa: e.scalar_tensor_tensor(out=acc[:, f, :], in0=acc[:, f, :], scalar=gtf[:, f:f + 1], in1=xa, op0=ALU.mult, op1=ALU.add),
                      reads=[('acc', f), xak, 'modc'], writes=[('acc', f)])
            rstd, rk_, nmr, nk_ = ln_stats(lambda f: (acc[:, f, :], ('acc', f)), 'ln2')
            for f in range(KT):
                kb.op('dve', lambda e: e.tensor_tensor(out=acc[:, f, :], in0=acc[:, f, :], in1=rstd, op=ALU.mult), reads=[('acc', f), rk_], writes=[('acc', f)])
                kb.op('dve', lambda e: e.tensor_tensor(out=acc[:, f, :], in0=acc[:, f, :], in1=nmr, op=ALU.add), reads=[('acc', f), nk_], writes=[('acc', f)])
                kb.op('act', lambda e: e.activation(out=acc[:, f, :], in_=acc[:, f, :], func=AF.Identity, scale=pcol(P_L2G + f), bias=pcol(P_L2B + f)), reads=[('acc', f), 'prm'], writes=[('acc', f)])
            for f4 in range(8):
                for tq in range(4):
                    ps, pk = newps()
                    for q in range(4):
                        f = f4 * 4 + q
                        kb.op('pe', lambda e, q=q, f=f: e.transpose(ps[:, q * 128:(q + 1) * 128], acc[:, f, tq * 128:(tq + 1) * 128], ident), reads=[('acc', f), 'cst'], writes=[pk], inc=(q == 3))
                    yi = (f4 * 4 + tq) % 2
                    kb.op('act', lambda e, yi=yi, ps=ps: e.activation(out=yst[:, yi, :], in_=ps[:, :], func=AF.Identity), reads=[pk], writes=[('yst', yi)])
                    r0 = hf * HALF + tq * 128
                    kb.dma('sp', out=y_d[r0:r0 + 128, f4 * 512:(f4 + 1) * 512], in_=yst[:, yi, :], reads=[('yst', yi)], writes=['y_d'])
    kb.barrier(['sp'])
    print("program: insts", kb.ninst, "waits", kb.nwait)
    return nc


def _consts():
    c = np.zeros((128, NCONST), np.float32)
    c[:, C_ID:C_ID + 128] = np.eye(128, dtype=np.float32)
    c[:, C_ONE:C_ONE + 128] = 1.0
    i = np.arange(64)
    src = i[:, None]; dst = i[None, :]
    c[0:64, C_TRIF:C_TRIF + 64] = (src <= dst)
    c[0:64, C_TRIB:C_TRIB + 64] = (src >= dst)
    c[0:64, C_NMF:C_NMF + 64] = np.where(src >= dst, 0.0, NEG)
    c[0:64, C_NMT:C_NMT + 64] = np.where(dst >= src, 0.0, NEG)
    c[0:64, C_SMF:C_SMF + 64] = (src > dst)
    c[0:64, C_SMT:C_SMT + 64] = (dst > src)
    c[:, C_JIDX:C_JIDX + 8] = np.arange(8)[None, :] * 128 + np.arange(128)[:, None]
    c[:, C_NIDX:C_NIDX + 64] = np.arange(64)[None, :]
    return c


def _colT(v, nt):
    return np.ascontiguousarray(np.asarray(v).reshape(nt, 128).T)


def _params(cvec, chain, posflag, h0, I):
    p = np.zeros((128, NPRM), np.float32)
    p[:, P_CT:P_CT + 32] = _colT(cvec, 32)
    p[:, P_BMOD:P_BMOD + 192] = _colT(I['b_mod'][0], 192)
    p[:, P_LCW:P_LCW + 64] = I['lru_conv_w'][0].reshape(4, 16, 128).transpose(2, 1, 0).reshape(128, 64)
    p[:, P_LCB:P_LCB + 16] = _colT(I['lru_conv_b'][0], 16)
    p[:, P_LGB:P_LGB + 64] = I['lru_gate_b'][0].reshape(2, 2, 16, 128).transpose(3, 2, 0, 1).reshape(128, 64)
    p[:, P_LAM:P_LAM + 32] = I['lru_lambda'][0].reshape(2, 16, 128).transpose(2, 1, 0).reshape(128, 32)
    p[:, P_DCW:P_DCW + 192] = I['dn_conv_w'][0].reshape(4, 3, 16, 128).transpose(3, 1, 2, 0).reshape(128, 192)
    p[:, P_ALOG:P_ALOG + 32] = I['dn_a_log'][0].reshape(1, 32)
    p[:, P_DTB:P_DTB + 32] = I['dn_dt_bias'][0].reshape(1, 32)
    p[:, P_NW] = I['dn_norm_w'][0]
    p[:, P_BBR:P_BBR + 64] = I['b_branch'][0].reshape(2, 32, 128).transpose(2, 0, 1).reshape(128, 64)
    p[:, P_L1G:P_L1G + 32] = _colT(I['ln1_g'][0], 32)
    p[:, P_L1B:P_L1B + 32] = _colT(I['ln1_b'][0], 32)
    p[:, P_L2G:P_L2G + 32] = _colT(I['ln2_g'][0], 32)
    p[:, P_L2B:P_L2B + 32] = _colT(I['ln2_b'][0], 32)
    p[:, P_FLAG] = chain
    p[:, P_FLAG + 1] = posflag
    p[:, P_H0:P_H0 + 32] = h0.reshape(2, 16, 128).transpose(2, 1, 0).reshape(128, 32)
    return p


_NC_CACHE = {}


def kernel(**I):
    I = {k: np.asarray(v) for k, v in I.items()}
    ncores = 8
    dbg = DEBUG.get('dbg', None)
    key = (repr(sorted((dbg or {}).items(), key=lambda kv: kv[0])), DEBUG.get('stop'))
    if key not in _NC_CACHE:
        d2 = dict(dbg or {})
        if 'modc_in' in DEBUG:
            d2['__modc_in'] = 1
        _NC_CACHE[key] = build_program(d2, DEBUG.get('stop'))
    nc = _NC_CACHE[key]
    cst = _consts()
    shared = {
        'cst': cst,
        'w_mod': np.ascontiguousarray(I['w_mod'][0]), 'w_in': np.ascontiguousarray(I['w_in'][0]),
        'lru_gw': np.ascontiguousarray(I['lru_gate_w'][0].transpose(2, 3, 0, 1, 4).reshape(16, 128, 4, 128)),
        'w_lp': np.ascontiguousarray(I['w_lru_proj'][0]), 'w_dp': np.ascontiguousarray(I['w_dn_proj'][0]),
        'w_o': np.ascontiguousarray(I['w_o'][0]), 'w_up': np.ascontiguousarray(I['w_up'][0]), 'w_down': np.ascontiguousarray(I['w_down'][0]),
    }
    zeros_x = np.zeros((T, D), np.float32)
    zeros_s0 = np.zeros((2, NH, 128, 128), np.float32)
    in_maps = []
    ncores_used = DEBUG.get('ncores', ncores)
    for core in range(ncores_used):
        m = dict(shared)
        if core < 2:
            b = core
            m['x'] = np.ascontiguousarray(I['x_sample'][b])
            m['prm'] = _params(I['c'][b], 1.0, 1.0, I['state_lru'][b, 0], I)
            m['dn_s0'] = np.ascontiguousarray(I['state_dn'][b, 0])
        elif core < 6:
            s = (core - 2) * 4
            m['x'] = np.ascontiguousarray(I['x_prompt'][s:s + 4].reshape(T, D))
            m['prm'] = _params(I['c_ctx'], 0.0, 0.0, np.zeros((2, 2048), np.float32), I)
            m['dn_s0'] = zeros_s0
        else:
            m['x'] = zeros_x
            m['prm'] = _params(I['c_ctx'], 0.0, 0.0, np.zeros((2, 2048), np.float32), I)
            m['dn_s0'] = zeros_s0
        in_maps.append(m)
    declared = set()
    for alloc in nc.allocations:
        try:
            if alloc.kind == "ExternalInput":
                declared.add(alloc.memorylocations[0].name)
        except Exception:
            pass
    if 'modc_in' in declared:
        for m in in_maps:
            m['modc_in'] = DEBUG['modc_in']
    in_maps = [{k: v for k, v in m.items() if k in declared} for m in in_maps]
    res = run_bass_kernel_spmd(nc, in_maps, core_ids=list(range(ncores_used)))
    R = res.results
    DEBUG['last'] = R
    if DEBUG.get('stop'):
        return None
    B = I['x_prompt'].shape[0]
    y_prompt = np.zeros((B, SEG, D), np.float32)
    y_sample = np.zeros((2, T, D), np.float32)
    st_lru = np.zeros((B, 1, 2, 2048), np.float32)
    st_dn = np.zeros((B, 1, 2, NH, 128, 128), np.float32)
    for core in range(min(ncores_used, 6)):
        r = R[core]
        if core < 2:
            y_sample[core] = r['y']
        else:
            s = (core - 2) * 4
            y_prompt[s:s + 4] = r['y'].reshape(4, SEG, D)
            sl = r['st_lru'].reshape(128, NH, NSEG, 2)
            st_lru[s:s + 4, 0] = sl.transpose(2, 3, 1, 0).reshape(4, 2, 2048)
            st_dn[s:s + 4, 0] = r['st_dn']
    return (y_prompt, y_sample, st_lru, st_dn)
```

```python
import numpy as np
import concourse.bass as bass
import concourse.mybir as mybir
from concourse.bass_utils import run_bass_kernel_spmd

F32 = mybir.dt.float32
BF16 = mybir.dt.bfloat16
AF = mybir.ActivationFunctionType
ALU = mybir.AluOpType

D = 4096; T = 1024; NSEG = 4; SEG = 256; NH = 16; KT = 32; C = 64; NCH = 16; DFF = 16384
HALF = 512
N_IN = 20544
ALPHA = 2.0 ** 0.25
LN_EPS = 1e-5
RMS_EPS = 1e-6
NEG = -30000.0

C_ID = 0; C_ONE = 128; C_TRIF = 256; C_TRIB = 320; C_NMF = 384; C_NMT = 448; C_SMF = 512; C_SMT = 576
C_JIDX = 640; C_NIDX = 648; NCONST = 712
P_CT = 0; P_BMOD = 32; P_LCW = 224; P_LCB = 288; P_LGB = 304; P_LAM = 368; P_DCW = 400; P_ALOG = 592
P_DTB = 624; P_NW = 656; P_BBR = 657; P_L1G = 721; P_L1B = 753; P_L2G = 785; P_L2B = 817; P_FLAG = 849
P_H0 = 851; NPRM = 883

DEBUG = {}


class KB:
    def __init__(self, nc):
        self.nc = nc
        self.E = {'pe': nc.tensor, 'act': nc.scalar, 'dve': nc.vector, 'pool': nc.gpsimd, 'sp': nc.sync}
        self.sem = {}
        self.cnt = {}
        for e in self.E:
            self.sem[('c', e)] = nc.alloc_semaphore(name=f"c_{e}")
            self.cnt[e] = 0
        self.NDS = 8
        self.dcnt = {'pool': 0, 'sp': 0}
        for q in self.dcnt:
            for i in range(self.NDS):
                self.sem[('d', q, i)] = nc.alloc_semaphore(name=f"d_{q}{i}")
        self.known = {e: {} for e in self.E}
        self.lastw = {}
        self.readers = {}
        self.nwait = 0
        self.ninst = 0

    def need(self, e, sk, val):
        if self.known[e].get(sk, 0) < val:
            self.E[e].wait_ge(self.sem[sk], val)
            self.known[e][sk] = val
            self.nwait += 1

    def _deps(self, e, reads, writes):
        own = ('c', e)
        for k in reads:
            lw = self.lastw.get(k)
            if lw is not None:
                if lw[0] == own and e == 'pe':
                    continue
                self.need(e, lw[0], lw[1])
        for k in writes:
            lw = self.lastw.get(k)
            if lw is not None and lw[0] != own:
                self.need(e, lw[0], lw[1])
            for rk, rv in self.readers.get(k, {}).items():
                if rk != own:
                    self.need(e, rk, rv)

    def _mark(self, sk, val, reads, writes):
        for k in writes:
            self.lastw[k] = (sk, val)
            self.readers[k] = {}
        for k in reads:
            d = self.readers.setdefault(k, {})
            if d.get(sk, 0) < val:
                d[sk] = val

    def op(self, e, fn, reads=(), writes=(), inc=True):
        self._deps(e, reads, writes)
        inst = fn(self.E[e])
        self.ninst += 1
        if inc:
            self.cnt[e] += 1
            inst.then_inc(self.sem[('c', e)], 1)
            val = self.cnt[e]
        else:
            val = self.cnt[e] + 1
        self._mark(('c', e), val, reads, writes)
        return inst

    def dma(self, q, out, in_, reads=(), writes=()):
        i = self.dcnt[q]
        slot = i % self.NDS
        rnd = i // self.NDS
        sk = ('d', q, slot)
        if rnd > 0:
            self.need(q, sk, 16 * rnd)
        self._deps(q, reads, writes)
        inst = self.E[q].dma_start(out=out, in_=in_)
        inst.then_inc(self.sem[sk], 16)
        self.ninst += 1
        self.dcnt[q] += 1
        self._mark(sk, 16 * (rnd + 1), reads, writes)
        return inst

    def barrier(self, engines=None):
        cur = {}
        for e in self.E:
            if self.cnt[e] > 0:
                cur[('c', e)] = self.cnt[e]
        for q, n in self.dcnt.items():
            for s in range(self.NDS):
                k = (n - 1 - s) // self.NDS + 1 if n > s else 0
                if k > 0:
                    cur[('d', q, s)] = 16 * k
        for e in (engines or self.E):
            for sk, v in cur.items():
                if sk == ('c', e):
                    continue
                self.need(e, sk, v)


def build_program(dbg=None, stop=None):
    dbg = dbg or {}
    nc = bass.Bass("TRN2", target_bir_lowering=False)
    kb = KB(nc)

    def din(name, shape, dt=F32):
        return nc.dram_tensor(name, list(shape), dt, kind="ExternalInput").ap()

    def dout(name, shape, dt=F32):
        return nc.dram_tensor(name, list(shape), dt, kind="ExternalOutput").ap()

    def dscr(name, shape, dt):
        return nc.dram_tensor(name, list(shape), dt).ap()

    NEED = {'0': {'w_mod'}, 'A': {'w_mod'}, 'S': {'w_mod', 'w_in'}, 'S1': {'w_mod', 'w_in'}, 'S2': {'w_mod', 'w_in'}, 'L0': {'w_mod', 'w_in', 'lru_gw'}, 'D1': {'w_mod', 'w_in', 'lru_gw'}, 'D2': {'w_mod', 'w_in', 'lru_gw'}, 'D3': {'w_mod', 'w_in', 'lru_gw'}, 'M1': {'w_mod', 'w_in', 'lru_gw'},
            'M': {'w_mod', 'w_in', 'lru_gw'}, 'G': {'w_mod', 'w_in', 'lru_gw', 'w_lp', 'w_dp'}}
    need = NEED.get(stop)
    modc_in = dbg.pop('__modc_in', None) is not None
    if need is not None and modc_in:
        need = need - {'w_mod'}
    _din = din

    def din(name, shape, dt=F32):
        if need is not None and name.startswith('w_') or (need is not None and name == 'lru_gw'):
            if name not in need:
                return None
        return _din(name, shape, dt)

    x_d = din("x", [T, D])
    prm_d = din("prm", [128, NPRM])
    cst_d = din("cst", [128, NCONST])
    s0_d = din("dn_s0", [2, NH, 128, 128])
    wmod_d = din("w_mod", [D, 6 * D])
    win_d = din("w_in", [D, N_IN])
    lgw_d = din("lru_gw", [NH, 128, 4, 128])
    wlp_d = din("w_lp", [2048, D])
    wdp_d = din("w_dp", [2048, D])
    wo_d = din("w_o", [D, D])
    wup_d = din("w_up", [D, DFF])
    wdn_d = din("w_down", [DFF, D])
    y_d = dout("y", [T, D])
    stl_d = dout("st_lru", [128, NH * NSEG * 2])
    std_d = dout("st_dn", [NSEG, 2, NH, 128, 128])
    xT_scr = dscr("xT_scr", [KT, 128, T], F32)
    lru_scr = dscr("lru_scr", [NH, 128, T], BF16)
    dn_scr = dscr("dn_scr", [NH, 128, T], BF16)
    mg_scr = dscr("mg_scr", [2, KT, 128, HALF], BF16)
    r1_scr = dscr("r1_scr", [KT, 128, HALF], F32)
    x1_scr = dscr("x1_scr", [KT, 128, HALF], F32)
    dbg_d = {}
    for name, shape in dbg.items():
        if isinstance(shape, tuple):
            dbg_d[name] = dout("dbg_" + name, shape[0], shape[1])
        else:
            dbg_d[name] = dout("dbg_" + name, shape)
    modc_d = _din("modc_in", [128, 192]) if modc_in else None

    def sb(name, shape, dt=F32):
        return nc.alloc_sbuf_tensor(name + "_sb", list(shape), dt)

    cst = sb("cst", [128, NCONST])
    prm = sb("prm", [128, NPRM])
    identb = sb("identb", [128, 128], BF16)
    modc = sb("modc", [128, 192])
    sc1m = sb("sc1m", [128, 32])
    sc1f = sb("sc1f", [128, 32])
    cdl = sb("cdl", [128, 32])
    cdl2 = sb("cdl2", [128, 32])
    stl = sb("stl", [128, NH * NSEG * 2])
    wring = sb("wring", [128, 3 * 32 * 128], BF16)
    psum = [nc.alloc_psum_tensor(f"ps{i}", [128, 512], F32) for i in range(8)]
    ps_state = {'i': 0}

    def newps():
        i = ps_state['i']
        ps_state['i'] = (i + 1) % 6
        return psum[i], ('ps', i)

    def resps(i):
        return psum[i], ('ps', i)

    ident = cst[:, C_ID:C_ID + 128]
    ones = cst[:, C_ONE:C_ONE + 128]

    def pcol(off, n=1):
        return prm[:, off:off + n]

    class Ring:
        def __init__(self, nslot, kt, cols, extra=None, base=None, prefix='wr'):
            self.kt = kt; self.cols = cols; self.i = 0; self.prefix = prefix
            per = kt * cols
            base = wring if base is None else base
            assert nslot * per <= base.shape[1]
            self.views = [base[:, s * per:(s + 1) * per].rearrange("p (k c) -> p k c", k=kt) for s in range(nslot)]
            if extra is not None:
                ne = extra.shape[1] // per
                self.views += [extra[:, s * per:(s + 1) * per].rearrange("p (k c) -> p k c", k=kt) for s in range(ne)]
            self.nslot = len(self.views)

        def load(self, src_ap, nk=None, ncol=None):
            s = self.i % self.nslot
            self.i += 1
            v = self.views[s]
            if nk is not None or ncol is not None:
                v = v[:, 0:(nk or self.kt), 0:(ncol or self.cols)]
            key = (self.prefix, s)
            kb.dma('pool', out=v, in_=src_ap, writes=[key])
            return v, key

    def wview(w_d, r0, nk, c0, ncol):
        return w_d[r0:r0 + nk * 128, c0:c0 + ncol].rearrange("(k p) n -> p k n", p=128)

    def dump(name, ap, key):
        if name in dbg_d:
            kb.dma('sp', out=dbg_d[name], in_=ap, reads=[key] if not isinstance(key, list) else key)

    def finish():
        kb.barrier(['sp'])
        print("program: insts", kb.ninst, "waits", kb.nwait)
        return nc

    kb.dma('sp', out=cst[:], in_=cst_d, writes=['cst'])
    kb.dma('sp', out=prm[:], in_=prm_d, writes=['prm'])
    kb.op('dve', lambda e: e.tensor_copy(out=identb[:], in_=ident), reads=['cst'], writes=['identb'])
    kb.op('dve', lambda e: e.memset(stl[:], 0.0), writes=['stl'])

    if modc_in:
        kb.dma('sp', out=modc[:], in_=modc_d, writes=['modc'])
    cs = sb("cs", [128, 32], BF16)
    rowb = sb("rowb", [1, 2 * 128], F32)
    kb.op('act', lambda e: e.activation(out=cs[:], in_=pcol(P_CT, 32), func=AF.Silu), reads=['prm'], writes=['cs'])
    mb_state = {'i': 0}

    def mod_block(ring, col0, ncols, rowb=rowb):
        i = mb_state['i']
        mb_state['i'] += 1
        wv, wk = ring.load(wview(wmod_d, 0, 32, col0, ncols), 32, ncols)
        ps, pk = newps()
        for kt in range(KT):
            kb.op('pe', lambda e, kt=kt: e.matmul(ps[0:1, 0:ncols], cs[:, kt:kt + 1], wv[:, kt, :], start=(kt == 0), stop=(kt == KT - 1)),
                  reads=['cs', wk], writes=[pk], inc=(kt == KT - 1))
        rb = rowb[0:1, (i % 2) * ncols:(i % 2) * ncols + ncols]
        rk = ('rowb', i % 2)
        kb.op('act', lambda e: e.activation(out=rb, in_=ps[0:1, 0:ncols], func=AF.Identity), reads=[pk], writes=[rk])
        ps2, pk2 = newps()
        nj = ncols // 128
        for j in range(nj):
            kb.op('pe', lambda e, j=j: e.matmul(ps2[:, j:j + 1], rb[0:1, j * 128:(j + 1) * 128], ones[0:1, 0:1], start=True, stop=True),
                  reads=[rk, 'cst'], writes=[pk2], inc=(j == nj - 1))
        c0 = col0 // 128
        kb.op('dve', lambda e: e.tensor_tensor(out=modc[:, c0:c0 + nj], in0=ps2[:, 0:nj], in1=pcol(P_BMOD + c0, nj), op=ALU.add),
              reads=[pk2, 'prm'], writes=['modc'])

    if not modc_in:
        with nc.sbuf_tensor("wr0", [128, 3 * 32 * 256], BF16) as wr0, nc.sbuf_tensor("rowb0", [1, 2 * 256], F32) as rowb0:
            ring0 = Ring(3, 32, 256, base=wr0, prefix='wr0')
            for blk in range(32):
                mod_block(ring0, blk * 256, 256, rowb=rowb0)
            kb.barrier()
    kb.op('dve', lambda e: e.tensor_scalar_add(out=sc1m[:], in0=modc[:, 32:64], scalar1=1.0), reads=['modc'], writes=['sc1m'])
    shm = modc[:, 0:32]; gtm = modc[:, 64:96]; shf = modc[:, 96:128]; gtf = modc[:, 160:192]
    with nc.sbuf_tensor("etmp", [128, 32], F32) as etmp, nc.sbuf_tensor("etmp2", [128, 32], F32) as etmp2:
        kb.op('act', lambda e: e.activation(out=etmp[:], in_=pcol(P_LAM, 32), func=AF.Exp, scale=-1.0), reads=['prm'], writes=['etmp'])
        kb.op('dve', lambda e: e.tensor_scalar(out=etmp2[:], in0=etmp[:], scalar1=1.0 / 3.0, scalar2=-0.5, op0=ALU.mult, op1=ALU.add), reads=['etmp'], writes=['etmp2'])
        kb.op('dve', lambda e: e.tensor_tensor(out=etmp2[:], in0=etmp2[:], in1=etmp[:], op=ALU.mult), reads=['etmp', 'etmp2'], writes=['etmp2'])
        kb.op('dve', lambda e: e.tensor_scalar_add(out=etmp2[:], in0=etmp2[:], scalar1=1.0), reads=['etmp2'], writes=['etmp2'])
        kb.op('dve', lambda e: e.tensor_tensor(out=etmp2[:], in0=etmp2[:], in1=etmp[:], op=ALU.mult), reads=['etmp', 'etmp2'], writes=['etmp2'])
        kb.op('dve', lambda e: e.tensor_scalar_mul(out=cdl[:], in0=etmp2[:], scalar1=-8.0), reads=['etmp2'], writes=['cdl'])
        kb.op('dve', lambda e: e.tensor_scalar_mul(out=cdl2[:], in0=etmp2[:], scalar1=-16.0), reads=['etmp2'], writes=['cdl'])
    dump('modc', modc[:], 'modc')
    kb.barrier()
    if stop == '0':
        return finish()
    ringW = Ring(3, 32, 128)

    hT_guard = nc.sbuf_tensor("hT", [128, KT, T], BF16)
    hT = hT_guard.__enter__()

    with nc.sbuf_tensor("xtok", [128, 4, D], F32) as xtok, nc.sbuf_tensor("xr", [128, 4, HALF], F32) as xr, \
            nc.sbuf_tensor("tabS", [128, 8, 64], F32) as tabS, nc.sbuf_tensor("tabC", [128, 8, 64], F32) as tabC, \
            nc.sbuf_tensor("om", [128, 8], F32) as om, nc.sbuf_tensor("targ", [128, 8, 64], F32) as targ, \
            nc.sbuf_tensor("tk", [128, 8, 64], F32) as tk:
        kb.op('act', lambda e: e.activation(out=om[:], in_=cst[:, C_JIDX:C_JIDX + 8], func=AF.Exp, scale=-float(np.log(10000.0)) / 1024.0),
              reads=['cst'], writes=['om'])
        for tab, off in ((tabS, 0.0), (tabC, 0.25)):
            kb.op('dve', lambda e: e.tensor_tensor(out=targ[:], in0=om[:].unsqueeze(2).to_broadcast([128, 8, 64]),
                                                   in1=cst[:, C_NIDX:C_NIDX + 64].unsqueeze(1).to_broadcast([128, 8, 64]), op=ALU.mult),
                  reads=['om', 'cst'], writes=['targ'])
            kb.op('dve', lambda e, off=off: e.tensor_scalar(out=targ[:], in0=targ[:], scalar1=1.0 / (2 * np.pi), scalar2=off, op0=ALU.mult, op1=ALU.add),
                  reads=['targ'], writes=['targ'])
            kb.op('dve', lambda e: e.memset(tk[:], 0.0), writes=['tk'])
            for m in range(1, 12):
                kb.op('dve', lambda e, m=m: e.scalar_tensor_tensor(out=tk[:], in0=targ[:], scalar=float(m), in1=tk[:], op0=ALU.is_ge, op1=ALU.add),
                      reads=['targ', 'tk'], writes=['tk'])
            kb.op('dve', lambda e: e.tensor_tensor(out=targ[:], in0=targ[:], in1=tk[:], op=ALU.subtract), reads=['targ', 'tk'], writes=['targ'])
            kb.op('dve', lambda e: e.tensor_scalar(out=tk[:], in0=targ[:], scalar1=0.5, scalar2=None, op0=ALU.is_gt), reads=['targ'], writes=['tk'])
            kb.op('dve', lambda e: e.tensor_tensor(out=targ[:], in0=targ[:], in1=tk[:], op=ALU.subtract), reads=['targ', 'tk'], writes=['targ'])
            kb.op('act', lambda e, tab=tab: e.activation(out=tab[:], in_=targ[:], func=AF.Sin, scale=float(2 * np.pi)), reads=['targ'], writes=['tab'])
        posflag = pcol(P_FLAG + 1)
        for hf in range(2):
            for j in range(4):
                tt = hf * 4 + j
                kb.dma('sp', out=xtok[:, j, :], in_=x_d[tt * 128:(tt + 1) * 128, :], writes=[('xtok', j)])
            for ft in range(KT):
                ps, pk = newps()
                for j in range(4):
                    kb.op('pe', lambda e, j=j: e.transpose(ps[:, j * 128:(j + 1) * 128], xtok[:, j, ft * 128:(ft + 1) * 128], ident),
                          reads=[('xtok', j), 'cst'], writes=[pk], inc=(j == 3))
                qd, jt = ft // 8, ft % 8
                tab = tabS if qd in (0, 2) else tabC
                if qd < 2:
                    pos_ap = tab[:, jt, 8 * hf:8 * hf + 8].unsqueeze(2).to_broadcast([128, 8, 64])
                else:
                    pos_ap = tab[:, jt, :].unsqueeze(1).to_broadcast([128, 8, 64])
                xrv = xr[:, ft % 4, :]
                xk = ('xr', ft % 4)
                kb.op('dve', lambda e: e.scalar_tensor_tensor(out=xrv.rearrange("p (a b) -> p a b", a=8), in0=pos_ap, scalar=posflag,
                                                              in1=ps[:, :].rearrange("p (a b) -> p a b", a=8), op0=ALU.mult, op1=ALU.add),
                      reads=[pk, 'tab', 'prm'], writes=[xk])
                kb.op('act', lambda e: e.activation(out=hT[:, ft, hf * HALF:(hf + 1) * HALF], in_=xrv, func=AF.Identity,
                                                    scale=sc1m[:, ft:ft + 1], bias=shm[:, ft:ft + 1]),
                      reads=[xk, 'sc1m', 'modc'], writes=['hT'])
                kb.dma('sp', out=xT_scr[ft, :, hf * HALF:(hf + 1) * HALF], in_=xrv, reads=[xk], writes=['xT_scr'])
    dump('hT', hT[:, 0:4, :], 'hT')
    if stop == 'A':
        return finish()

    pm = nc.sbuf_tensor("pmF", [128, 14 * T], F32)
    pmF = pm.__enter__()
    pmb = nc.sbuf_tensor("pmB", [128, 16 * T], BF16)
    pmB = pmb.__enter__()
    xpg = nc.sbuf_tensor("xpad", [128, NSEG, SEG + 4], F32)
    xpad = xpg.__enter__()
    smallT = pmF[0:64, 13 * T:14 * T].rearrange("p (c k) -> p c k", c=NCH)
    tmg = nc.sbuf_tensor("tokm", [64, 3, NCH, 32], F32)
    tokm = tmg.__enter__()
    ntg = nc.sbuf_tensor("negA", [64, 32], F32)
    negA = ntg.__enter__()
    s3g = nc.sbuf_tensor("S32", [128, 2, 128], F32)
    S32 = s3g.__enter__()
    sbg = nc.sbuf_tensor("Sbf", [128, 2, 128], BF16)
    Sbf = sbg.__enter__()
    sog = nc.sbuf_tensor("Sout", [128, 8, 128], F32)
    Sout = sog.__enter__()
    gwg = nc.sbuf_tensor("gw", [128, 2, 4, 128], BF16)
    gwt = gwg.__enter__()
    hig = nc.sbuf_tensor("hinit", [128, 2], F32)
    hinit = hig.__enter__()
    egg = nc.sbuf_tensor("egl", [128, 2, NCH], F32)
    egl = egg.__enter__()
    rrg = nc.sbuf_tensor("RR", [64, 4, 128], BF16)
    RR = rrg.__enter__()
    vng = nc.sbuf_tensor("VN", [64, 4, 128], BF16)
    VN = vng.__enter__()

    def FB(i):
        return pmF[:, i * T:(i + 1) * T], ('F', i)

    def BB(i):
        return pmB[:, i * T:(i + 1) * T], ('B', i)

    chain = pcol(P_FLAG)
    kb.op('dve', lambda e: e.memset(xpad[:], 0.0), writes=['xpad'])

    wsm, wsk = ringW.load(wview(win_d, 0, 32, 12288, 64), 32, 64)
    for half in range(2):
        ps, pk = newps()
        for cc in range(8):
            c = half * 8 + cc
            for kt in range(KT):
                kb.op('pe', lambda e, kt=kt, c=c, cc=cc: e.matmul(ps[0:64, cc * 64:(cc + 1) * 64], hT[:, kt, c * 64:(c + 1) * 64], wsm[:, kt, :],
                                                                  start=(kt == 0), stop=(kt == KT - 1)),
                      reads=['hT', wsk], writes=[pk], inc=(kt == KT - 1 and cc == 7))
        kb.op('act', lambda e: e.activation(out=smallT[:, half * 8:(half + 1) * 8, :], in_=ps[0:64, :].rearrange("p (a b) -> p a b", a=8), func=AF.Identity),
              reads=[pk], writes=[('F', 13)])
    if stop == 'S1':
        dump('tokm', smallT[:, :, 0:32].rearrange("p (a c) k -> p a c k", a=4), [('F', 13)]) if False else None
        return finish()
    TB, TGC, TKD = range(3)
    TG = TKD
    tkt = pmF[0:64, 12 * T:12 * T + NCH * 32].rearrange("p (c k) -> p c k", c=NCH)
    kb.op('act', lambda e: e.activation(out=tokm[:, TB], in_=smallT[:, :, 0:32], func=AF.Sigmoid), reads=[('F', 13)], writes=[('tokm', TB)])
    kb.op('act', lambda e: e.activation(out=negA[:], in_=prm[0:64, P_ALOG:P_ALOG + 32], func=AF.Exp), reads=['prm'], writes=['negA'])
    kb.op('dve', lambda e: e.tensor_tensor(out=tkt, in0=smallT[:, :, 32:64], in1=prm[0:64, P_DTB:P_DTB + 32].unsqueeze(1).to_broadcast([64, NCH, 32]), op=ALU.add),
          reads=[('F', 13), 'prm'], writes=[('F', 12)])
    kb.op('act', lambda e: e.activation(out=tkt, in_=tkt, func=AF.Exp), reads=[('F', 12)], writes=[('F', 12)])
    kb.op('act', lambda e: e.activation(out=tkt, in_=tkt, func=AF.Ln, bias=1.0), reads=[('F', 12)], writes=[('F', 12)])
    kb.op('dve', lambda e: e.scalar_tensor_tensor(out=tokm[:, TG], in0=tkt, scalar=-1.0, in1=negA[:].unsqueeze(1).to_broadcast([64, NCH, 32]), op0=ALU.mult, op1=ALU.mult),
          reads=[('F', 12), 'negA'], writes=[('tokm', TG)])
    if stop == 'S2':
        dump('tokm', tokm[:].rearrange("p a b c -> p (a b c)"), [('tokm', i) for i in range(3)])
        return finish()
    ps, pk = newps()
    ps2, pk2 = newps()
    for c in range(NCH):
        for d in range(2):
            tri = cst[0:64, (C_TRIF if d == 0 else C_TRIB):(C_TRIF if d == 0 else C_TRIB) + 64]
            last = (c == NCH - 1 and d == 1)
            kb.op('pe', lambda e, c=c, d=d, tri=tri: e.matmul(ps[0:64, c * 32 + d * 16:c * 32 + d * 16 + 16], tri, tokm[:, TG, c, d * 16:(d + 1) * 16], start=True, stop=True),
                  reads=['cst', ('tokm', TG)], writes=[pk], inc=False)
            kb.op('pe', lambda e, c=c, d=d: e.matmul(ps2[0:64, c * 32 + d * 16:c * 32 + d * 16 + 16], ones[0:64, 0:64], tokm[:, TG, c, d * 16:(d + 1) * 16], start=True, stop=True),
                  reads=['cst', ('tokm', TG)], writes=[pk2], inc=last)
    kb.op('dve', lambda e: e.tensor_copy(out=tokm[:, TGC], in_=ps[0:64, :].rearrange("p (a b) -> p a b", a=NCH)), reads=[pk], writes=[('tokm', TGC)])
    kb.op('dve', lambda e: e.tensor_copy(out=tkt, in_=ps2[0:64, :].rearrange("p (a b) -> p a b", a=NCH)), reads=[pk2], writes=[('F', 12)])
    kb.op('dve', lambda e: e.tensor_tensor(out=tokm[:, TKD], in0=tkt, in1=tokm[:, TGC], op=ALU.subtract), reads=[('F', 12), ('tokm', TGC)], writes=[('tokm', TKD)])
    kb.op('act', lambda e: e.activation(out=tokm[:, TKD], in_=tokm[:, TKD], func=AF.Exp), reads=[('tokm', TKD)], writes=[('tokm', TKD)])
    dump('tokm', tokm[:].rearrange("p a b c -> p (a b c)"), [('tokm', i) for i in range(3)])
    if stop == 'S':
        return finish()

    ringM = ringW

    def proj_fm(col0):
        wv, wk = ringM.load(wview(win_d, 0, 32, col0, 128))
        pa, ka = newps()
        pb, kbk = newps()
        for kt in range(KT):
            kb.op('pe', lambda e, kt=kt: e.matmul(pa[:, :], wv[:, kt, :], hT[:, kt, 0:HALF], start=(kt == 0), stop=(kt == KT - 1)),
                  reads=['hT', wk], writes=[ka], inc=False)
            kb.op('pe', lambda e, kt=kt: e.matmul(pb[:, :], wv[:, kt, :], hT[:, kt, HALF:T], start=(kt == 0), stop=(kt == KT - 1)),
                  reads=['hT', wk], writes=[kbk], inc=(kt == KT - 1))
        return (pa, ka), (pb, kbk)

    def conv_from_psum(pp, cw_off, bias_ap, out_ap, out_key):
        for h2, (p_, k_) in enumerate(pp):
            kb.op('act', lambda e, h2=h2, p_=p_: e.activation(out=xpad[:, 2 * h2:2 * h2 + 2, 2:2 + SEG], in_=p_[:, :].rearrange("p (a b) -> p a b", a=2), func=AF.Identity),
                  reads=[k_], writes=['xpad'])
        kb.op('dve', lambda e: e.tensor_scalar(out=xpad[:, 1:4, 0:2], in0=xpad[:, 0:3, SEG:SEG + 2], scalar1=chain, scalar2=None, op0=ALU.mult),
              reads=['xpad', 'prm'], writes=['xpad'])
        kb.op('dve', lambda e: e.tensor_scalar(out=xpad[:, 0:3, SEG + 2:SEG + 3], in0=xpad[:, 1:4, 2:3], scalar1=chain, scalar2=None, op0=ALU.mult),
              reads=['xpad', 'prm'], writes=['xpad'])
        o3 = out_ap.rearrange("p (a b) -> p a b", a=NSEG)
        if bias_ap is None:
            kb.op('dve', lambda e: e.tensor_scalar(out=o3, in0=xpad[:, :, 0:SEG], scalar1=pcol(cw_off), scalar2=None, op0=ALU.mult),
                  reads=['xpad', 'prm'], writes=[out_key])
        else:
            kb.op('dve', lambda e: e.tensor_scalar(out=o3, in0=xpad[:, :, 0:SEG], scalar1=pcol(cw_off), scalar2=bias_ap, op0=ALU.mult, op1=ALU.add),
                  reads=['xpad', 'prm'], writes=[out_key])
        for j in range(1, 4):
            kb.op('dve', lambda e, j=j: e.scalar_tensor_tensor(out=o3, in0=xpad[:, :, j:j + SEG], scalar=pcol(cw_off + j), in1=o3, op0=ALU.mult, op1=ALU.add),
                  reads=['xpad', 'prm', out_key], writes=[out_key])

    def l2norm_to_bf(src, sk, tmp, tk_, rn, rk, dst, dk_, scale):
        kb.op('act', lambda e: e.activation(out=tmp, in_=src, func=AF.Square), reads=[sk], writes=[tk_])
        for h2 in range(2):
            ps, pk = newps()
            kb.op('pe', lambda e, h2=h2: e.matmul(ps[:, :], ones, tmp[:, h2 * HALF:(h2 + 1) * HALF], start=True, stop=True), reads=['cst', tk_], writes=[pk])
            kb.op('act', lambda e, h2=h2: e.activation(out=rn[:, h2 * HALF:(h2 + 1) * HALF], in_=ps[:, :], func=AF.Sqrt, bias=RMS_EPS), reads=[pk], writes=[rk])
        kb.op('dve', lambda e: e.reciprocal(out=rn, in_=rn), reads=[rk], writes=[rk])
        kb.op('dve', lambda e: e.scalar_tensor_tensor(out=dst, in0=src, scalar=float(scale), in1=rn, op0=ALU.mult, op1=ALU.mult), reads=[sk, rk], writes=[dk_])

    for n in range(NH):
        pending_mod = [] if modc_in else [8192 + (n * 8 + i_) * 128 for i_ in range(8)]

        def mb():
            if pending_mod:
                mod_block(ringW, pending_mod.pop(0), 128)
        gws = gwt[:, n % 2]
        gwk = ('gw', n % 2)
        kb.dma('pool', out=gws, in_=lgw_d[n], writes=[gwk])
        xc, xck = FB(0)
        xcb, xcbk = BB(0)
        gy, gyk = BB(1)
        mb()
        pp = proj_fm(n * 128)
        conv_from_psum(pp, P_LCW + n * 4, pcol(P_LCB + n), xc, xck)
        kb.op('act', lambda e: e.activation(out=xcb, in_=xc, func=AF.Identity), reads=[xck], writes=[xcbk])
        mb()
        pp = proj_fm(2048 + n * 128)
        for h2, (p_, k_) in enumerate(pp):
            kb.op('act', lambda e, h2=h2, p_=p_: e.activation(out=gy[:, h2 * HALF:(h2 + 1) * HALF], in_=p_[:, :], func=AF.Gelu), reads=[k_], writes=[gyk])
        hbufs = []
        for d in range(2):
            A_, Ak = FB(1 + d * 4)
            S_, Sk = FB(2 + d * 4)
            B_, Bk = FB(3 + d * 4)
            H_, Hk = FB(4 + d * 4)
            for g in range(2):
                for h2 in range(2):
                    ps, pk = newps()
                    kb.op('pe', lambda e, h2=h2, g=g: e.matmul(ps[:, :], gws[:, d * 2 + g, :], xcb[:, h2 * HALF:(h2 + 1) * HALF], start=True, stop=True),
                          reads=[gwk, xcbk], writes=[pk])
                    dst, dk_ = (A_, Ak) if g == 0 else (B_, Bk)
                    kb.op('act', lambda e, h2=h2, dst=dst, g=g: e.activation(out=dst[:, h2 * HALF:(h2 + 1) * HALF], in_=ps[:, :], func=AF.Sigmoid,
                                                                           bias=pcol(P_LGB + n * 4 + d * 2 + g)),
                          reads=[pk, 'prm'], writes=[dk_])
            kb.op('act', lambda e: e.activation(out=S_, in_=A_, func=AF.Exp, scale=cdl2[:, n * 2 + d:n * 2 + d + 1]), reads=[Ak, 'cdl'], writes=[Sk])
            kb.op('act', lambda e: e.activation(out=A_, in_=A_, func=AF.Exp, scale=cdl[:, n * 2 + d:n * 2 + d + 1]), reads=[Ak, 'cdl'], writes=[Ak])
            kb.op('act', lambda e: e.activation(out=S_, in_=S_, func=AF.Sqrt, scale=-1.0, bias=1.0), reads=[Sk], writes=[Sk])
            kb.op('dve', lambda e: e.tensor_tensor(out=B_, in0=B_, in1=xc, op=ALU.mult), reads=[Bk, xck], writes=[Bk])
            kb.op('dve', lambda e: e.tensor_tensor(out=B_, in0=B_, in1=S_, op=ALU.mult), reads=[Bk, Sk], writes=[Bk])
            order = range(NSEG) if d == 0 else range(NSEG - 1, -1, -1)
            for si, s in enumerate(order):
                lo_, hi_ = s * SEG, (s + 1) * SEG
                if si == 0:
                    init = pcol(P_H0 + n * 2 + d)
                    ik = 'prm'
                else:
                    init = hinit[:, d:d + 1]
                    ik = ('hinit', d)
                if d == 0:
                    kb.op('dve', lambda e, lo_=lo_, hi_=hi_, init=init: e.tensor_tensor_scan(out=H_[:, lo_:hi_], data0=A_[:, lo_:hi_], data1=B_[:, lo_:hi_], initial=init, op0=ALU.mult, op1=ALU.add),
                          reads=[Ak, Bk, ik], writes=[Hk])
                    endcol = H_[:, hi_ - 1:hi_]
                else:
                    kb.op('dve', lambda e, lo_=lo_, hi_=hi_, init=init: e.tensor_tensor_scan(out=H_[:, hi_ - 1:lo_ - 1 if lo_ > 0 else None:-1], data0=A_[:, hi_ - 1:lo_ - 1 if lo_ > 0 else None:-1],
                                                                                       data1=B_[:, hi_ - 1:lo_ - 1 if lo_ > 0 else None:-1], initial=init, op0=ALU.mult, op1=ALU.add),
                          reads=[Ak, Bk, ik], writes=[Hk])
                    endcol = H_[:, lo_:lo_ + 1]
                sidx = (n * NSEG + s) * 2 + d
                kb.op('act', lambda e, endcol=endcol, sidx=sidx: e.activation(out=stl[:, sidx:sidx + 1], in_=endcol, func=AF.Identity), reads=[Hk], writes=['stl'])
                if si < NSEG - 1:
                    kb.op('dve', lambda e, endcol=endcol: e.tensor_tensor(out=hinit[:, d:d + 1], in0=endcol, in1=chain, op=ALU.mult), reads=[Hk, 'prm'], writes=[('hinit', d)])
            hbufs.append((H_, Hk))
        lo, lok = BB(2 + (n % 2))
        tmpf, tmpk = FB(1)
        kb.op('dve', lambda e: e.tensor_tensor(out=tmpf, in0=hbufs[0][0], in1=hbufs[1][0], op=ALU.add), reads=[hbufs[0][1], hbufs[1][1]], writes=[tmpk])
        kb.op('dve', lambda e: e.tensor_tensor(out=lo, in0=tmpf, in1=gy, op=ALU.mult), reads=[tmpk, gyk], writes=[lok])
        kb.dma('sp', out=lru_scr[n], in_=lo, reads=[lok], writes=['lru_scr'])
        if n == 0:
            dump('lru0', lo, lok)
            if stop == 'L0':
                return finish()

        qs, qsk = FB(0)
        t1, t1k = FB(1)
        t2, t2k = FB(2)
        qT, qTk = BB(4)
        kT, kTk = BB(5)
        zs, zsk = BB(6)
        mb()
        pp = proj_fm(4096 + n * 128)
        conv_from_psum(pp, P_DCW + 0 * 64 + n * 4, None, qs, qsk)
        kb.op('act', lambda e: e.activation(out=qs, in_=qs, func=AF.Silu), reads=[qsk], writes=[qsk])
        l2norm_to_bf(qs, qsk, t1, t1k, t2, t2k, qT, qTk, 128.0 ** -0.5)
        mb()
        pp = proj_fm(6144 + n * 128)
        conv_from_psum(pp, P_DCW + 1 * 64 + n * 4, None, qs, qsk)
        kb.op('act', lambda e: e.activation(out=qs, in_=qs, func=AF.Silu), reads=[qsk], writes=[qsk])
        l2norm_to_bf(qs, qsk, t1, t1k, t2, t2k, kT, kTk, 1.0)
        mb()
        pp = proj_fm(8192 + n * 128)
        conv_from_psum(pp, P_DCW + 2 * 64 + n * 4, None, qs, qsk)
        kb.op('act', lambda e: e.activation(out=qs, in_=qs, func=AF.Silu), reads=[qsk], writes=[qsk])
        Vc = pmF[0:64, 3 * T:5 * T].rearrange("p (c v) -> p c v", c=NCH)
        Vck = ('F', 3)
        Vck2 = ('F', 4)
        for grp in range(4):
            ps, pk = newps()
            for cc in range(4):
                c = grp * 4 + cc
                kb.op('pe', lambda e, c=c, cc=cc: e.transpose(ps[0:64, cc * 128:(cc + 1) * 128], qs[:, c * 64:(c + 1) * 64], ident), reads=[qsk, 'cst'], writes=[pk], inc=(cc == 3))
            kb.op('act', lambda e, grp=grp: e.activation(out=Vc[:, grp * 4:(grp + 1) * 4, :], in_=ps[0:64, :].rearrange("p (a b) -> p a b", a=4), func=AF.Identity),
                  reads=[pk], writes=[Vck, Vck2])
        ktok = pmB[0:64, 7 * T:9 * T].rearrange("p (c v) -> p c v", c=NCH)
        ktk = [('B', 7), ('B', 8)]
        for grp in range(4):
            ps, pk = newps()
            psv = ps[:, :].bitcast(BF16)
            for cc in range(4):
                c = grp * 4 + cc
                kb.op('pe', lambda e, c=c, cc=cc: e.transpose(psv[0:64, cc * 128:(cc + 1) * 128], kT[:, c * 64:(c + 1) * 64], identb[:]), reads=[kTk, 'identb'], writes=[pk], inc=(cc == 3))
            kb.op('act', lambda e, grp=grp: e.activation(out=ktok[:, grp * 4:(grp + 1) * 4, :], in_=psv[0:64, 0:512].rearrange("p (a b) -> p a b", a=4), func=AF.Identity),
                  reads=[pk], writes=ktk)
        mb()
        pp = proj_fm(10240 + n * 128)
        for h2, (p_, k_) in enumerate(pp):
            kb.op('act', lambda e, h2=h2, p_=p_: e.activation(out=zs[:, h2 * HALF:(h2 + 1) * HALF], in_=p_[:, :], func=AF.Silu), reads=[k_], writes=[zsk])
        Oacc, Oack = FB(0)
        if stop == 'D1':
            return finish()

        DD = {}
        for d in range(2):
            dh = d * 16 + n
            nmD = cst[0:64, (C_NMF if d == 0 else C_NMT):(C_NMF if d == 0 else C_NMT) + 64]
            nmDT = cst[0:64, (C_NMT if d == 0 else C_NMF):(C_NMT if d == 0 else C_NMF) + 64]
            smY = cst[0:64, (C_SMF if d == 0 else C_SMT):(C_SMF if d == 0 else C_SMT) + 64]
            smW = cst[0:64, (C_SMT if d == 0 else C_SMF):(C_SMT if d == 0 else C_SMF) + 64]
            id64 = cst[0:64, C_ID:C_ID + 64]

            def c3(ap):
                return ap.rearrange("p (c i) -> p c i", c=NCH)

            def bc_f(ap2):
                return ap2.unsqueeze(1).to_broadcast([64, NCH, 64])

            def bc_col(slot):
                return tokm[:, slot, :, dh:dh + 1].to_broadcast([64, NCH, 64])

            GCb, GCbk = FB(5)
            Eg, Egk = FB(6)
            DG, DGk = FB(1)
            BbS, BbSk = FB(2)
            DT, DTk = FB(7)
            Dn, Dnk = FB(8)
            Wa, Wak = FB(9)
            Ya, Yak = FB(10)
            Wb, Wbk = FB(11)
            Yb, Ybk = FB(12)
            Z_, Zk = FB(13)
            if d == 0:
                QgT, QgTk = BB(9)
                KgT, KgTk = BB(10)
                MTAT, MTATk = BB(11)
                Kd = pmB[0:64, 0 * T:2 * T].rearrange("p (c v) -> p c v", c=NCH)
                Kdk = [('B', 0), ('B', 1)]
                AT, ATk = BB(2 + ((n + 1) % 2))
            else:
                QgT, QgTk = BB(12)
                KgT, KgTk = BB(13)
                MTAT, MTATk = BB(7)
                Kd = pmB[0:64, 14 * T:16 * T].rearrange("p (c v) -> p c v", c=NCH)
                Kdk = [('B', 14), ('B', 15)]
                AT, ATk = BB(8)

            for which, dst, dstk in ((TGC, GCb, GCbk), (TB, BbS, BbSk)):
                kb.op('dve', lambda e, which=which: e.tensor_tensor(out=c3(DG[0:64, :]), in0=bc_f(id64), in1=bc_col(which), op=ALU.mult),
                      reads=['cst', ('tokm', which)], writes=[DGk])
                for h2 in range(2):
                    ps, pk = newps()
                    kb.op('pe', lambda e, h2=h2: e.matmul(ps[:, :], ones[0:64, :], DG[0:64, h2 * HALF:(h2 + 1) * HALF], start=True, stop=True), reads=['cst', DGk], writes=[pk])
                    if which == TGC:
                        kb.op('act', lambda e, h2=h2: e.activation(out=Eg[:, h2 * HALF:(h2 + 1) * HALF], in_=ps[:, :], func=AF.Exp), reads=[pk], writes=[Egk])
                        kb.op('act', lambda e, h2=h2: e.activation(out=GCb[:, h2 * HALF:(h2 + 1) * HALF], in_=ps[:, :], func=AF.Identity), reads=[pk], writes=[GCbk])
                    else:
                        kb.op('dve', lambda e, h2=h2: e.tensor_tensor(out=BbS[0:64, h2 * HALF:(h2 + 1) * HALF].rearrange("p (c i) -> p c i", c=8), in0=ps[0:64, :].rearrange("p (c i) -> p c i", c=8),
                                                                      in1=smW.unsqueeze(1).to_broadcast([64, 8, 64]), op=ALU.mult),
                              reads=[pk, 'cst'], writes=[BbSk])
            egcol = 63 if d == 0 else 0
            kb.op('act', lambda e: e.activation(out=egl[:, d, :], in_=Eg.rearrange("p (c i) -> p c i", c=NCH)[:, :, egcol], func=AF.Identity), reads=[Egk], writes=[('egl', d)])
            kb.op('dve', lambda e: e.tensor_tensor(out=QgT, in0=qT, in1=Eg, op=ALU.mult), reads=[qTk, Egk], writes=[QgTk])
            kb.op('dve', lambda e: e.tensor_tensor(out=KgT, in0=kT, in1=Eg, op=ALU.mult), reads=[kTk, Egk], writes=[KgTk])
            kb.op('dve', lambda e: e.tensor_tensor(out=Kd, in0=ktok, in1=tokm[:, TKD, :, dh:dh + 1].to_broadcast([64, NCH, 128]), op=ALU.mult),
                  reads=ktk + [('tokm', TKD)], writes=Kdk)
            raw = GCb[0:64, :]
            kb.op('dve', lambda e: e.tensor_tensor(out=c3(raw), in0=c3(raw), in1=bc_col(TGC), op=ALU.subtract), reads=[GCbk, ('tokm', TGC)], writes=[GCbk])
            kb.op('dve', lambda e: e.scalar_tensor_tensor(out=c3(DT[0:64, :]), in0=c3(raw), scalar=0.0, in1=bc_f(nmDT), op0=ALU.min, op1=ALU.add), reads=[GCbk, 'cst'], writes=[DTk])
            kb.op('act', lambda e: e.activation(out=DT[0:64, :], in_=DT[0:64, :], func=AF.Exp), reads=[DTk], writes=[DTk])
            kb.op('dve', lambda e: e.scalar_tensor_tensor(out=c3(Dn[0:64, :]), in0=c3(raw), scalar=0.0, in1=bc_f(nmD), op0=ALU.max, op1=ALU.subtract), reads=[GCbk, 'cst'], writes=[Dnk])
            kb.op('act', lambda e: e.activation(out=Dn[0:64, :], in_=Dn[0:64, :], func=AF.Exp, scale=-1.0), reads=[Dnk], writes=[Dnk])
            if stop == 'D2':
                return finish()
            mb()
            gps = []
            qps = []
            for h2 in range(2):
                ps, pk = newps()
                for cc in range(8):
                    c = h2 * 8 + cc
                    kb.op('pe', lambda e, c=c, cc=cc: e.matmul(ps[0:64, cc * 64:(cc + 1) * 64], kT[:, c * 64:(c + 1) * 64], kT[:, c * 64:(c + 1) * 64], start=True, stop=True),
                          reads=[kTk], writes=[pk], inc=(cc == 7))
                gps.append((ps, pk))
                ps, pk = newps()
                for cc in range(8):
                    c = h2 * 8 + cc
                    kb.op('pe', lambda e, c=c, cc=cc: e.matmul(ps[0:64, cc * 64:(cc + 1) * 64], kT[:, c * 64:(c + 1) * 64], qT[:, c * 64:(c + 1) * 64], start=True, stop=True),
                          reads=[kTk, qTk], writes=[pk], inc=(cc == 7))
                qps.append((ps, pk))
            for h2 in range(2):
                kb.op('dve', lambda e, h2=h2: e.tensor_tensor(out=AT[0:64, h2 * HALF:(h2 + 1) * HALF], in0=qps[h2][0][0:64, :], in1=DT[0:64, h2 * HALF:(h2 + 1) * HALF], op=ALU.mult),
                      reads=[qps[h2][1], DTk], writes=[ATk])
            kb.op('dve', lambda e: e.scalar_tensor_tensor(out=DT[0:64, :], in0=DT[0:64, :], scalar=-1.0, in1=BbS[0:64, :], op0=ALU.mult, op1=ALU.mult), reads=[DTk, BbSk], writes=[DTk])
            kb.op('dve', lambda e: e.tensor_tensor(out=c3(DG[0:64, :]), in0=bc_f(smY), in1=bc_col(TB), op=ALU.mult), reads=['cst', ('tokm', TB)], writes=[DGk])
            kb.op('dve', lambda e: e.scalar_tensor_tensor(out=Dn[0:64, :], in0=Dn[0:64, :], scalar=-1.0, in1=DG[0:64, :], op0=ALU.mult, op1=ALU.mult), reads=[Dnk, DGk], writes=[Dnk])
            for h2 in range(2):
                sl = slice(h2 * HALF, (h2 + 1) * HALF)
                kb.op('dve', lambda e, h2=h2, sl=sl: e.tensor_tensor(out=Wa[0:64, sl], in0=gps[h2][0][0:64, :], in1=DT[0:64, sl], op=ALU.mult), reads=[gps[h2][1], DTk], writes=[Wak])
                kb.op('dve', lambda e, h2=h2, sl=sl: e.tensor_tensor(out=Ya[0:64, sl], in0=gps[h2][0][0:64, :], in1=Dn[0:64, sl], op=ALU.mult), reads=[gps[h2][1], Dnk], writes=[Yak])
            kb.op('dve', lambda e: e.tensor_tensor(out=c3(Z_[0:64, :]), in0=c3(Wa[0:64, :]), in1=bc_f(id64), op=ALU.add), reads=[Wak, 'cst'], writes=[Zk])
            Wc, Wck, Yc, Yck = Wa, Wak, Ya, Yak
            Wn, Wnk, Yn, Ynk = Wb, Wbk, Yb, Ybk
            for lvl in range(1, 6):
                for h2 in range(2):
                    ps, pk = newps()
                    for cc in range(8):
                        c = h2 * 8 + cc
                        sl = slice(c * 64, (c + 1) * 64)
                        kb.op('pe', lambda e, cc=cc, sl=sl: e.matmul(ps[0:64, cc * 64:(cc + 1) * 64], Wc[0:64, sl], Yc[0:64, sl], start=True, stop=True), reads=[Wck, Yck], writes=[pk], inc=(cc == 7))
                    kb.op('act', lambda e, h2=h2: e.activation(out=Yn[0:64, h2 * HALF:(h2 + 1) * HALF], in_=ps[0:64, :], func=AF.Identity), reads=[pk], writes=[Ynk])
                    if lvl < 5:
                        ps, pk = newps()
                        for cc in range(8):
                            c = h2 * 8 + cc
                            sl = slice(c * 64, (c + 1) * 64)
                            kb.op('pe', lambda e, cc=cc, sl=sl: e.matmul(ps[0:64, cc * 64:(cc + 1) * 64], Yc[0:64, sl], Wc[0:64, sl], start=True, stop=True), reads=[Wck, Yck], writes=[pk], inc=(cc == 7))
                        kb.op('act', lambda e, h2=h2: e.activation(out=Wn[0:64, h2 * HALF:(h2 + 1) * HALF], in_=ps[0:64, :], func=AF.Identity), reads=[pk], writes=[Wnk])
                for h2 in range(2):
                    ps, pk = newps()
                    for cc in range(8):
                        c = h2 * 8 + cc
                        sl = slice(c * 64, (c + 1) * 64)
                        kb.op('pe', lambda e, cc=cc, sl=sl: e.matmul(ps[0:64, cc * 64:(cc + 1) * 64], Yn[0:64, sl], Z_[0:64, sl], start=True, stop=True), reads=[Ynk, Zk], writes=[pk], inc=(cc == 7))
                    kb.op('dve', lambda e, h2=h2: e.tensor_tensor(out=Z_[0:64, h2 * HALF:(h2 + 1) * HALF], in0=Z_[0:64, h2 * HALF:(h2 + 1) * HALF], in1=ps[0:64, :], op=ALU.add), reads=[pk, Zk], writes=[Zk])
                Wc, Wck, Yc, Yck, Wn, Wnk, Yn, Ynk = Wn, Wnk, Yn, Ynk, Wc, Wck, Yc, Yck
            MT = MTAT
            kb.op('dve', lambda e: e.tensor_tensor(out=c3(MT[0:64, :]), in0=c3(Z_[0:64, :]), in1=bc_col(TB), op=ALU.mult), reads=[Zk, ('tokm', TB)], writes=[MTATk])
            if n == 0 and d == 0:
                dump('MT', MT[0:64, :], MTATk)
                dump('AT', AT[0:64, :], ATk)
                dump('Eg', Eg, Egk)

            if stop == 'D3':
                return finish()
            DD[d] = dict(QgT=QgT, QgTk=QgTk, KgT=KgT, KgTk=KgTk, MT=MT, MTk=MTATk, AT=AT, ATk=ATk, Kd=Kd, Kdk=Kdk)

        while pending_mod:
            mb()
        Ob_, Obk = FB(5)
        for d in range(2):
            kb.dma('sp', out=S32[:, d, :], in_=s0_d[d, n], writes=[('S32', d)])
            kb.op('act', lambda e, d=d: e.activation(out=Sbf[:, d, :], in_=S32[:, d, :], func=AF.Identity), reads=[('S32', d)], writes=[('Sbf', d)])
        for ci in range(NCH):
            for d in range(2):
                P_ = DD[d]
                QgT, QgTk, KgT, KgTk, MT, MTk, AT, ATk, Kd, Kdk = (P_[k_] for k_ in ('QgT', 'QgTk', 'KgT', 'KgTk', 'MT', 'MTk', 'AT', 'ATk', 'Kd', 'Kdk'))
                c = ci if d == 0 else NCH - 1 - ci
                Sd32 = S32[:, d, :]
                Sdb = Sbf[:, d, :]
                sk32 = ('S32', d)
                skb = ('Sbf', d)
                t0 = c * 64
                seg = c // 4
                first_in_seg = (c % 4 == 0) if d == 0 else (c % 4 == 3)
                last_in_seg = (c % 4 == 3) if d == 0 else (c % 4 == 0)
                if first_in_seg and ci > 0:
                    kb.op('dve', lambda e, Sd32=Sd32: e.tensor_scalar(out=Sd32, in0=Sd32, scalar1=chain, scalar2=None, op0=ALU.mult), reads=[sk32, 'prm'], writes=[sk32])
                    kb.op('act', lambda e, Sd32=Sd32, Sdb=Sdb: e.activation(out=Sdb, in_=Sd32, func=AF.Identity), reads=[sk32], writes=[skb])
                ob, obk = resps(6 + d)
                ps, pk = newps()
                kb.op('pe', lambda e, t0=t0, KgT=KgT, Sdb=Sdb, ps=ps: e.matmul(ps[0:64, 0:128], KgT[:, t0:t0 + 64], Sdb, start=True, stop=True), reads=[KgTk, skb], writes=[pk])
                rslot = ci % 2
                Rv = RR[:, d * 2 + rslot, :]
                Rkey = ('RR', d * 2 + rslot)
                kb.op('dve', lambda e, c=c, Rv=Rv, ps=ps: e.tensor_tensor(out=Rv, in0=Vc[:, c, :], in1=ps[0:64, 0:128], op=ALU.subtract), reads=[Vck, Vck2, pk], writes=[Rkey])
                ps2, pk2 = newps()
                kb.op('pe', lambda e, t0=t0, Rv=Rv, MT=MT, ps2=ps2: e.matmul(ps2[0:64, 0:128], MT[0:64, t0:t0 + 64], Rv, start=True, stop=True), reads=[MTk, Rkey], writes=[pk2])
                vn = VN[:, d * 2 + rslot, :]
                vnk = ('VN', d * 2 + rslot)
                kb.op('act', lambda e, vn=vn, ps2=ps2: e.activation(out=vn, in_=ps2[0:64, 0:128], func=AF.Identity), reads=[pk2], writes=[vnk])
                col = (c % 8) * 64
                kb.op('pe', lambda e, t0=t0, col=col, ob=ob, Sdb=Sdb, QgT=QgT: e.matmul(ob[:, col:col + 64], Sdb, QgT[:, t0:t0 + 64], start=True, stop=False), reads=[skb, QgTk], writes=[obk], inc=False)
                kb.op('pe', lambda e, t0=t0, col=col, vn=vn, ob=ob, AT=AT: e.matmul(ob[:, col:col + 64], vn, AT[0:64, t0:t0 + 64], start=False, stop=True), reads=[vnk, ATk], writes=[obk])
                ps3, pk3 = newps()
                kb.op('pe', lambda e, c=c, vn=vn, Kd=Kd, ps3=ps3: e.matmul(ps3[:, 0:128], Kd[:, c, :], vn, start=True, stop=True), reads=Kdk + [vnk], writes=[pk3])
                kb.op('dve', lambda e, c=c, d=d, Sd32=Sd32, ps3=ps3: e.scalar_tensor_tensor(out=Sd32, in0=Sd32, scalar=egl[:, d, c:c + 1], in1=ps3[:, 0:128], op0=ALU.mult, op1=ALU.add),
                      reads=[sk32, ('egl', d), pk3], writes=[sk32])
                kb.op('act', lambda e, Sd32=Sd32, Sdb=Sdb: e.activation(out=Sdb, in_=Sd32, func=AF.Identity), reads=[sk32], writes=[skb])
                if last_in_seg:
                    si_ = d * 4 + (ci // 4) % 4
                    so = Sout[:, si_, :]
                    sok = ('Sout', si_)
                    kb.op('act', lambda e, so=so, Sd32=Sd32: e.activation(out=so, in_=Sd32, func=AF.Identity), reads=[sk32], writes=[sok])
                    kb.dma('sp', out=std_d[seg, d, n], in_=so, reads=[sok], writes=['std'])
                if ci % 8 == 7:
                    hb = (c // 8)
                    sl = slice(hb * HALF, (hb + 1) * HALF)
                    if d == 0:
                        kb.op('act', lambda e, sl=sl, ob=ob: e.activation(out=Oacc[:, sl], in_=ob[:, :], func=AF.Identity), reads=[obk], writes=[Oack])
                    else:
                        kb.op('act', lambda e, sl=sl, ob=ob: e.activation(out=Ob_[:, sl], in_=ob[:, :], func=AF.Identity), reads=[obk], writes=[Obk])
        kb.op('dve', lambda e: e.tensor_tensor(out=Oacc, in0=Oacc, in1=Ob_, op=ALU.add), reads=[Oack, Obk], writes=[Oack])
        t1, t1k = FB(1)
        t2, t2k = FB(2)
        kb.op('act', lambda e: e.activation(out=t1, in_=Oacc, func=AF.Square), reads=[Oack], writes=[t1k])
        for h2 in range(2):
            ps, pk = newps()
            kb.op('pe', lambda e, h2=h2: e.matmul(ps[:, :], ones, t1[:, h2 * HALF:(h2 + 1) * HALF], start=True, stop=True), reads=['cst', t1k], writes=[pk])
            kb.op('act', lambda e, h2=h2: e.activation(out=t2[:, h2 * HALF:(h2 + 1) * HALF], in_=ps[:, :], func=AF.Sqrt, scale=1.0 / 128.0, bias=RMS_EPS), reads=[pk], writes=[t2k])
        kb.op('dve', lambda e: e.reciprocal(out=t2, in_=t2), reads=[t2k], writes=[t2k])
        kb.op('dve', lambda e: e.scalar_tensor_tensor(out=t1, in0=Oacc, scalar=pcol(P_NW), in1=t2, op0=ALU.mult, op1=ALU.mult), reads=[Oack, t2k, 'prm'], writes=[t1k])
        dob, dobk = BB(2 + ((n + 1) % 2))
        kb.op('dve', lambda e: e.tensor_tensor(out=dob, in0=t1, in1=zs, op=ALU.mult), reads=[t1k, zsk], writes=[dobk])
        kb.dma('sp', out=dn_scr[n], in_=dob, reads=[dobk], writes=['dn_scr'])
        if n == 0:
            dump('dn0', dob, dobk)
            dump('oacc0', Oacc, Oack)
            if stop == 'M1':
                kb.dma('sp', out=stl_d, in_=stl[:], reads=['stl'], writes=['stl_d'])
                return finish()

    kb.dma('sp', out=stl_d, in_=stl[:], reads=['stl'], writes=['stl_d'])
    kb.op('dve', lambda e: e.tensor_scalar_add(out=sc1f[:], in0=modc[:, 128:160], scalar1=1.0), reads=['modc'], writes=['sc1f'])
    kb.barrier()
    for g_ in (vng, rrg, egg, hig, gwg, sog, sbg, s3g, ntg, tmg, xpg, pmb, pm):
        g_.__exit__(None, None, None)

    with nc.sbuf_tensor("wring2g", [128, 5 * 32 * 128], BF16) as wring2g, nc.sbuf_tensor("lod", [128, 2, NH, HALF], BF16) as lod, nc.sbuf_tensor("gsb", [128, 2, 2, HALF], F32) as gsb, \
            nc.sbuf_tensor("mgt", [128, 2, HALF], BF16) as mgt:
        ring32 = Ring(3, 32, 128, extra=wring2g)
        for hf in range(2):
            tsl = slice(hf * HALF, (hf + 1) * HALF)
            for n in range(NH):
                kb.dma('sp', out=lod[:, 0, n, :], in_=lru_scr[n, :, tsl], reads=['lru_scr'], writes=['lod'])
                kb.dma('sp', out=lod[:, 1, n, :], in_=dn_scr[n, :, tsl], reads=['dn_scr'], writes=['lod'])
            for f in range(KT):
                pss = []
                for br, w_d in ((0, wlp_d), (1, wdp_d)):
                    v, key = ring32.load(wview(w_d, 0, 16, f * 128, 128), 16, 128)
                    ps, pk = newps()
                    for kt in range(16):
                        kb.op('pe', lambda e, kt=kt, v=v, br=br: e.matmul(ps[:, :], v[:, kt, :], lod[:, br, kt, :], start=(kt == 0), stop=(kt == 15)), reads=[key, 'lod'], writes=[pk], inc=(kt == 15))
                    pss.append((ps, pk))
                gss = []
                for g in range(2):
                    wv, wk = ring32.load(wview(win_d, 0, 32, 12352 + g * D + f * 128, 128))
                    ps, pk = newps()
                    for kt in range(KT):
                        kb.op('pe', lambda e, kt=kt, wv=wv: e.matmul(ps[:, :], wv[:, kt, :], hT[:, kt, tsl], start=(kt == 0), stop=(kt == KT - 1)), reads=[wk, 'hT'], writes=[pk], inc=(kt == KT - 1))
                    gk = ('gsb', f % 2, g)
                    kb.op('act', lambda e, g=g, ps=ps: e.activation(out=gsb[:, f % 2, g, :], in_=ps[:, :], func=AF.Sigmoid, bias=pcol(P_BBR + g * 32 + f)), reads=[pk, 'prm'], writes=[gk])
                    gss.append(gk)
                for g in range(2):
                    kb.op('dve', lambda e, g=g: e.tensor_tensor(out=gsb[:, f % 2, g, :], in0=gsb[:, f % 2, g, :], in1=pss[g][0][:, :], op=ALU.mult), reads=[gss[g], pss[g][1]], writes=[gss[g]])
                mk = ('mgt', f % 2)
                kb.op('dve', lambda e: e.tensor_tensor(out=mgt[:, f % 2, :], in0=gsb[:, f % 2, 0, :], in1=gsb[:, f % 2, 1, :], op=ALU.add), reads=gss, writes=[mk])
                kb.dma('sp', out=mg_scr[hf, f], in_=mgt[:, f % 2, :], reads=[mk], writes=['mg_scr'])
    kb.barrier()
    hT_guard.__exit__(None, None, None)

    with nc.sbuf_tensor("acc", [128, KT, HALF], F32) as acc, nc.sbuf_tensor("H2", [128, KT, HALF], BF16) as H2, \
            nc.sbuf_tensor("mgh", [128, KT, HALF], BF16) as mgh, \
            nc.sbuf_tensor("lnt", [128, 4, HALF], F32) as lnt, nc.sbuf_tensor("lns", [128, 5, HALF], F32) as lns, \
            nc.sbuf_tensor("yst", [128, 2, HALF], F32) as yst, nc.sbuf_tensor("wring2o", [128, 2 * 32 * 128], BF16) as wring2o:
        ones_f = ones
        actb = mgh[:, 0:16, :]
        MGK = [('actb', j) for j in range(16)] + ['mgh']

        def ln_stats(get_tile, tag):
            s1, k1 = resps(6)
            s2, k2 = resps(7)
            for f in range(KT):
                ap, key = get_tile(f)
                sq = lnt[:, f % 2, :]
                sqk = ('lnt', f % 2)
                kb.op('act', lambda e, ap=ap, sq=sq: e.activation(out=sq, in_=ap, func=AF.Square), reads=[key], writes=[sqk])
                kb.op('pe', lambda e, ap=ap: e.matmul(s1[:, :], ones_f, ap, start=(f == 0), stop=(f == KT - 1)), reads=['cst', key], writes=[k1], inc=(f == KT - 1))
                kb.op('pe', lambda e, sq=sq: e.matmul(s2[:, :], ones_f, sq, start=(f == 0), stop=(f == KT - 1)), reads=['cst', sqk], writes=[k2], inc=True)
            mean, msq, var, rstd, nmr = (lns[:, i, :] for i in range(5))
            kb.op('dve', lambda e: e.tensor_scalar_mul(out=mean, in0=s1[:, :], scalar1=1.0 / D), reads=[k1], writes=[('lns', 0)])
            kb.op('dve', lambda e: e.tensor_tensor(out=msq, in0=mean, in1=mean, op=ALU.mult), reads=[('lns', 0)], writes=[('lns', 1)])
            kb.op('dve', lambda e: e.scalar_tensor_tensor(out=var, in0=s2[:, :], scalar=1.0 / D, in1=msq, op0=ALU.mult, op1=ALU.subtract), reads=[k2, ('lns', 1)], writes=[('lns', 2)])
            kb.op('act', lambda e: e.activation(out=var, in_=var, func=AF.Sqrt, bias=LN_EPS), reads=[('lns', 2)], writes=[('lns', 2)])
            kb.op('dve', lambda e: e.reciprocal(out=rstd, in_=var), reads=[('lns', 2)], writes=[('lns', 3)])
            kb.op('dve', lambda e: e.scalar_tensor_tensor(out=nmr, in0=mean, scalar=-1.0, in1=rstd, op0=ALU.mult, op1=ALU.mult), reads=[('lns', 0), ('lns', 3)], writes=[('lns', 4)])
            return rstd, ('lns', 3), nmr, ('lns', 4)

        ringO = Ring(3, 32, 128, extra=wring2o)
        ringU = ringO
        for hf in range(2):
            tsl = slice(hf * HALF, (hf + 1) * HALF)
            kb.dma('sp', out=mgh[:], in_=mg_scr[hf].rearrange("k p t -> p k t"), reads=['mg_scr'], writes=MGK)
            for f in range(KT):
                wv, wk = ringO.load(wview(wo_d, 0, 32, f * 128, 128))
                ps, pk = newps()
                for kt in range(KT):
                    kb.op('pe', lambda e, kt=kt, wv=wv: e.matmul(ps[:, :], wv[:, kt, :], mgh[:, kt, :], start=(kt == 0), stop=(kt == KT - 1)), reads=[wk] + MGK, writes=[pk], inc=(kt == KT - 1))
                xa = lnt[:, 2 + f % 2, :]
                xak = ('lnt', 2 + f % 2)
                kb.dma('sp', out=xa, in_=xT_scr[f, :, tsl], reads=['xT_scr'], writes=[xak])
                kb.op('act', lambda e, xa=xa: e.activation(out=xa, in_=xa, func=AF.Identity, scale=float(ALPHA)), reads=[xak], writes=[xak])
                kb.op('dve', lambda e, xa=xa, ps=ps: e.scalar_tensor_tensor(out=acc[:, f, :], in0=ps[:, :], scalar=gtm[:, f:f + 1], in1=xa, op0=ALU.mult, op1=ALU.add),
                      reads=[pk, xak, 'modc'], writes=[('acc', f)])
            rstd, rk_, nmr, nk_ = ln_stats(lambda f: (acc[:, f, :], ('acc', f)), 'ln1')
            for f in range(KT):
                tt_ = lnt[:, f % 2, :]
                ttk = ('lnt', f % 2)
                kb.op('dve', lambda e, tt_=tt_: e.tensor_tensor(out=tt_, in0=acc[:, f, :], in1=rstd, op=ALU.mult), reads=[('acc', f), rk_], writes=[ttk])
                kb.op('dve', lambda e, tt_=tt_: e.tensor_tensor(out=tt_, in0=tt_, in1=nmr, op=ALU.add), reads=[ttk, nk_], writes=[ttk])
                x1t = lnt[:, 2 + f % 2, :]
                x1k = ('lnt', 2 + f % 2)
                kb.op('act', lambda e, tt_=tt_, x1t=x1t: e.activation(out=x1t, in_=tt_, func=AF.Identity, scale=pcol(P_L1G + f), bias=pcol(P_L1B + f)), reads=[ttk, 'prm'], writes=[x1k])
                kb.dma('sp', out=x1_scr[f], in_=x1t, reads=[x1k], writes=[('x1_scr', f)])
                kb.op('act', lambda e, x1t=x1t: e.activation(out=H2[:, f, :], in_=x1t, func=AF.Identity, scale=sc1f[:, f:f + 1], bias=shf[:, f:f + 1]), reads=[x1k, 'sc1f', 'modc'], writes=['H2'])
            for g in range(8):
                for j in range(16):
                    wv, wk = ringU.load(wview(wup_d, 0, 32, (g * 16 + j) * 128, 128))
                    ps, pk = newps()
                    for kt in range(KT):
                        kb.op('pe', lambda e, kt=kt, wv=wv: e.matmul(ps[:, :], wv[:, kt, :], H2[:, kt, :], start=(kt == 0), stop=(kt == KT - 1)), reads=[wk, 'H2'], writes=[pk], inc=(kt == KT - 1))
                    rl = lnt[:, j % 2, :]
                    rlk = ('lnt', j % 2)
                    kb.op('act', lambda e, rl=rl, ps=ps: e.activation(out=rl, in_=ps[:, :], func=AF.Relu), reads=[pk], writes=[rlk])
                    kb.op('dve', lambda e, rl=rl, j=j: e.tensor_tensor(out=actb[:, j, :], in0=rl, in1=rl, op=ALU.mult), reads=[rlk], writes=[('actb', j)])
                for f in range(KT):
                    v, key = ringU.load(wview(wdn_d, g * 2048, 16, f * 128, 128), 16, 128)
                    ps, pk = newps()
                    for j in range(16):
                        kb.op('pe', lambda e, j=j, v=v: e.matmul(ps[:, :], v[:, j, :], actb[:, j, :], start=(j == 0), stop=(j == 15)), reads=[key, ('actb', j)], writes=[pk], inc=(j == 15))
                    if g == 0:
                        kb.op('act', lambda e, ps=ps: e.activation(out=acc[:, f, :], in_=ps[:, :], func=AF.Identity), reads=[pk], writes=[('acc', f)])
                    else:
                        kb.op('dve', lambda e, ps=ps: e.tensor_tensor(out=acc[:, f, :], in0=acc[:, f, :], in1=ps[:, :], op=ALU.add), reads=[pk, ('acc', f)], writes=[('acc', f)])
            for f in range(KT):
                xa = lnt[:, 2 + f % 2, :]
                xak = ('lnt', 2 + f % 2)
                kb.dma('sp', out=xa, in_=x1_scr[f], reads=[('x1_scr', f)], writes=[xak])
                kb.op('act', lambda e, xa=xa: e.activation(out=xa, in_=xa, func=AF.Identity, scale=float(ALPHA)), reads=[xak], writes=[xak])
                kb.op('dve', lambda e, xa=xa: e.scalar_tensor_tensor(out=acc[:, f, :], in0=acc[:, f, :], scalar=gtf[:, f:f + 1], in1=xa, op0=ALU.mult, op1=ALU.add),
                      reads=[('acc', f), xak, 'modc'], writes=[('acc', f)])
            rstd, rk_, nmr, nk_ = ln_stats(lambda f: (acc[:, f, :], ('acc', f)), 'ln2')
            for f in range(KT):
                kb.op('dve', lambda e: e.tensor_tensor(out=acc[:, f, :], in0=acc[:, f, :], in1=rstd, op=ALU.mult), reads=[('acc', f), rk_], writes=[('acc', f)])
                kb.op('dve', lambda e: e.tensor_tensor(out=acc[:, f, :], in0=acc[:, f, :], in1=nmr, op=ALU.add), reads=[('acc', f), nk_], writes=[('acc', f)])
                kb.op('act', lambda e: e.activation(out=acc[:, f, :], in_=acc[:, f, :], func=AF.Identity, scale=pcol(P_L2G + f), bias=pcol(P_L2B + f)), reads=[('acc', f), 'prm'], writes=[('acc', f)])
            for f4 in range(8):
                for tq in range(4):
                    ps, pk = newps()
                    for q in range(4):
                        f = f4 * 4 + q
                        kb.op('pe', lambda e, q=q, f=f: e.transpose(ps[:, q * 128:(q + 1) * 128], acc[:, f, tq * 128:(tq + 1) * 128], ident), reads=[('acc', f), 'cst'], writes=[pk], inc=(q == 3))
                    yi = (f4 * 4 + tq) % 2
                    kb.op('act', lambda e, yi=yi, ps=ps: e.activation(out=yst[:, yi, :], in_=ps[:, :], func=AF.Identity), reads=[pk], writes=[('yst', yi)])
                    r0 = hf * HALF + tq * 128
                    kb.dma('sp', out=y_d[r0:r0 + 128, f4 * 512:(f4 + 1) * 512], in_=yst[:, yi, :], reads=[('yst', yi)], writes=['y_d'])
    kb.barrier(['sp'])
    print("program: insts", kb.ninst, "waits", kb.nwait)
    return nc


def _consts():
    c = np.zeros((128, NCONST), np.float32)
    c[:, C_ID:C_ID + 128] = np.eye(128, dtype=np.float32)
    c[:, C_ONE:C_ONE + 128] = 1.0
    i = np.arange(64)
    src = i[:, None]; dst = i[None, :]
    c[0:64, C_TRIF:C_TRIF + 64] = (src <= dst)
    c[0:64, C_TRIB:C_TRIB + 64] = (src >= dst)
    c[0:64, C_NMF:C_NMF + 64] = np.where(src >= dst, 0.0, NEG)
    c[0:64, C_NMT:C_NMT + 64] = np.where(dst >= src, 0.0, NEG)
    c[0:64, C_SMF:C_SMF + 64] = (src > dst)
    c[0:64, C_SMT:C_SMT + 64] = (dst > src)
    c[:, C_JIDX:C_JIDX + 8] = np.arange(8)[None, :] * 128 + np.arange(128)[:, None]
    c[:, C_NIDX:C_NIDX + 64] = np.arange(64)[None, :]
    return c


def _colT(v, nt):
    return np.ascontiguousarray(np.asarray(v).reshape(nt, 128).T)


def _params(cvec, chain, posflag, h0, I):
    p = np.zeros((128, NPRM), np.float32)
    p[:, P_CT:P_CT + 32] = _colT(cvec, 32)
    p[:, P_BMOD:P_BMOD + 192] = _colT(I['b_mod'][0], 192)
    p[:, P_LCW:P_LCW + 64] = I['lru_conv_w'][0].reshape(4, 16, 128).transpose(2, 1, 0).reshape(128, 64)
    p[:, P_LCB:P_LCB + 16] = _colT(I['lru_conv_b'][0], 16)
    p[:, P_LGB:P_LGB + 64] = I['lru_gate_b'][0].reshape(2, 2, 16, 128).transpose(3, 2, 0, 1).reshape(128, 64)
    p[:, P_LAM:P_LAM + 32] = I['lru_lambda'][0].reshape(2, 16, 128).transpose(2, 1, 0).reshape(128, 32)
    p[:, P_DCW:P_DCW + 192] = I['dn_conv_w'][0].reshape(4, 3, 16, 128).transpose(3, 1, 2, 0).reshape(128, 192)
    p[:, P_ALOG:P_ALOG + 32] = I['dn_a_log'][0].reshape(1, 32)
    p[:, P_DTB:P_DTB + 32] = I['dn_dt_bias'][0].reshape(1, 32)
    p[:, P_NW] = I['dn_norm_w'][0]
    p[:, P_BBR:P_BBR + 64] = I['b_branch'][0].reshape(2, 32, 128).transpose(2, 0, 1).reshape(128, 64)
    p[:, P_L1G:P_L1G + 32] = _colT(I['ln1_g'][0], 32)
    p[:, P_L1B:P_L1B + 32] = _colT(I['ln1_b'][0], 32)
    p[:, P_L2G:P_L2G + 32] = _colT(I['ln2_g'][0], 32)
    p[:, P_L2B:P_L2B + 32] = _colT(I['ln2_b'][0], 32)
    p[:, P_FLAG] = chain
    p[:, P_FLAG + 1] = posflag
    p[:, P_H0:P_H0 + 32] = h0.reshape(2, 16, 128).transpose(2, 1, 0).reshape(128, 32)
    return p


_NC_CACHE = {}


def kernel(**I):
    I = {k: np.asarray(v) for k, v in I.items()}
    ncores = 8
    dbg = DEBUG.get('dbg', None)
    key = (repr(sorted((dbg or {}).items(), key=lambda kv: kv[0])), DEBUG.get('stop'))
    if key not in _NC_CACHE:
        d2 = dict(dbg or {})
        if 'modc_in' in DEBUG:
            d2['__modc_in'] = 1
        _NC_CACHE[key] = build_program(d2, DEBUG.get('stop'))
    nc = _NC_CACHE[key]
    cst = _consts()
    shared = {
        'cst': cst,
        'w_mod': np.ascontiguousarray(I['w_mod'][0]), 'w_in': np.ascontiguousarray(I['w_in'][0]),
        'lru_gw': np.ascontiguousarray(I['lru_gate_w'][0].transpose(2, 3, 0, 1, 4).reshape(16, 128, 4, 128)),
        'w_lp': np.ascontiguousarray(I['w_lru_proj'][0]), 'w_dp': np.ascontiguousarray(I['w_dn_proj'][0]),
        'w_o': np.ascontiguousarray(I['w_o'][0]), 'w_up': np.ascontiguousarray(I['w_up'][0]), 'w_down': np.ascontiguousarray(I['w_down'][0]),
    }
    zeros_x = np.zeros((T, D), np.float32)
    zeros_s0 = np.zeros((2, NH, 128, 128), np.float32)
    in_maps = []
    ncores_used = DEBUG.get('ncores', ncores)
    for core in range(ncores_used):
        m = dict(shared)
        if core < 2:
            b = core
            m['x'] = np.ascontiguousarray(I['x_sample'][b])
            m['prm'] = _params(I['c'][b], 1.0, 1.0, I['state_lru'][b, 0], I)
            m['dn_s0'] = np.ascontiguousarray(I['state_dn'][b, 0])
        elif core < 6:
            s = (core - 2) * 4
            m['x'] = np.ascontiguousarray(I['x_prompt'][s:s + 4].reshape(T, D))
            m['prm'] = _params(I['c_ctx'], 0.0, 0.0, np.zeros((2, 2048), np.float32), I)
            m['dn_s0'] = zeros_s0
        else:
            m['x'] = zeros_x
            m['prm'] = _params(I['c_ctx'], 0.0, 0.0, np.zeros((2, 2048), np.float32), I)
            m['dn_s0'] = zeros_s0
        in_maps.append(m)
    declared = set()
    for alloc in nc.allocations:
        try:
            if alloc.kind == "ExternalInput":
                declared.add(alloc.memorylocations[0].name)
        except Exception:
            pass
    if 'modc_in' in declared:
        for m in in_maps:
            m['modc_in'] = DEBUG['modc_in']
    in_maps = [{k: v for k, v in m.items() if k in declared} for m in in_maps]
    res = run_bass_kernel_spmd(nc, in_maps, core_ids=list(range(ncores_used)))
    R = res.results
    DEBUG['last'] = R
    if DEBUG.get('stop'):
        return None
    B = I['x_prompt'].shape[0]
    y_prompt = np.zeros((B, SEG, D), np.float32)
    y_sample = np.zeros((2, T, D), np.float32)
    st_lru = np.zeros((B, 1, 2, 2048), np.float32)
    st_dn = np.zeros((B, 1, 2, NH, 128, 128), np.float32)
    for core in range(min(ncores_used, 6)):
        r = R[core]
        if core < 2:
            y_sample[core] = r['y']
        else:
            s = (core - 2) * 4
            y_prompt[s:s + 4] = r['y'].reshape(4, SEG, D)
            sl = r['st_lru'].reshape(128, NH, NSEG, 2)
            st_lru[s:s + 4, 0] = sl.transpose(2, 3, 1, 0).reshape(4, 2, 2048)
            st_dn[s:s + 4, 0] = r['st_dn']
    return (y_prompt, y_sample, st_lru, st_dn)
```
